# Optimizing a Trainium2 kernel written in Bass

```python
import math
import jax, jax.numpy as jnp
from jax import lax
import numpy as np

D_MODEL = 1024
BATCH = 8
SEQ = 2048
DEPTH = 1

DN_HEADS = 4
DN_HEAD_DIM = 128
DN_WIDTH = DN_HEADS * DN_HEAD_DIM
CONV_WIDTH = 4
CHUNK = 64
POOL_WINDOWS = (2, 4, 8, 16)
POOL_GROUPS = 4
POOL_GROUP_DIM = 128
POOL_WIDTH = POOL_GROUPS * POOL_GROUP_DIM
N_BRANCHES = 2
D_FF = 2816
ALPHA = (2.0 * DEPTH) ** 0.25
BETA_INIT = (8.0 * DEPTH) ** -0.25
LN_EPS = 1e-5
RMS_EPS = 1e-6
QKV_COLS = 3 * DN_WIDTH
Z_COLS = DN_WIDTH
B_COLS = DN_HEADS
A_COLS = DN_HEADS
POOL_COLS = POOL_WIDTH
GATE_COLS = N_BRANCHES * D_MODEL
IN_COLS = QKV_COLS + Z_COLS + B_COLS + A_COLS + POOL_COLS + GATE_COLS
IN_SPLITS = [QKV_COLS,
             QKV_COLS + Z_COLS,
             QKV_COLS + Z_COLS + B_COLS,
             QKV_COLS + Z_COLS + B_COLS + A_COLS,
             QKV_COLS + Z_COLS + B_COLS + A_COLS + POOL_COLS]

kernel_name = "hybrid_gated_deltanet_multiscale_pool_macaron_deepnorm"


def layer_norm(x, g, b):
    xf = x.astype(jnp.float32)
    mu = jnp.mean(xf, axis=-1, keepdims=True)
    var = jnp.mean(jnp.square(xf - mu), axis=-1, keepdims=True)
    return ((xf - mu) * lax.rsqrt(var + LN_EPS) * g + b).astype(x.dtype)


def swiglu(x, w_gate, w_up, w_down):
    return (jax.nn.silu(x @ w_gate) * (x @ w_up)) @ w_down


def causal_depthwise_conv_silu(x, w):
    K = w.shape[0]
    S_ = x.shape[1]
    xp = jnp.pad(x, ((0, 0), (K - 1, 0), (0, 0)))
    y = xp[:, 0:S_] * w[0]
    for j in range(1, K):
        y = y + xp[:, j:j + S_] * w[j]
    return jax.nn.silu(y)


def l2norm(x):
    return x * lax.rsqrt(jnp.sum(x * x, axis=-1, keepdims=True) + RMS_EPS)


def chunk_gated_delta_rule(q, k, v, g, beta):
    f32 = jnp.float32
    B_, S_, H, dk = q.shape
    dv = v.shape[-1]
    N = S_ // CHUNK
    q = l2norm(q.astype(f32)) * (dk ** -0.5)
    k = l2norm(k.astype(f32))
    v = v.astype(f32)
    g = g.astype(f32)
    beta = beta.astype(f32)

    def chunkify(t):
        t = t.reshape(B_, N, CHUNK, H, *t.shape[3:])
        return jnp.moveaxis(t, 3, 1)

    q, k, v, g, beta = map(chunkify, (q, k, v, g, beta))
    g = jnp.cumsum(g, axis=-1)
    causal = jnp.tril(jnp.ones((CHUNK, CHUNK), dtype=bool))
    strict = jnp.tril(jnp.ones((CHUNK, CHUNK), dtype=bool), -1)
    decay = jnp.exp(jnp.where(causal, g[..., :, None] - g[..., None, :], -jnp.inf))
    k_beta = k * beta[..., None]
    v_beta = v * beta[..., None]
    lower = jnp.where(strict, jnp.einsum('bhnik,bhnjk->bhnij', k_beta, k) * decay, 0.0)
    a_mat = lower + jnp.eye(CHUNK, dtype=f32)
    u = lax.linalg.triangular_solve(a_mat, v_beta, left_side=True, lower=True)
    w = lax.linalg.triangular_solve(a_mat, k_beta * jnp.exp(g)[..., None], left_side=True, lower=True)
    qk = jnp.where(causal, jnp.einsum('bhnik,bhnjk->bhnij', q, k) * decay, 0.0)
    q_dec = q * jnp.exp(g)[..., None]
    k_dec = k * jnp.exp(g[..., -1:] - g)[..., None]
    g_last = jnp.exp(g[..., -1])
    xs = tuple(jnp.moveaxis(t, 2, 0) for t in (qk, q_dec, k_dec, u, w, g_last))

    def step(state, inp):
        qk_c, qd_c, kd_c, u_c, w_c, gl_c = inp
        v_new = u_c - jnp.einsum('bhck,bhkv->bhcv', w_c, state)
        o_c = jnp.einsum('bhck,bhkv->bhcv', qd_c, state) + jnp.einsum('bhij,bhjv->bhiv', qk_c, v_new)
        state = state * gl_c[..., None, None] + jnp.einsum('bhck,bhcv->bhkv', kd_c, v_new)
        return state, o_c

    s0 = jnp.zeros((B_, H, dk, dv), f32)
    _, o = lax.scan(step, s0, xs)
    o = jnp.transpose(o, (1, 0, 3, 2, 4))
    return o.reshape(B_, S_, H, dv)


def multiscale_causal_pool(p):
    B_, S_, _ = p.shape
    pg = p.reshape(B_, S_, POOL_GROUPS, POOL_GROUP_DIM).astype(jnp.float32)
    csum = jnp.pad(jnp.cumsum(pg, axis=1), ((0, 0), (1, 0), (0, 0), (0, 0)))
    t = jnp.arange(S_)[:, None]
    win = jnp.array(POOL_WINDOWS, dtype=jnp.int32)[None, :]
    lo = jnp.maximum(t + 1 - win, 0)
    count = (t + 1 - lo).astype(jnp.float32)
    grp = jnp.arange(POOL_GROUPS)[None, :]
    mean = (csum[:, 1:] - csum[:, lo, grp]) / count[None, :, :, None]
    return (mean - pg).astype(p.dtype)


def hybrid_mixer(h, w_in, conv_w, a_log, dt_bias, dn_norm_g, dn_w_proj,
                 pool_w, pool_scale, pool_w_proj, w_out):
    B_, S_, _ = h.shape
    proj = h @ w_in
    qkv, z, b_raw, a_raw, p, gates = jnp.split(proj, IN_SPLITS, axis=-1)
    qkv = causal_depthwise_conv_silu(qkv, conv_w)
    q, k, v = jnp.split(qkv, 3, axis=-1)
    hd = (B_, S_, DN_HEADS, DN_HEAD_DIM)
    q, k, v = q.reshape(hd), k.reshape(hd), v.reshape(hd)
    beta = jax.nn.sigmoid(b_raw)
    g = -jnp.exp(a_log) * jax.nn.softplus(a_raw + dt_bias)
    o = chunk_gated_delta_rule(q, k, v, g, beta)
    o = o * lax.rsqrt(jnp.mean(o * o, axis=-1, keepdims=True) + RMS_EPS)
    o = o * dn_norm_g * jax.nn.silu(z.reshape(hd).astype(jnp.float32))
    y_dn = o.reshape(B_, S_, DN_WIDTH).astype(h.dtype) @ dn_w_proj
    pooled = multiscale_causal_pool(p)
    pooled = jnp.einsum('bsgc,gcd->bsgd', pooled, pool_w) * pool_scale
    y_pool = pooled.reshape(B_, S_, POOL_WIDTH) @ pool_w_proj
    g_dn, g_pool = jnp.split(gates, N_BRANCHES, axis=-1)
    merged = jax.nn.sigmoid(g_dn) * y_dn + jax.nn.sigmoid(g_pool) * y_pool
    return merged @ w_out


def setup_inputs(seed: int = 0) -> dict:
    key = jax.random.key(seed)
    ks = jax.random.split(key, 24)
    L = DEPTH
    nrm = lambda k, shape, scale: jax.random.normal(k, shape, jnp.float32) * scale
    gain = lambda k, shape: 1.0 + 0.02 * jax.random.normal(k, shape, jnp.float32)
    x = jax.random.normal(ks[0], (BATCH, SEQ, D_MODEL), jnp.float32)
    dt = jnp.exp(jax.random.uniform(ks[9], (L, DN_HEADS), jnp.float32,
                                    math.log(1e-3), math.log(1e-1)))
    return {
        "x": x,
        "ffn_pre_w_gate": nrm(ks[1], (L, D_MODEL, D_FF), D_MODEL ** -0.5),
        "ffn_pre_w_up": nrm(ks[2], (L, D_MODEL, D_FF), D_MODEL ** -0.5),
        "ffn_pre_w_down": nrm(ks[3], (L, D_FF, D_MODEL), D_FF ** -0.5 * BETA_INIT),
        "norm_pre_g": gain(ks[4], (L, D_MODEL)),
        "norm_pre_b": nrm(ks[5], (L, D_MODEL), 0.02),
        "mix_w_in": nrm(ks[6], (L, D_MODEL, IN_COLS), D_MODEL ** -0.5),
        "mix_conv_w": nrm(ks[7], (L, CONV_WIDTH, QKV_COLS), CONV_WIDTH ** -0.5),
        "dn_a_log": jnp.log(jax.random.uniform(ks[8], (L, DN_HEADS), jnp.float32, 1.0, 16.0)),
        "dn_dt_bias": dt + jnp.log(-jnp.expm1(-dt)),
        "dn_norm_g": gain(ks[10], (L, DN_HEAD_DIM)),
        "dn_w_proj": nrm(ks[11], (L, DN_WIDTH, D_MODEL), DN_WIDTH ** -0.5),
        "pool_w": nrm(ks[12], (L, POOL_GROUPS, POOL_GROUP_DIM, POOL_GROUP_DIM), POOL_GROUP_DIM ** -0.5),
        "pool_scale": 1.0 + 0.1 * jax.random.normal(ks[13], (L, POOL_GROUPS, POOL_GROUP_DIM), jnp.float32),
        "pool_w_proj": nrm(ks[14], (L, POOL_WIDTH, D_MODEL), POOL_WIDTH ** -0.5),
        "mix_w_out": nrm(ks[15], (L, D_MODEL, D_MODEL), D_MODEL ** -0.5 * BETA_INIT),
        "norm_mix_g": gain(ks[16], (L, D_MODEL)),
        "norm_mix_b": nrm(ks[17], (L, D_MODEL), 0.02),
        "ffn_post_w_gate": nrm(ks[18], (L, D_MODEL, D_FF), D_MODEL ** -0.5),
        "ffn_post_w_up": nrm(ks[19], (L, D_MODEL, D_FF), D_MODEL ** -0.5),
        "ffn_post_w_down": nrm(ks[20], (L, D_FF, D_MODEL), D_FF ** -0.5 * BETA_INIT),
        "norm_post_g": gain(ks[21], (L, D_MODEL)),
        "norm_post_b": nrm(ks[22], (L, D_MODEL), 0.02),
    }


def reference(x, ffn_pre_w_gate, ffn_pre_w_up, ffn_pre_w_down, norm_pre_g, norm_pre_b,
              mix_w_in, mix_conv_w, dn_a_log, dn_dt_bias, dn_norm_g, dn_w_proj,
              pool_w, pool_scale, pool_w_proj, mix_w_out, norm_mix_g, norm_mix_b,
              ffn_post_w_gate, ffn_post_w_up, ffn_post_w_down, norm_post_g, norm_post_b):
    h = x
    for l in range(DEPTH):
        f = swiglu(h, ffn_pre_w_gate[l], ffn_pre_w_up[l], ffn_pre_w_down[l])
        h = layer_norm(ALPHA * h + 0.5 * f, norm_pre_g[l], norm_pre_b[l])
        m = hybrid_mixer(h, mix_w_in[l], mix_conv_w[l], dn_a_log[l], dn_dt_bias[l], dn_norm_g[l],
                         dn_w_proj[l], pool_w[l], pool_scale[l], pool_w_proj[l], mix_w_out[l])
        h = layer_norm(ALPHA * h + m, norm_mix_g[l], norm_mix_b[l])
        f = swiglu(h, ffn_post_w_gate[l], ffn_post_w_up[l], ffn_post_w_down[l])
        h = layer_norm(ALPHA * h + 0.5 * f, norm_post_g[l], norm_post_b[l])
    return h
```

```python
import contextlib
import numpy as np
import concourse.bass as bass
import concourse.mybir as mybir
from concourse.bass_utils import run_bass_kernel_spmd

F32, BF16 = mybir.dt.float32, mybir.dt.bfloat16
AF = mybir.ActivationFunctionType
ALU = mybir.AluOpType

T = 2048
NB = 16
D = 1024
FF = 2816
NFC = 22
ALPHA = 2.0 ** 0.25
LN_EPS = 1e-5 / (ALPHA * ALPHA)
RMS_EPS = 1e-6
FFN_GROUPS = [list(range(0, 3)), list(range(3, 6)), list(range(6, 10)), list(range(10, 14)), list(range(14, 18)), list(range(18, 22))]
NSLOT = 9
MERGE_SEQ = False
SLOT_ELEMS = 4096


class Sy:
    def __init__(s, nc, es):
        s.nc = nc
        s.es = es
        s.eng = dict(pe=nc.tensor, act=nc.scalar, dve=nc.vector, pool=nc.gpsimd, sp=nc.sync)
        s.sem = {k: es.enter_context(nc.semaphore("sem_" + k)) for k in ("pe", "act", "dve", "pool")}
        s.cnt = {k: 0 for k in s.sem}
        s.waited = {}
        s.ndma = 0

    def sig(s, e, ins):
        s.cnt[e] += 1
        ins.then_inc(s.sem[e], 1)
        return (e, s.cnt[e])

    def new_dma_sem(s, name):
        sem = s.es.enter_context(s.nc.semaphore(name))
        return [sem, 0, name, 0]

    def dma(s, e, dsem, out, in_):
        if dsem[3] > 0:
            s.wait(e, ("dma", dsem, dsem[3]))
        ins = s.eng[e].dma_start(out=out, in_=in_)
        dsem[1] += 16
        ins.then_inc(dsem[0], 16)
        return ("dma", dsem, dsem[1])

    def wait(s, e, *tokens):
        flat = []

        def fl(ts):
            for t in ts:
                if t is None:
                    continue
                if isinstance(t, list):
                    fl(t)
                else:
                    flat.append(t)
        fl(tokens)
        best = {}
        for t in flat:
            key = ("dma:" + t[1][2]) if t[0] == "dma" else t[0]
            v = t[2] if t[0] == "dma" else t[1]
            if key not in best or best[key][0] < v:
                best[key] = (v, t)
        for t in [bt for _, bt in best.values()]:
            if t[0] == "dma":
                _, dsem, v = t
                key = (e, "dma:" + dsem[2])
                dsem[3] = max(dsem[3], v)
                if s.waited.get(key, 0) >= v:
                    continue
                s.eng[e].wait_ge(dsem[0], v)
                s.waited[key] = v
            else:
                src, v = t
                key = (e, src)
                if s.waited.get(key, 0) >= v:
                    continue
                s.eng[e].wait_ge(s.sem[src], v)
                s.waited[key] = v


class Ring:
    def __init__(s, sy, es, nc):
        s.sy = sy
        s.slots = [es.enter_context(nc.sbuf_tensor("wslot%d" % i, [128, SLOT_ELEMS], BF16)) for i in range(NSLOT)]
        s.dsem = [sy.new_dma_sem("wsem%d" % i) for i in range(NSLOT)]
        s.plan = []
        s.issued = {}
        s.next = 0
        s.free_tok = [None] * NSLOT
        s.slot_busy = [False] * NSLOT

    def add(s, name, parts):
        s.plan.append((name, parts))

    def _issue(s, slot):
        if s.next >= len(s.plan):
            return
        name, parts = s.plan[s.next]
        s.next += 1
        s.sy.wait("pool", s.free_tok[slot])
        tok = None
        for viewfn, src in parts:
            tok = s.sy.dma("pool", s.dsem[slot], viewfn(s.slots[slot]), src)
        s.issued[name] = (slot, tok)
        s.slot_busy[slot] = True

    def start(s):
        for i in range(NSLOT):
            s._issue(i)

    def get(s, name):
        slot, tok = s.issued[name]
        return s.slots[slot], tok

    def release(s, name, *tokens):
        slot, _ = s.issued.pop(name)
        s.free_tok[slot] = list(tokens)
        s.slot_busy[slot] = False
        s._issue(slot)


class Kern:
    def __init__(k, debug_stage=None):
        k.debug_stage = debug_stage
        k.nc = bass.Bass("TRN2", target_bir_lowering=False)

    def build(k):
        nc = k.nc
        dt = nc.dram_tensor
        k.x = dt("x", [T, D], F32, kind="ExternalInput").ap()
        k.w = {}
        for nm, shp in [("ffn_pre_w_gate", [D, FF]), ("ffn_pre_w_up", [D, FF]), ("ffn_pre_w_down", [FF, D]),
                        ("ffn_post_w_gate", [D, FF]), ("ffn_post_w_up", [D, FF]), ("ffn_post_w_down", [FF, D]),
                        ("norm_pre_g", [1, D]), ("norm_pre_b", [1, D]), ("norm_mix_g", [1, D]), ("norm_mix_b", [1, D]),
                        ("norm_post_g", [1, D]), ("norm_post_b", [1, D]),
                        ("mix_w_in", [D, 4616]), ("dn_w_proj", [512, D]), ("pool_w_proj", [512, D]), ("mix_w_out", [D, D]),
                        ("pool_w", [4, 128, 128]), ("cw", [128, 12, 4]), ("pscale", [128, 4]), ("dn_norm_g", [1, 128]),
                        ("dn_a_log", [1, 4]), ("dn_dt_bias", [1, 4]),
                        ("ident", [128, 128]), ("Umat", [128, 128]), ("SLmat", [128, 128]), ("mask_u", [128, 4, 128]),
                        ("nmask_bd", [128, 4, 128]), ("nmask_off", [128, 4, 128]), ("ident4", [128, 4, 128]), ("invc", [128, 16])]:
            k.w[nm] = dt(nm, shp, F32, kind="ExternalInput").ap()
        k.out = dt("out", [T, D], F32, kind="ExternalOutput").ap()
        k.scr = dt("scr", [T, D], F32, kind="Internal").ap()
        with contextlib.ExitStack() as es:
            k.es = es
            e = es.enter_context
            k.sy = sy = Sy(nc, es)
            k.hT = e(nc.sbuf_tensor("hT", [128, 8, T], BF16))
            k.ring = Ring(sy, es, nc)
            k.identf = e(nc.sbuf_tensor("identf", [128, 128], F32))
            k.identb = e(nc.sbuf_tensor("identb", [128, 128], BF16))
            k.xb = [e(nc.sbuf_tensor("xb%d" % i, [128, D], BF16)) for i in range(2)]
            k.st = e(nc.sbuf_tensor("st", [128, 4, 2, 6], F32))
            k.mv = e(nc.sbuf_tensor("mv", [128, 4, 2], F32))
            k.sd = e(nc.sbuf_tensor("sd", [128, 4], F32))
            k.rstd = e(nc.sbuf_tensor("rstd", [128, 4], F32))
            k.w_ab = e(nc.sbuf_tensor("w_ab", [128, 8, 8], BF16))
            k.w_pw = e(nc.sbuf_tensor("w_pw", [128, 4, 128], BF16))
            k.ds_x = [sy.new_dma_sem("ds_x%d" % i) for i in range(4)]
            k.ds_id = sy.new_dma_sem("ds_id")
            k.ds_p = sy.new_dma_sem("ds_p")
            k.ds_o = sy.new_dma_sem("ds_o")
            k.ds_c = sy.new_dma_sem("ds_c")
            k.ds_s = sy.new_dma_sem("ds_s")
            k.ds_r2 = [sy.new_dma_sem("ds_r2_%d" % i) for i in range(4)]
            k.ds_s4 = [sy.new_dma_sem("ds_s4_%d" % i) for i in range(4)]
            k.ds_sm = sy.new_dma_sem("ds_sm")
            k.free = {}
            k.ln_tok_last = None
            k.out_tok = []
            k.scr_tok = []
            k.trc = 0
            k.psT_free = [None, None]
            k.xb_free = [None, None]
            k.hT_tok = [None] * NB
            k.hT_rd = [None] * 4
            Win = k.w["mix_w_in"].rearrange("(kc p) n -> p kc n", p=128)
            sy.dma("pool", k.ds_sm, k.w_ab[:], Win[:, :, 2048:2056])
            k.sm_tok = sy.dma("pool", k.ds_sm, k.w_pw[:], k.w["pool_w"].rearrange("g c d -> c g d"))
            k.plan_weights()
            k.ring.start()
            t_id = sy.dma("sp", k.ds_id, k.identf[:], k.w["ident"])
            sy.wait("dve", t_id)
            k.id_tok = sy.sig("dve", nc.vector.tensor_copy(out=k.identb[:], in_=k.identf[:]))
            stage = k.debug_stage
            k.ffn_phase("ffn_pre", "norm_pre", k.x, first=True, mode=("out" if stage == "A" else "spill"))
            if stage != "A":
                k.barrier()
                k.mixer_phase(mode=("out" if stage == "B" else "spill"))
                if stage != "B":
                    k.barrier()
                    k.ffn_phase("ffn_post", "norm_post", k.scr, first=False, mode="out")
            sy.wait("sp", k.out_tok)
        return nc

    def barrier(k):
        sy = k.sy
        toks = [(e, sy.cnt[e]) for e in ("pe", "act", "dve", "pool") if sy.cnt[e] > 0]
        for e in ("pe", "act", "dve", "pool", "sp"):
            sy.wait(e, toks, k.scr_tok, k.out_tok)

    def plan_weights(k):
        ring = k.ring

        def v8(cols):
            return lambda s, h: s[:, 0:8 * cols].rearrange("p (kc c) -> p kc c", kc=8)[:, 4 * h:4 * h + 4, :]

        def ffn(prefix):
            Wg = k.w[prefix + "_w_gate"].rearrange("(kc p) n -> p kc n", p=128)
            Wu = k.w[prefix + "_w_up"].rearrange("(kc p) n -> p kc n", p=128)
            Wd = k.w[prefix + "_w_down"].rearrange("(fc p) n -> p fc n", p=128)
            for gi, grp in enumerate(FFN_GROUPS):
                n = len(grp)
                c0, c1 = grp[0] * 128, (grp[-1] + 1) * 128
                for nm, W in (("g", Wg), ("u", Wu)):
                    parts = []
                    for h in range(2):
                        parts.append((lambda s, f=v8(n * 128), h=h: f(s, h), W[:, 4 * h:4 * h + 4, c0:c1]))
                    ring.add("%s_%s%d" % (prefix, nm, gi), parts)
                parts = []
                for h in range(2):
                    a = 0 if h == 0 else n // 2
                    b = n // 2 if h == 0 else n
                    parts.append((lambda s, n=n, a=a, b=b: s[:, 0:n * 1024].rearrange("p (fc c) -> p fc c", fc=n)[:, a:b, :],
                                  Wd[:, grp[0] + a:grp[0] + b, :]))
                ring.add("%s_d%d" % (prefix, gi), parts)

        def win(name, c0):
            Win = k.w["mix_w_in"].rearrange("(kc p) n -> p kc n", p=128)
            parts = [(lambda s, f=v8(512), h=h: f(s, h), Win[:, 4 * h:4 * h + 4, c0:c0 + 512]) for h in range(2)]
            ring.add(name, parts)

        def k4(name, W):
            Wv = W.rearrange("(kc p) n -> p kc n", p=128)
            parts = [(lambda s, h=h: s[:, :].rearrange("p (kc c) -> p kc c", kc=4)[:, 2 * h:2 * h + 2, :], Wv[:, 2 * h:2 * h + 2, :])
                     for h in range(2)]
            ring.add(name, parts)

        ffn("ffn_pre")
        if k.debug_stage != "A":
            win("mix_q", 0)
            win("mix_k", 512)
            win("mix_v", 1024)
            win("mix_z", 1536)
            win("mix_p", 2056)
            k4("dn_proj", k.w["dn_w_proj"])
            win("gdn0", 2568)
            win("gdn1", 3080)
            k4("pool_proj", k.w["pool_w_proj"])
            win("gpool0", 3592)
            win("gpool1", 4104)
            k4("wout0", k.w["mix_w_out"][0:512, :])
            k4("wout1", k.w["mix_w_out"][512:1024, :])
            if k.debug_stage != "B":
                ffn("ffn_post")

    def emit_transposes(k, src_ap, b, src_tok, psT):
        nc, sy = k.nc, k.sy
        i = k.trc % 2
        k.trc += 1
        sy.wait("act", src_tok, k.xb_free[i])
        tB = sy.sig("act", nc.scalar.copy(out=k.xb[i][:], in_=src_ap))
        sy.wait("pe", tB, k.psT_free[i], k.id_tok)
        for kc in range(8):
            ins = nc.tensor.transpose(out=psT[i][:, kc, :], in_=k.xb[i][:, kc * 128:(kc + 1) * 128], identity=k.identb[:])
        tT = sy.sig("pe", ins)
        k.xb_free[i] = tT
        sy.wait("act", tT, k.hT_rd[b // 4])
        tH = sy.sig("act", nc.scalar.copy(out=k.hT[:, :, b * 128:(b + 1) * 128], in_=psT[i][:]))
        k.psT_free[i] = tH
        k.hT_tok[b] = tH

    def load_ln_params(k, norm):
        sy = k.sy
        sy.wait("sp", k.free.get("gam"))
        sy.dma("sp", k.ds_p, k.gam[:], k.w[norm + "_g"].to_broadcast([128, D]))
        return sy.dma("sp", k.ds_p, k.bet[:], k.w[norm + "_b"].to_broadcast([128, D]))

    def layernorm(k, blocks, tok_in, tgb):
        nc, sy = k.nc, k.sy
        nb = len(blocks)
        sy.wait("dve", tok_in, tgb, k.ln_tok_last)
        for bb, Rb in enumerate(blocks):
            nc.vector.bn_stats(out=k.st[:, bb, 0, :], in_=Rb[:, 0:512])
            ins = nc.vector.bn_stats(out=k.st[:, bb, 1, :], in_=Rb[:, 512:1024])
        t1 = sy.sig("dve", ins)
        sy.wait("dve", t1)
        for bb in range(nb):
            ins = nc.vector.bn_aggr(out=k.mv[:, bb, :], in_=k.st[:, bb, :, :])
        t2 = sy.sig("dve", ins)
        sy.wait("act", t2)
        tA = sy.sig("act", nc.scalar.activation(out=k.sd[:, 0:nb], in_=k.mv[:, 0:nb, 1], func=AF.Sqrt, bias=LN_EPS, scale=1.0))
        sy.wait("dve", tA)
        t3 = sy.sig("dve", nc.vector.reciprocal(out=k.rstd[:, 0:nb], in_=k.sd[:, 0:nb]))
        sy.wait("dve", t3)
        for bb, Rb in enumerate(blocks):
            ins = nc.vector.scalar_tensor_tensor(out=Rb, in0=Rb, scalar=k.mv[:, bb, 0:1], in1=k.gam[:], op0=ALU.subtract, op1=ALU.mult)
        t4 = sy.sig("dve", ins)
        sy.wait("dve", t4)
        for bb, Rb in enumerate(blocks):
            ins = nc.vector.scalar_tensor_tensor(out=Rb, in0=Rb, scalar=k.rstd[:, bb:bb + 1], in1=k.bet[:], op0=ALU.mult, op1=ALU.add)
        t6 = sy.sig("dve", ins)
        k.ln_tok_last = t6
        return t6

    def ffn_phase(k, prefix, norm, src, first, mode):
        nc, sy, ring = k.nc, k.sy, k.ring
        c = 0.5 / ALPHA
        NG = len(FFN_GROUPS)
        with contextlib.ExitStack() as es:
            e = es.enter_context
            R = e(nc.sbuf_tensor("R_" + prefix, [128, NB, D], F32))
            k.gam = e(nc.sbuf_tensor("gam_" + prefix, [128, D], F32))
            k.bet = e(nc.sbuf_tensor("bet_" + prefix, [128, D], F32))
            hid = [e(nc.sbuf_tensor("hid%d_%s" % (i, prefix), [128, 4, 512], BF16)) for i in range(2)]
            sg = [e(nc.sbuf_tensor("sg%d_%s" % (i, prefix), [128, 512], BF16)) for i in range(2)]
            psG = [e(nc.psum_tensor("psG%d_%s" % (i, prefix), [128, 512], F32)) for i in range(2)]
            psU = [e(nc.psum_tensor("psU%d_%s" % (i, prefix), [128, 512], F32)) for i in range(2)]
            psO = [e(nc.psum_tensor("psO%d_%s" % (i, prefix), [128, 512], F32)) for i in range(2)]
            psT = [e(nc.psum_tensor("psT%d_%s" % (i, prefix), [128, 8, 128], BF16)) for i in range(2)]
            k.psT_free = [None, None]
            fr = {}
            tgb = k.load_ln_params(norm)
            sv = src.rearrange("(b p) d -> p b d", p=128)
            R_tok = []
            for q in range(4):
                tq = sy.dma("sp", k.ds_x[q], R[:, 4 * q:4 * q + 4, :], sv[:, 4 * q:4 * q + 4, :])
                R_tok += [tq] * 4
            if first:
                for b in range(NB):
                    k.emit_transposes(R[:, b, :], b, R_tok[b], psT)
            gu_ctr = [0]
            o_ctr = [0]
            pend = []
            pend_ln = []
            lastln = [None]

            def GU(gi, t, pp):
                grp = FFN_GROUPS[gi]
                n = len(grp)
                Wg_s, tWg = ring.get("%s_g%d" % (prefix, gi))
                Wu_s, tWu = ring.get("%s_u%d" % (prefix, gi))
                Wg_v = Wg_s[:, 0:8 * n * 128].rearrange("p (kc c) -> p kc c", kc=8)
                Wu_v = Wu_s[:, 0:8 * n * 128].rearrange("p (kc c) -> p kc c", kc=8)
                toks = []
                lastmm = None
                for i in range(n):
                    j = gu_ctr[0] % 2
                    gu_ctr[0] += 1
                    sy.wait("pe", tWg, tWu, [k.hT_tok[b] for b in range(4 * t, 4 * t + 4)], fr.get("psG%d" % j), fr.get("psU%d" % j))
                    for kc in range(8):
                        ins = nc.tensor.matmul(psG[j][:], lhsT=Wg_v[:, kc, i * 128:(i + 1) * 128], rhs=k.hT[:, kc, t * 512:(t + 1) * 512],
                                               start=(kc == 0), stop=(kc == 7))
                    tG = sy.sig("pe", ins)
                    for kc in range(8):
                        ins = nc.tensor.matmul(psU[j][:], lhsT=Wu_v[:, kc, i * 128:(i + 1) * 128], rhs=k.hT[:, kc, t * 512:(t + 1) * 512],
                                               start=(kc == 0), stop=(kc == 7))
                    tU = sy.sig("pe", ins)
                    lastmm = tU
                    sy.wait("act", tG, fr.get("sg%d" % j))
                    tS = sy.sig("act", nc.scalar.activation(out=sg[j][:], in_=psG[j][:], func=AF.Silu))
                    fr["psG%d" % j] = tS
                    sy.wait("dve", tU, tS, fr.get("hid%d" % pp))
                    tH = sy.sig("dve", nc.vector.tensor_tensor(out=hid[pp][:, i, :], in0=sg[j][:], in1=psU[j][:], op=ALU.mult))
                    fr["psU%d" % j] = tH
                    fr["sg%d" % j] = tH
                    toks.append(tH)
                    slot()
                k.hT_rd[t] = lastmm
                if t == 3:
                    ring.release("%s_g%d" % (prefix, gi), lastmm)
                    ring.release("%s_u%d" % (prefix, gi), lastmm)
                return toks

            def DOWN(gi, t, pp, toks):
                grp = FFN_GROUPS[gi]
                n = len(grp)
                Wd_s, tWd = ring.get("%s_d%d" % (prefix, gi))
                Wd_v = Wd_s[:, 0:n * 1024].rearrange("p (fc c) -> p fc c", fc=n)
                lastmm = None
                rtok = None
                for bb in range(4):
                    b = 4 * t + bb
                    for half in range(2):
                        j = o_ctr[0] % 2
                        o_ctr[0] += 1
                        sy.wait("pe", tWd, toks, fr.get("psO%d" % j))
                        for i in range(n):
                            ins = nc.tensor.matmul(psO[j][:], lhsT=hid[pp][:, i, bb * 128:(bb + 1) * 128],
                                                   rhs=Wd_v[:, i, half * 512:(half + 1) * 512], start=(i == 0), stop=(i == n - 1))
                        tO = sy.sig("pe", ins)
                        lastmm = tO
                        sy.wait("dve", tO, R_tok[b])
                        Rv = R[:, b, half * 512:(half + 1) * 512]
                        rtok = sy.sig("dve", nc.vector.scalar_tensor_tensor(out=Rv, in0=psO[j][:], scalar=c, in1=Rv,
                                                                            op0=ALU.mult, op1=ALU.add))
                        fr["psO%d" % j] = rtok
                fr["hid%d" % pp] = lastmm
                if t == 3:
                    ring.release("%s_d%d" % (prefix, gi), lastmm)
                if gi == NG - 1:
                    for bb in range(4):
                        pend_ln.append((4 * t + bb, rtok))

            def ln_block(b, rtok):
                t6 = k.layernorm([R[:, b, :]], rtok, tgb)
                lastln[0] = t6
                sy.wait("sp", t6)
                if mode == "out":
                    ov = k.out.rearrange("(b p) d -> p b d", p=128)
                    k.out_tok.append(sy.dma("sp", k.ds_o, ov[:, b, :], R[:, b, :]))
                else:
                    ov = k.scr.rearrange("(b p) d -> p b d", p=128)
                    k.scr_tok.append(sy.dma("sp", k.ds_s, ov[:, b, :], R[:, b, :]))
                    pend.append((b, t6))

            def slot():
                if pend:
                    b_, t6_ = pend.pop(0)
                    k.emit_transposes(R[:, b_, :], b_, t6_, psT)
                if pend_ln:
                    ln_block(*pend_ln.pop(0))

            seq = [(gi, t) for gi in range(NG) for t in range(4)]
            prev = None
            for idx, (gi, t) in enumerate(seq):
                pp = idx % 2
                toks = GU(gi, t, pp)
                if prev is not None:
                    DOWN(*prev)
                prev = (gi, t, pp, toks)
            DOWN(*prev)
            while pend_ln or pend:
                slot()
            k.free["gam"] = lastln[0]

    def run(k, eng, fn, rd=(), wr=(), extra=(), cost=None):
        if getattr(k, "rec", None) is not None:
            k.rec.append((eng, fn, tuple(rd), tuple(wr), tuple(extra), cost))
            return None
        sy = k.sy
        toks = list(extra)
        for b in rd:
            toks.append(k.wtok.get(b))
        for b in wr:
            toks.append(k.wtok.get(b))
            toks += list(k.rtoks.get(b, {}).values())
        sy.wait(eng, toks)
        t = sy.sig(eng, fn())
        for b in rd:
            k.rtoks.setdefault(b, {})[eng] = t
        for b in wr:
            k.wtok[b] = t
            k.rtoks[b] = {}
        return t

    def record(k, fns):
        k.rec = []
        for f in fns:
            f()
        ops, k.rec = k.rec, None
        return ops

    def merge_emit(k, seqs):
        eng_free = {}
        avail = {}
        heads = [0] * len(seqs)
        DEF = {"pe": 0.5, "act": 0.65, "dve": 0.7, "pool": 2.5}
        while True:
            best = None
            for si, seq in enumerate(seqs):
                if heads[si] >= len(seq):
                    continue
                eng, fn, rd, wr, extra, cost = seq[heads[si]]
                st = eng_free.get(eng, 0.0)
                for b in rd + wr:
                    st = max(st, avail.get(b, 0.0))
                if MERGE_SEQ:
                    if best is None:
                        best = (st, si)
                elif best is None or st < best[0] - 1e-9:
                    best = (st, si)
            if best is None:
                break
            st, si = best
            eng, fn, rd, wr, extra, cost = seqs[si][heads[si]]
            heads[si] += 1
            c = cost if cost is not None else DEF[eng]
            fin = st + c
            eng_free[eng] = fin
            for b in wr:
                avail[b] = fin + 0.3
            for b in rd:
                avail[b] = max(avail.get(b, 0.0), st + 0.05)
            k.run(eng, fn, rd=rd, wr=wr, extra=extra)

    def mixer_phase(k, mode):
        nc, sy, ring = k.nc, k.sy, k.ring
        V, A_, PE, G = nc.vector, nc.scalar, nc.tensor, nc.gpsimd
        ENG = {"dve": V, "act": A_, "pool": G}
        k.wtok, k.rtoks = {}, {}
        run = k.run
        with contextlib.ExitStack() as es:
            e = es.enter_context

            def sb(name, shape, dtp=F32):
                return e(nc.sbuf_tensor("mx_" + name, shape, dtp))

            NBK = [4]
            bank = [e(nc.psum_tensor("mxb%d" % i, [128, 512], F32)) for i in range(6)]
            psTb = [e(nc.psum_tensor("mxT%d" % i, [128, 8, 128], BF16)) for i in range(2)]
            k.psT_free = [None, None]
            bctr = [0]
            tctr = [0]

            def getbank(pool=None):
                if pool == "prep":
                    i = pctr[0] % 3
                    pctr[0] += 1
                    return i
                if pool == "scan":
                    return 3
                i = bctr[0] % NBK[0]
                bctr[0] += 1
                return i
            pctr = [0]

            def gett():
                i = tctr[0] % 2
                tctr[0] += 1
                return i

            o_gT = sb("o_gT", [128, 4, T], BF16)
            Um = sb("Um", [128, 128]); SLm = sb("SLm", [128, 128]); ones_f = sb("ones_f", [128, 128])
            ones_b = sb("ones_b", [128, 128], BF16)
            cst = sb("cst", [128, 4, 128])
            mask_u = sb("mask_u", [128, 4, 128], BF16); nmask = sb("nmask", [128, 4, 128], BF16); nmoff = sb("nmoff", [128, 4, 128], BF16)
            id4 = sb("id4", [128, 4, 128], BF16)
            gnorm = sb("gnorm", [128, 4, 128], BF16)
            cw = sb("cw", [128, 12, 4]); pscale = sb("pscale", [128, 4])
            alog = sb("alog", [128, 4]); dtb = sb("dtb", [128, 4]); invc = sb("invc", [128, 16]); ea = sb("ea", [128, 4])
            for dst, nm in ((mask_u, "mask_u"), (nmask, "nmask_bd"), (nmoff, "nmask_off"), (id4, "ident4")):
                tq = sy.dma("sp", k.ds_c, cst[:], k.w[nm])
                sy.wait("sp", tq)
                k.wtok["cst"] = tq
                run("dve", lambda dst=dst: V.tensor_copy(out=dst[:], in_=cst[:]), rd=["cst"], wr=[nm])
                sy.wait("sp", k.wtok[nm])
            for h in range(4):
                tq = sy.dma("sp", k.ds_c, cst[:, h, :], k.w["dn_norm_g"].to_broadcast([128, 128]))
            k.wtok["cst"] = tq
            run("dve", lambda: V.tensor_copy(out=gnorm[:], in_=cst[:]), rd=["cst"], wr=["gnorm"])
            for dst, nm in ((Um, "Umat"), (SLm, "SLmat"), (cw, "cw"), (pscale, "pscale"), (invc, "invc")):
                tq = sy.dma("sp", k.ds_c, dst[:], k.w[nm])
            sy.dma("sp", k.ds_c, alog[:], k.w["dn_a_log"].to_broadcast([128, 4]))
            ctok = sy.dma("sp", k.ds_c, dtb[:], k.w["dn_dt_bias"].to_broadcast([128, 4]))
            for nm in ("Um", "SLm", "cw", "pscale", "invc", "alog", "dtb"):
                k.wtok[nm] = ctok
            run("dve", lambda: V.memset(ones_f[:], 1.0), wr=["ones_f"])
            run("dve", lambda: V.memset(ones_b[:], 1.0), wr=["ones_b"])
            run("act", lambda: A_.activation(out=ea[:], in_=alog[:], func=AF.Exp), rd=["alog"], wr=["ea"])
            for b in range(NB):
                k.wtok[("hT", b)] = k.hT_tok[b]
            hTt = lambda t: [("hT", b) for b in range(4 * t, 4 * t + 4)]
            k.wtok["w_ab"] = k.sm_tok
            k.wtok["w_pw"] = k.sm_tok

            def getw(name, kc):
                s_, tk = ring.get(name)
                k.wtok["W:" + name] = tk
                return s_[:, :].rearrange("p (kc c) -> p kc c", kc=kc)

            with contextlib.ExitStack() as es1:
                e1 = es1.enter_context

                def sb1(name, shape, dtp=F32):
                    return e1(nc.sbuf_tensor("b1_" + name, shape, dtp))
                ab = sb1("ab", [128, 16, 8]); beta = sb1("beta", [128, 16, 4]); lnb = sb1("lnb", [128, 16, 4])
                gt = sb1("gt", [128, 16, 4]); xs = sb1("xs", [128, 16, 4])
                egc = sb1("egc", [128, 64]); negc = sb1("negc", [128, 64]); kds = sb1("kds", [128, 64]); egl = sb1("egl", [128, 64])
                g2 = gt[:].rearrange("p b h -> p (b h)")
                lnb2 = lnb[:].rearrange("p b h -> p (b h)")
                bi = getbank()

                def f():
                    for b in range(NB):
                        for kc in range(8):
                            ins = PE.matmul(bank[bi][:, b * 8:(b + 1) * 8], lhsT=k.hT[:, kc, b * 128:(b + 1) * 128], rhs=k.w_ab[:, kc, :],
                                            start=(kc == 0), stop=(kc == 7))
                    return ins
                run("pe", f, rd=[("hT", b) for b in range(NB)] + ["w_ab"], wr=[("bk", bi)])
                run("act", lambda: A_.copy(out=ab[:], in_=bank[bi][:, 0:128].rearrange("p (b c) -> p b c", b=16)), rd=[("bk", bi)], wr=["ab"])
                run("act", lambda: A_.activation(out=beta[:], in_=ab[:, :, 0:4], func=AF.Sigmoid), rd=["ab"], wr=["beta"])
                run("act", lambda: A_.activation(out=lnb[:], in_=beta[:], func=AF.Ln), rd=["beta"], wr=["lnb"])
                run("dve", lambda: V.tensor_tensor(out=xs[:], in0=ab[:, :, 4:8], in1=dtb[:][:, None, :].broadcast_to([128, 16, 4]), op=ALU.add),
                    rd=["ab", "dtb"], wr=["xs"])
                run("act", lambda: A_.activation(out=xs[:], in_=xs[:], func=AF.Exp), rd=["xs"], wr=["xs"])
                run("act", lambda: A_.activation(out=xs[:], in_=xs[:], func=AF.Ln, bias=1.0, scale=1.0), rd=["xs"], wr=["xs"])
                run("dve", lambda: V.scalar_tensor_tensor(out=gt[:], in0=xs[:], scalar=-1.0, in1=ea[:][:, None, :].broadcast_to([128, 16, 4]),
                                                         op0=ALU.mult, op1=ALU.mult), rd=["xs", "ea"], wr=["gt"])
                pc = [sb1("pc%d" % i, [128, 515]) for i in range(2)]
                acc = [sb1("acc%d" % i, [128, 512]) for i in range(2)]
                sq = [sb1("sq%d" % i, [128, 512], BF16) for i in range(2)]
                hist = sb1("hist", [128, 12, 3])
                qkvT = sb1("qkvT", [128, 12, 512], BF16)
                kvtok = sb1("kvtok", [128, 4, 8, 128], BF16)
                gz = sb1("gz", [128, 4, 512], BF16)
                gSL = sb1("gSL", [128, 4, 128]); Db = sb1("Db", [128, 4, 128]); t1b = sb1("t1b", [128, 4, 128])
                Pm = [sb1("Pm%d" % i, [128, 4, 128], BF16) for i in range(2)]
                Pt = [sb1("Pt%d" % i, [128, 4, 128], BF16) for i in range(2)]
                Nt = [sb1("Nt%d" % i, [128, 4, 128], BF16) for i in range(2)]
                kdecb = [sb1("kdecb%d" % i, [128, 4, 128], BF16) for i in range(2)]
                QKb = [sb1("QKb%d" % i, [128, 4, 128], BF16) for i in range(2)]
                Tt = [sb1("Tt%d" % i, [128, 4, 128], BF16) for i in range(2)]
                S = sb1("S", [128, 4, 128]); Shi = sb1("Shi", [128, 4, 128], BF16); Slo = sb1("Slo", [128, 4, 128], BF16)
                r1f = sb1("r1f", [128, 4, 128]); r1b = sb1("r1b", [128, 4, 128], BF16); x1b = sb1("x1b", [128, 4, 128], BF16)
                r2b = sb1("r2b", [128, 4, 128], BF16)
                xhi = sb1("xhi", [128, 4, 128], BF16)
                o2s = sb1("o2s", [128, 4, 128]); ot = sb1("ot", [128, 4, 128])
                ssq = sb1("ssq", [128, 4]); rr = sb1("rr", [128, 4]); og = sb1("og", [128, 4, 128], BF16)
                fl = lambda a: a[:].rearrange("p h d -> p (h d)")
                run("dve", lambda: V.memset(hist[:], 0.0), wr=[("hist", ch) for ch in range(12)])
                run("dve", lambda: V.memset(S[:], 0.0), wr=["S"])
                run("dve", lambda: V.memset(Shi[:], 0.0), wr=["Shi"])
                run("dve", lambda: V.memset(Slo[:], 0.0), wr=["Slo"])
                Wq = getw("mix_q", 8); Wk = getw("mix_k", 8); Wv = getw("mix_v", 8); Wz = getw("mix_z", 8)

                def proj_front(t, ch):
                    tsl = slice(t * 512, (t + 1) * 512)
                    Wx, wn = (Wq, "W:mix_q") if ch < 4 else ((Wk, "W:mix_k") if ch < 8 else (Wv, "W:mix_v"))
                    c0 = (ch % 4) * 128
                    bi = getbank()
                    pi = ch % 2
                    pcb, accb = pc[pi], acc[pi]
                    pk, ak = ("pc", pi), ("acc", pi)

                    def f():
                        for kc in range(8):
                            ins = PE.matmul(bank[bi][:], lhsT=Wx[:, kc, c0:c0 + 128], rhs=k.hT[:, kc, tsl], start=(kc == 0), stop=(kc == 7))
                        return ins
                    run("pe", f, rd=hTt(t) + [wn], wr=[("bk", bi)])
                    run("act", lambda: A_.copy(out=pcb[:, 3:515], in_=bank[bi][:]), rd=[("bk", bi)], wr=[pk])
                    run("act", lambda: A_.mul(out=accb[:], in_=bank[bi][:], mul=cw[:, ch, 3:4]), rd=[("bk", bi), "cw"], wr=[ak])
                    run("dve", lambda: V.tensor_copy(out=pcb[:, 0:3], in_=hist[:, ch, :]), rd=[("hist", ch)], wr=[pk])
                    run("dve", lambda: V.tensor_copy(out=hist[:, ch, :], in_=pcb[:, 512:515]), rd=[pk], wr=[("hist", ch)])

                def proj_back(t, ch):
                    pi = ch % 2
                    pcb, accb = pc[pi], acc[pi]
                    pk, ak = ("pc", pi), ("acc", pi)
                    for j in (0, 1, 2):
                        run("dve", lambda j=j: V.scalar_tensor_tensor(out=accb[:], in0=pcb[:, j:j + 512], scalar=cw[:, ch, j:j + 1], in1=accb[:],
                                                                     op0=ALU.mult, op1=ALU.add), rd=[pk, ak], wr=[ak])
                    run("act", lambda: A_.activation(out=qkvT[:, ch, :], in_=accb[:], func=AF.Silu), rd=[ak], wr=[("qkvT", ch)])

                l2b = {}

                def l2norm_front(t, ch):
                    pi = ch % 2
                    sqb, sk_ = sq[pi], ("sq", pi)
                    run("act", lambda: A_.activation(out=sqb[:], in_=qkvT[:, ch, :], func=AF.Square), rd=[("qkvT", ch)], wr=[sk_])
                    b2 = getbank()
                    l2b[ch] = b2
                    run("pe", lambda: PE.matmul(bank[b2][:], lhsT=ones_b[:], rhs=sqb[:], start=True, stop=True), rd=[sk_, "ones_b"], wr=[("bk", b2)])

                def l2norm_back(t, ch):
                    pi = ch % 2
                    rinv, rk_ = acc[pi], ("acc", pi)
                    b2 = l2b[ch]
                    sc = 128.0 if ch < 4 else 1.0
                    run("act", lambda: A_.activation(out=rinv[:], in_=bank[b2][:], func=AF.Ln, bias=RMS_EPS * sc, scale=sc),
                        rd=[("bk", b2)], wr=[rk_])
                    run("act", lambda: A_.activation(out=rinv[:], in_=rinv[:], func=AF.Exp, scale=-0.5), rd=[rk_], wr=[rk_])
                    run("dve", lambda: V.tensor_tensor(out=qkvT[:, ch, :], in0=qkvT[:, ch, :], in1=rinv[:], op=ALU.mult),
                        rd=[("qkvT", ch), rk_], wr=[("qkvT", ch)])

                def kv_transposes(t, bb):
                    i = gett()

                    def f():
                        for c8 in range(8):
                            ins = PE.transpose(out=psTb[i][:, c8, :], in_=qkvT[:, 4 + c8, bb * 128:(bb + 1) * 128], identity=k.identb[:])
                        return ins
                    run("pe", f, rd=[("qkvT", 4 + c8) for c8 in range(8)], wr=[("pt", i)], extra=[k.id_tok])
                    run("act", lambda: A_.copy(out=kvtok[:, bb, :, :], in_=psTb[i][:]), rd=[("pt", i)], wr=[("kvtok", bb)])

                def z_block(t, bb):
                    b = 4 * t + bb
                    bi = getbank()

                    def f():
                        for kc in range(8):
                            ins = PE.matmul(bank[bi][:], lhsT=k.hT[:, kc, b * 128:(b + 1) * 128], rhs=Wz[:, kc, :], start=(kc == 0), stop=(kc == 7))
                        return ins
                    run("pe", f, rd=[("hT", b), "W:mix_z"], wr=[("bk", bi)])
                    run("act", lambda: A_.activation(out=gz[:, bb, :], in_=bank[bi][:], func=AF.Silu), rd=[("bk", bi)], wr=[("gz", bb)])
                    run("dve", lambda: V.tensor_tensor(out=gz[:, bb, :], in0=gz[:, bb, :], in1=fl(gnorm), op=ALU.mult),
                        rd=[("gz", bb), "gnorm"], wr=[("gz", bb)])

                NIT = 5

                def prep_pieces(t, bb):
                    c = 4 * t + bb
                    par = c % 2
                    csl = slice(bb * 128, (bb + 1) * 128)
                    col = lambda h: slice(c * 4 + h, c * 4 + h + 1)
                    kd, qk, tt = kdecb[par], QKb[par], Tt[par]
                    kdk, qkk, ttk = ("kdecb", par), ("QKb", par), ("Tt", par)
                    pieces = []

                    def pA():
                        def f():
                            for h in range(4):
                                ins = V.tensor_scalar(out=kd[:, h, :], in0=kvtok[:, bb, h, :], scalar1=kds[:, col(h)], scalar2=None, op0=ALU.mult)
                            return ins
                        run("dve", f, rd=[("kvtok", bb), "kds"], wr=[kdk])

                        def f():
                            for h in range(4):
                                ins = V.tensor_scalar(out=gSL[:, h, :], in0=SLm[:], scalar1=g2[:, col(h)], scalar2=None, op0=ALU.mult)
                            return ins
                        run("dve", f, rd=["SLm", "gt"], wr=["gSL"])
                        bA, bQK, bLD = getbank("prep"), getbank("prep"), getbank("prep")

                        def f():
                            for h in range(4):
                                hs = slice(h * 128, (h + 1) * 128)
                                PE.matmul(bank[bA][:, hs], lhsT=qkvT[:, 4 + h, csl], rhs=qkvT[:, 4 + h, csl], start=True, stop=True)
                                ins = PE.matmul(bank[bQK][:, hs], lhsT=qkvT[:, 4 + h, csl], rhs=qkvT[:, h, csl], start=True, stop=True)
                            return ins
                        run("pe", f, rd=[("qkvT", c8) for c8 in range(8)], wr=[("bk", bA), ("bk", bQK)])

                        def f():
                            for h in range(4):
                                ins = PE.matmul(bank[bLD][:, h * 128:(h + 1) * 128], lhsT=gSL[:, h, :], rhs=Um[:], start=True, stop=True)
                            return ins
                        run("pe", f, rd=["gSL", "Um"], wr=[("bk", bLD)])

                        def f():
                            for h in range(4):
                                ins = A_.activation(out=Db[:, h, :], in_=bank[bLD][:, h * 128:(h + 1) * 128], func=AF.Exp,
                                                    bias=lnb2[:, col(h)], scale=1.0)
                            return ins
                        run("act", f, rd=[("bk", bLD), "lnb"], wr=["Db"])
                        run("dve", lambda: V.tensor_tensor(out=fl(t1b), in0=bank[bA][:], in1=fl(Db), op=ALU.mult), rd=[("bk", bA), "Db"], wr=["t1b"])
                        run("dve", lambda: V.tensor_tensor(out=Pt[0][:], in0=t1b[:], in1=nmask[:], op=ALU.mult), rd=["t1b", "nmask_bd"], wr=[("Pt", 0)])
                        run("dve", lambda: V.tensor_tensor(out=Nt[par][:], in0=t1b[:], in1=nmoff[:], op=ALU.mult), rd=["t1b", "nmask_off"], wr=[("Nt", par)])
                        run("dve", lambda: V.tensor_tensor(out=fl(t1b), in0=bank[bQK][:], in1=fl(Db), op=ALU.mult), rd=[("bk", bQK), "Db"], wr=["t1b"])
                        run("pool", lambda: G.tensor_tensor(out=qk[:], in0=t1b[:], in1=mask_u[:], op=ALU.mult), rd=["t1b", "mask_u"], wr=[qkk])
                        run("pool", lambda: G.tensor_tensor(out=tt[:], in0=Pt[0][:], in1=id4[:], op=ALU.add), rd=[("Pt", 0), "ident4"], wr=[ttk])
                        i = 0

                        def f():
                            for h in range(4):
                                ins = PE.transpose(out=psTb[i][:, h, :], in_=Pt[0][:, h, :], identity=k.identb[:])
                            return ins
                        run("pe", f, rd=[("Pt", 0)], wr=[("pt", i)])
                        run("act", lambda: A_.copy(out=Pm[0][:], in_=psTb[i][:, 0:4, :]), rd=[("pt", i)], wr=[("Pm", 0)])
                    pieces.append(pA)

                    def mk_iter(n):
                        cur = (n - 1) % 2
                        nxt = n % 2

                        def pB():
                            bP = getbank("prep")

                            def f():
                                for h in range(4):
                                    ins = PE.matmul(bank[bP][:, h * 128:(h + 1) * 128], lhsT=Pt[cur][:, h, :], rhs=Pm[cur][:, h, :], start=True, stop=True)
                                return ins
                            run("pe", f, rd=[("Pt", cur), ("Pm", cur)], wr=[("bk", bP)])
                            if n < NIT:
                                bPt = getbank("prep")

                                def f():
                                    for h in range(4):
                                        ins = PE.matmul(bank[bPt][:, h * 128:(h + 1) * 128], lhsT=Pm[cur][:, h, :], rhs=Pt[cur][:, h, :], start=True, stop=True)
                                    return ins
                                run("pe", f, rd=[("Pt", cur), ("Pm", cur)], wr=[("bk", bPt)])
                            run("act", lambda: A_.copy(out=fl(Pm[nxt]), in_=bank[bP][:]), rd=[("bk", bP)], wr=[("Pm", nxt)])
                            if n < NIT:
                                if n % 2 == 0:
                                    run("act", lambda: A_.copy(out=fl(Pt[nxt]), in_=bank[bPt][:]), rd=[("bk", bPt)], wr=[("Pt", nxt)])
                                else:
                                    run("dve", lambda: V.tensor_copy(out=fl(Pt[nxt]), in_=bank[bPt][:]), rd=[("bk", bPt)], wr=[("Pt", nxt)])
                            bT = getbank("prep")

                            def f():
                                for h in range(4):
                                    ins = PE.matmul(bank[bT][:, h * 128:(h + 1) * 128], lhsT=Pm[nxt][:, h, :], rhs=tt[:, h, :], start=True, stop=True)
                                return ins
                            run("pe", f, rd=[("Pm", nxt), ttk], wr=[("bk", bT)])
                            run("dve", lambda: V.tensor_tensor(out=fl(tt), in0=bank[bT][:], in1=fl(tt), op=ALU.add), rd=[("bk", bT), ttk], wr=[ttk])
                        return pB
                    for n in range(1, NIT + 1):
                        pieces.append(mk_iter(n))
                    return pieces

                def scan_pieces(t, bb):
                    c = 4 * t + bb
                    par = c % 2
                    csl = slice(bb * 128, (bb + 1) * 128)
                    col = lambda h: slice(c * 4 + h, c * 4 + h + 1)
                    kd, qk, tt = kdecb[par], QKb[par], Tt[par]
                    kdk, qkk, ttk = ("kdecb", par), ("QKb", par), ("Tt", par)
                    st_ = {}

                    def H1():
                        bKS, bQS = getbank("scan"), 4
                        st_["bQS"] = bQS

                        def f():
                            for h in range(4):
                                hs = slice(h * 128, (h + 1) * 128)
                                PE.matmul(bank[bKS][:, hs], lhsT=qkvT[:, 4 + h, csl], rhs=Shi[:, h, :], start=True, stop=False)
                                PE.matmul(bank[bKS][:, hs], lhsT=qkvT[:, 4 + h, csl], rhs=Slo[:, h, :], start=False, stop=True)
                            for h in range(4):
                                hs = slice(h * 128, (h + 1) * 128)
                                PE.matmul(bank[bQS][:, hs], lhsT=qkvT[:, h, csl], rhs=Shi[:, h, :], start=True, stop=False)
                                ins = PE.matmul(bank[bQS][:, hs], lhsT=qkvT[:, h, csl], rhs=Slo[:, h, :], start=False, stop=True)
                            return ins
                        run("pe", f, rd=[("qkvT", c8) for c8 in range(8)] + ["Shi", "Slo"], wr=[("bk", bKS), ("bk", bQS)])

                        def f():
                            for h in range(4):
                                ins = V.scalar_tensor_tensor(out=r1f[:, h, :], in0=bank[bKS][:, h * 128:(h + 1) * 128], scalar=negc[:, col(h)],
                                                             in1=kvtok[:, bb, 4 + h, :], op0=ALU.mult, op1=ALU.add)
                            return ins
                        run("dve", f, rd=[("bk", bKS), "negc", ("kvtok", bb)], wr=["r1f"])
                        run("act", lambda: A_.copy(out=r1b[:], in_=r1f[:]), rd=["r1f"], wr=["r1b"])

                    def H2():
                        nt, ntk = Nt[par], ("Nt", par)
                        bX1 = getbank("scan")

                        def f():
                            for h in range(4):
                                ins = PE.matmul(bank[bX1][:, h * 128:(h + 1) * 128], lhsT=tt[:, h, :], rhs=r1b[:, h, :], start=True, stop=True)
                            return ins
                        run("pe", f, rd=[ttk, "r1b"], wr=[("bk", bX1)])
                        run("act", lambda: A_.copy(out=fl(x1b), in_=bank[bX1][:]), rd=[("bk", bX1)], wr=["x1b"])
                        bY = getbank("scan")

                        def f():
                            for h in range(4):
                                ins = PE.matmul(bank[bY][:, h * 128:(h + 1) * 128], lhsT=nt[:, h, :], rhs=x1b[:, h, :], start=True, stop=True)
                            return ins
                        run("pe", f, rd=[ntk, "x1b"], wr=[("bk", bY)])
                        run("dve", lambda: V.tensor_tensor(out=fl(r2b), in0=bank[bY][:], in1=fl(r1f), op=ALU.add), rd=[("bk", bY), "r1f"], wr=["r2b"])
                        bX = getbank("scan")

                        def f():
                            for h in range(4):
                                ins = PE.matmul(bank[bX][:, h * 128:(h + 1) * 128], lhsT=tt[:, h, :], rhs=r2b[:, h, :], start=True, stop=True)
                            return ins
                        run("pe", f, rd=[ttk, "r2b"], wr=[("bk", bX)])
                        run("act", lambda: A_.copy(out=fl(xhi), in_=bank[bX][:]), rd=[("bk", bX)], wr=["xhi"])

                    def H3():
                        bO2, bSU = 5, getbank("scan")
                        st_["bO2"] = bO2

                        def f():
                            for h in range(4):
                                hs = slice(h * 128, (h + 1) * 128)
                                PE.matmul(bank[bSU][:, hs], lhsT=kd[:, h, :], rhs=xhi[:, h, :], start=True, stop=True)
                            for h in range(4):
                                hs = slice(h * 128, (h + 1) * 128)
                                ins = PE.matmul(bank[bO2][:, hs], lhsT=qk[:, h, :], rhs=xhi[:, h, :], start=True, stop=True)
                            return ins
                        run("pe", f, rd=[kdk, qkk, "xhi"], wr=[("bk", bSU), ("bk", bO2)])

                        def f():
                            for h in range(4):
                                ins = V.scalar_tensor_tensor(out=S[:, h, :], in0=S[:, h, :], scalar=egl[:, col(h)],
                                                             in1=bank[bSU][:, h * 128:(h + 1) * 128], op0=ALU.mult, op1=ALU.add)
                            return ins
                        run("dve", f, rd=[("bk", bSU), "egl", "S"], wr=["S"])
                        run("act", lambda: A_.copy(out=Shi[:], in_=S[:]), rd=["S"], wr=["Shi"])
                        run("dve", lambda: V.tensor_tensor(out=Slo[:], in0=S[:], in1=Shi[:], op=ALU.subtract), rd=["S", "Shi"], wr=["Slo"])

                    def H4():
                        bO2, bQS = st_["bO2"], st_["bQS"]
                        run("act", lambda: A_.copy(out=fl(o2s), in_=bank[bO2][:]), rd=[("bk", bO2)], wr=["o2s"])

                        def f():
                            for h in range(4):
                                ins = V.scalar_tensor_tensor(out=ot[:, h, :], in0=bank[bQS][:, h * 128:(h + 1) * 128], scalar=egc[:, col(h)],
                                                             in1=o2s[:, h, :], op0=ALU.mult, op1=ALU.add)
                            return ins
                        run("dve", f, rd=[("bk", bQS), "egc", "o2s"], wr=["ot"])

                        def f():
                            for h in range(4):
                                ins = A_.activation(out=o2s[:, h, :], in_=ot[:, h, :], func=AF.Square, accum_out=ssq[:, h:h + 1])
                            return ins
                        run("act", f, rd=["ot"], wr=["ssq", "o2s"])
                        run("act", lambda: A_.activation(out=rr[:], in_=ssq[:], func=AF.Ln, bias=RMS_EPS, scale=1.0 / 128.0), rd=["ssq"], wr=["rr"])
                        run("act", lambda: A_.activation(out=rr[:], in_=rr[:], func=AF.Exp, scale=-0.5), rd=["rr"], wr=["rr"])

                        def f():
                            for h in range(4):
                                ins = V.scalar_tensor_tensor(out=og[:, h, :], in0=ot[:, h, :], scalar=rr[:, h:h + 1],
                                                             in1=gz[:, bb, h * 128:(h + 1) * 128], op0=ALU.mult, op1=ALU.mult)
                            return ins
                        run("dve", f, rd=["ot", "rr", ("gz", bb)], wr=["og"])
                        i = 1

                        def f():
                            for h in range(4):
                                ins = PE.transpose(out=psTb[i][:, h, :], in_=og[:, h, :], identity=k.identb[:])
                            return ins
                        run("pe", f, rd=["og"], wr=[("pt", i)])
                        run("act", lambda: A_.copy(out=o_gT[:, :, c * 128:(c + 1) * 128], in_=psTb[i][:, 0:4, :]), rd=[("pt", i)], wr=[("o_gT", c)])
                    return [H1, H2, H3, H4]

                def ab_part2():
                    bgc, bgm, bgl = getbank(), getbank(), getbank()

                    def f():
                        PE.matmul(bank[bgc][:, 0:64], lhsT=Um[:], rhs=g2, start=True, stop=True)
                        PE.matmul(bank[bgm][:, 0:64], lhsT=SLm[:], rhs=g2, start=True, stop=True)
                        return PE.matmul(bank[bgl][:, 0:64], lhsT=ones_f[:], rhs=g2, start=True, stop=True)
                    run("pe", f, rd=["gt", "Um", "SLm", "ones_f"], wr=[("bk", bgc), ("bk", bgm), ("bk", bgl)])

                    def f():
                        A_.activation(out=egc[:], in_=bank[bgc][:, 0:64], func=AF.Exp)
                        A_.activation(out=kds[:], in_=bank[bgm][:, 0:64], func=AF.Exp)
                        return A_.activation(out=egl[:], in_=bank[bgl][:, 0:64], func=AF.Exp)
                    run("act", f, rd=[("bk", bgc), ("bk", bgm), ("bk", bgl)], wr=["egc", "kds", "egl"])
                    run("dve", lambda: V.tensor_scalar(out=negc[:], in0=egc[:], scalar1=-1.0, scalar2=None, op0=ALU.mult), rd=["egc"], wr=["negc"])
                    run("dve", lambda: V.tensor_tensor(out=kds[:], in0=kds[:], in1=beta[:].rearrange("p b h -> p (b h)"), op=ALU.mult),
                        rd=["kds", "beta"], wr=["kds"])

                def bulk(t, mid=None):
                    proj_front(t, 0)
                    for ch in range(12):
                        if ch + 1 < 12:
                            proj_front(t, ch + 1)
                        proj_back(t, ch)
                        if ch % 3 == 2:
                            z_block(t, ch // 3)
                    if mid is not None:
                        mid()
                    l2norm_front(t, 0)
                    for ch in range(8):
                        if ch + 1 < 8:
                            l2norm_front(t, ch + 1)
                        l2norm_back(t, ch)
                    for bb in range(4):
                        kv_transposes(t, bb)
                chunks = [(t, bb) for t in range(4) for bb in range(4)]
                bulk(0, mid=ab_part2)
                for p_ in prep_pieces(0, 0):
                    p_()
                for ci, (t, bb) in enumerate(chunks):
                    sc = scan_pieces(t, bb)
                    nxt = chunks[ci + 1] if ci + 1 < len(chunks) else None
                    if nxt is not None and nxt[0] == t:
                        pp = prep_pieces(*nxt)
                        k.merge_emit([k.record(sc), k.record(pp)])
                    else:
                        for s_ in sc:
                            s_()
                        if nxt is not None:
                            bulk(nxt[0])
                            for p_ in prep_pieces(*nxt):
                                p_()
                lastpe = ("pe", sy.cnt["pe"])
                for nm in ("mix_q", "mix_k", "mix_v", "mix_z"):
                    ring.release(nm, lastpe)
                b1_end = [(en, sy.cnt[en]) for en in ("pe", "act", "dve", "pool") if sy.cnt[en] > 0]

            with contextlib.ExitStack() as es2:
                e2 = es2.enter_context

                def sb2(name, shape, dtp=F32):
                    return e2(nc.sbuf_tensor("b2_" + name, shape, dtp))
                NBK[0] = 6
                k.gam = sb2("gam", [128, D]); k.bet = sb2("bet", [128, D])
                R2 = sb2("R2", [128, 4, D])
                pb = [sb2("pb%d" % i, [128, 528]) for i in range(2)]
                sA = sb2("sA", [128, 528]); sB = sb2("sB", [128, 528])
                phist = sb2("phist", [128, 4, 16]); tmpc = sb2("tmpc", [128, 16])
                pooledT2 = [sb2("pooledT%d" % i, [128, 4, 512], BF16) for i in range(2)]
                pooled2T2 = [sb2("pooled2T%d" % i, [128, 4, 512], BF16) for i in range(2)]
                mergedT = sb2("mergedT", [128, 8, 512], BF16)
                s1 = [sb2("s1_%d" % i, [128, 512]) for i in range(2)]
                s2 = [sb2("s2_%d" % i, [128, 512]) for i in range(2)]
                m1 = [sb2("m1_%d" % i, [128, 512]) for i in range(2)]
                for en in ("pe", "act", "dve", "pool", "sp"):
                    sy.wait(en, b1_end)
                k.wtok = {kk: v for kk, v in k.wtok.items() if isinstance(kk, tuple) and kk[0] in ("hT", "o_gT") or kk in ("w_pw", "pscale", "invc")}
                k.rtoks = {}
                run("dve", lambda: V.memset(phist[:], 0.0), wr=["phist"])
                tgb = k.load_ln_params("norm_mix")
                scv = k.scr.rearrange("(b p) d -> p b d", p=128)
                ogt = lambda t: [("o_gT", c) for c in range(4 * t, 4 * t + 4)]
                Wp = getw("mix_p", 8)
                Wdn = getw("dn_proj", 4); Wpp = getw("pool_proj", 4)
                Wgd = [getw("gdn0", 8), getw("gdn1", 8)]
                Wgp = [getw("gpool0", 8), getw("gpool1", 8)]
                Wo = [getw("wout0", 4), getw("wout1", 4)]
                def pool_piece(t, g):
                    tsl = slice(t * 512, (t + 1) * 512)
                    pooledT, pooled2T = pooledT2[t % 2], pooled2T2[t % 2]
                    pkey, p2key = ("pooledT", t % 2, g), ("pooled2T", t % 2, g)
                    w = 2 ** (g + 1)
                    bi = getbank()
                    pbb, pbk = pb[g % 2], ("pb", g % 2)

                    def f():
                        for kc in range(8):
                            ins = PE.matmul(bank[bi][:], lhsT=Wp[:, kc, g * 128:(g + 1) * 128], rhs=k.hT[:, kc, tsl], start=(kc == 0), stop=(kc == 7))
                        return ins
                    run("pe", f, rd=hTt(t) + ["W:mix_p"], wr=[("bk", bi)])
                    run("act", lambda: A_.copy(out=pbb[:, 16:528], in_=bank[bi][:]), rd=[("bk", bi)], wr=[pbk])
                    run("dve", lambda: V.tensor_copy(out=pbb[:, 0:16], in_=phist[:, g, :]), rd=["phist"], wr=[pbk])
                    run("dve", lambda: V.tensor_copy(out=phist[:, g, 1:16], in_=pbb[:, 513:528]), rd=[pbk], wr=["phist"])
                    src_, dst_, sk, dk = pbb, sA, pbk, "sA"
                    sh = 1
                    while sh < w:
                        en = "dve"
                        E_ = ENG[en]
                        run(en, lambda E_=E_, src_=src_, dst_=dst_, sh=sh: E_.tensor_tensor(out=dst_[:, 2 * sh:528], in0=src_[:, 2 * sh:528],
                                                                                          in1=src_[:, sh:528 - sh], op=ALU.add),
                            rd=[sk], wr=[dk])
                        src_, sk = dst_, dk
                        dst_, dk = (sB, "sB") if dst_ is sA else (sA, "sA")
                        sh *= 2
                    run("dve", lambda src_=src_: V.scalar_tensor_tensor(out=pooledT[:, g, :], in0=src_[:, 16:528], scalar=1.0 / w, in1=pbb[:, 16:528],
                                                                       op0=ALU.mult, op1=ALU.subtract), rd=[sk, pbk], wr=[pkey])
                    if t == 0:
                        run("dve", lambda src_=src_: V.tensor_tensor(out=tmpc[:, 0:w - 1], in0=src_[:, 16:16 + w - 1], in1=invc[:, 0:w - 1], op=ALU.mult),
                            rd=[sk, "invc"], wr=["tmpc"])
                        run("dve", lambda: V.tensor_tensor(out=pooledT[:, g, 0:w - 1], in0=tmpc[:, 0:w - 1], in1=pbb[:, 16:16 + w - 1], op=ALU.subtract),
                            rd=["tmpc", pbk], wr=[pkey])
                    b2_ = getbank()
                    run("pe", lambda: PE.matmul(bank[b2_][:], lhsT=k.w_pw[:, g, :], rhs=pooledT[:, g, :], start=True, stop=True),
                        rd=["w_pw", pkey], wr=[("bk", b2_)])
                    run("act", lambda: A_.mul(out=pooled2T[:, g, :], in_=bank[b2_][:], mul=pscale[:, g:g + 1]), rd=[("bk", b2_), "pscale"],
                        wr=[p2key])

                pend_tr = []
                pend_ln = []
                lastln = [None]

                def do_ln_block(t, bb):
                    rk = ("R2", bb)
                    t6 = k.layernorm([R2[:, bb, :]], k.wtok[rk], tgb)
                    lastln[0] = t6
                    k.wtok[rk] = t6
                    k.rtoks[rk] = {}
                    sy.wait("sp", t6)
                    if mode == "out":
                        ov = k.out.rearrange("(b p) d -> p b d", p=128)
                        tst = sy.dma("sp", k.ds_s4[bb], ov[:, 4 * t + bb, :], R2[:, bb, :])
                        k.out_tok.append(tst)
                    else:
                        tst = sy.dma("sp", k.ds_s4[bb], scv[:, 4 * t + bb, :], R2[:, bb, :])
                        k.scr_tok.append(tst)
                        pend_tr.append((t, bb, t6))
                    k.rtoks[rk]["sp"] = tst

                def do_transposes(item):
                    t_, bb_, t6_ = item
                    k.emit_transposes(R2[:, bb_, :], 4 * t_ + bb_, t6_, psTb)
                    k.rtoks[("R2", bb_)]["act"] = ("act", sy.cnt["act"])

                for g in range(4):
                    pool_piece(0, g)
                for t in range(4):
                    tsl = slice(t * 512, (t + 1) * 512)
                    pooledT, pooled2T = pooledT2[t % 2], pooled2T2[t % 2]
                    for dc in range(8):
                        wg_, wp_ = Wgd[dc // 4], Wgp[dc // 4]
                        dl = slice((dc % 4) * 128, (dc % 4 + 1) * 128)
                        dsl = slice(dc * 128, (dc + 1) * 128)
                        b1_, b2_, b3_, b4_ = getbank(), getbank(), getbank(), getbank()
                        pi = dc % 2

                        def f():
                            for kc in range(8):
                                ins = PE.matmul(bank[b2_][:], lhsT=wg_[:, kc, dl], rhs=k.hT[:, kc, tsl], start=(kc == 0), stop=(kc == 7))
                            return ins
                        run("pe", f, rd=hTt(t) + ["W:gdn%d" % (dc // 4)], wr=[("bk", b2_)])

                        def f():
                            for kc in range(4):
                                ins = PE.matmul(bank[b1_][:], lhsT=Wdn[:, kc, dsl], rhs=o_gT[:, kc, tsl], start=(kc == 0), stop=(kc == 3))
                            return ins
                        run("pe", f, rd=ogt(t) + ["W:dn_proj"], wr=[("bk", b1_)])

                        def f():
                            for kc in range(8):
                                ins = PE.matmul(bank[b4_][:], lhsT=wp_[:, kc, dl], rhs=k.hT[:, kc, tsl], start=(kc == 0), stop=(kc == 7))
                            return ins
                        run("pe", f, rd=hTt(t) + ["W:gpool%d" % (dc // 4)], wr=[("bk", b4_)])

                        def f():
                            for kc in range(4):
                                ins = PE.matmul(bank[b3_][:], lhsT=Wpp[:, kc, dsl], rhs=pooled2T[:, kc, :], start=(kc == 0), stop=(kc == 3))
                            return ins
                        run("pe", f, rd=[("pooled2T", t % 2, g) for g in range(4)] + ["W:pool_proj"], wr=[("bk", b3_)])
                        run("act", lambda: A_.activation(out=s1[pi][:], in_=bank[b2_][:], func=AF.Sigmoid), rd=[("bk", b2_)], wr=[("s1", pi)])
                        run("act", lambda: A_.activation(out=s2[pi][:], in_=bank[b4_][:], func=AF.Sigmoid), rd=[("bk", b4_)], wr=[("s2", pi)])
                        run("dve", lambda: V.tensor_tensor(out=m1[pi][:], in0=bank[b1_][:], in1=s1[pi][:], op=ALU.mult), rd=[("bk", b1_), ("s1", pi)],
                            wr=[("m1", pi)])
                        run("dve", lambda: V.tensor_tensor(out=s2[pi][:], in0=bank[b3_][:], in1=s2[pi][:], op=ALU.mult), rd=[("bk", b3_), ("s2", pi)],
                            wr=[("s2", pi)])
                        run("pool", lambda: G.tensor_tensor(out=mergedT[:, dc, :], in0=m1[pi][:], in1=s2[pi][:], op=ALU.add), rd=[("m1", pi), ("s2", pi)],
                            wr=[("mergedT", dc)])
                        if dc % 2 == 1 and t + 1 < 4:
                            pool_piece(t + 1, dc // 2)
                        if dc % 2 == 1:
                            if pend_tr:
                                do_transposes(pend_tr.pop(0))
                            if pend_ln:
                                do_ln_block(*pend_ln.pop(0))
                    k.hT_rd[t] = ("pe", sy.cnt["pe"])
                    while pend_ln:
                        do_ln_block(*pend_ln.pop(0))
                    while pend_tr:
                        do_transposes(pend_tr.pop(0))
                    for bb in range(4):
                        rk = ("R2", bb)
                        sy.wait("sp", k.wtok.get(rk), list(k.rtoks.get(rk, {}).values()))
                        k.wtok[rk] = sy.dma("sp", k.ds_r2[bb], R2[:, bb, :], scv[:, 4 * t + bb, :])
                        k.rtoks[rk] = {}
                    if t == 3:
                        lp = ("pe", sy.cnt["pe"])
                        for nm in ("mix_p", "dn_proj", "pool_proj", "gdn0", "gdn1", "gpool0", "gpool1"):
                            ring.release(nm, lp)
                    for bb in range(4):
                        rk = ("R2", bb)
                        for half in range(2):
                            bi = getbank()

                            def f():
                                for kc in range(8):
                                    ins = PE.matmul(bank[bi][:], lhsT=mergedT[:, kc, bb * 128:(bb + 1) * 128],
                                                    rhs=Wo[kc // 4][:, kc % 4, half * 512:(half + 1) * 512], start=(kc == 0), stop=(kc == 7))
                                return ins
                            run("pe", f, rd=[("mergedT", dc) for dc in range(8)] + ["W:wout0", "W:wout1"], wr=[("bk", bi)])
                            Rv = R2[:, bb, half * 512:(half + 1) * 512]
                            run("dve", lambda: V.scalar_tensor_tensor(out=Rv, in0=bank[bi][:], scalar=1.0 / ALPHA, in1=Rv, op0=ALU.mult, op1=ALU.add),
                                rd=[("bk", bi), rk], wr=[rk])
                        if t == 3 and bb == 3:
                            lp = ("pe", sy.cnt["pe"])
                            ring.release("wout0", lp)
                            ring.release("wout1", lp)
                    for bb in range(4):
                        pend_ln.append((t, bb))
                while pend_ln:
                    do_ln_block(*pend_ln.pop(0))
                while pend_tr:
                    do_transposes(pend_tr.pop(0))
                k.free["gam"] = lastln[0]


_CACHE = {}


def _get_nc(debug_stage=None):
    if debug_stage not in _CACHE:
        kk = Kern(debug_stage)
        _CACHE[debug_stage] = kk.build()
    return _CACHE[debug_stage]


def _consts():
    i = np.arange(128)
    c = {}
    c["ident"] = np.eye(128, dtype=np.float32)
    c["Umat"] = (i[:, None] <= i[None, :]).astype(np.float32)
    c["SLmat"] = (i[:, None] > i[None, :]).astype(np.float32)
    mu = (i[None, :] >= i[:, None]).astype(np.float32)
    su = (i[None, :] > i[:, None]).astype(np.float32)
    c["mask_u"] = np.ascontiguousarray(np.repeat(mu[:, None, :], 4, axis=1))
    blk = (i[:, None] // 64 == i[None, :] // 64).astype(np.float32)
    c["nmask_bd"] = np.ascontiguousarray(np.repeat((-su * blk)[:, None, :], 4, axis=1))
    c["nmask_off"] = np.ascontiguousarray(np.repeat((-su * (1.0 - blk))[:, None, :], 4, axis=1))
    c["ident4"] = np.ascontiguousarray(np.repeat(np.eye(128, dtype=np.float32)[:, None, :], 4, axis=1))
    c["invc"] = np.ascontiguousarray(np.repeat((1.0 / (np.arange(16) + 1.0))[None, :], 128, axis=0).astype(np.float32))
    return c


def kernel(debug_stage=None, trace=False, **inputs):
    nc = _get_nc(debug_stage)
    x = np.ascontiguousarray(inputs["x"], dtype=np.float32)
    shared = _consts()
    for nm in ["ffn_pre_w_gate", "ffn_pre_w_up", "ffn_pre_w_down", "ffn_post_w_gate", "ffn_post_w_up", "ffn_post_w_down",
               "mix_w_in", "dn_w_proj", "pool_w_proj", "mix_w_out", "pool_w"]:
        shared[nm] = np.ascontiguousarray(inputs[nm][0], dtype=np.float32)
    for nm in ["norm_pre_g", "norm_pre_b", "norm_mix_g", "norm_mix_b", "norm_post_g", "norm_post_b"]:
        shared[nm] = np.ascontiguousarray(inputs[nm], dtype=np.float32).reshape(1, D)
    cwv = np.asarray(inputs["mix_conv_w"], dtype=np.float32)[0]
    shared["cw"] = np.ascontiguousarray(cwv.reshape(4, 12, 128).transpose(2, 1, 0))
    shared["pscale"] = np.ascontiguousarray(np.asarray(inputs["pool_scale"], dtype=np.float32)[0].T)
    shared["dn_norm_g"] = np.ascontiguousarray(inputs["dn_norm_g"], dtype=np.float32).reshape(1, 128)
    shared["dn_a_log"] = np.ascontiguousarray(inputs["dn_a_log"], dtype=np.float32).reshape(1, 4)
    shared["dn_dt_bias"] = np.ascontiguousarray(inputs["dn_dt_bias"], dtype=np.float32).reshape(1, 4)
    in_maps = []
    for c in range(8):
        m = dict(shared)
        m["x"] = x[c]
        in_maps.append(m)
    res = run_bass_kernel_spmd(nc, in_maps, core_ids=list(range(8)), **({"trace": True} if trace else {}))
    out = np.stack([r["out"] for r in res.results], axis=0).astype(np.float32)
    if trace:
        return out, res
    return out
```

```python
import contextlib
import numpy as np
import ml_dtypes
import concourse.bass as bass
import concourse.mybir as mybir
from concourse.bass_utils import run_bass_kernel_spmd

F32, BF16 = mybir.dt.float32, mybir.dt.bfloat16
AF = mybir.ActivationFunctionType
ALU = mybir.AluOpType

T = 2048
NB = 16
D = 1024
FF = 2816
NFC = 22
ALPHA = 2.0 ** 0.25
LN_EPS = 1e-5 / (ALPHA * ALPHA)
RMS_EPS = 1e-6
FFN_GROUPS = [list(range(0, 4)), list(range(4, 8)), list(range(8, 12)), list(range(12, 16)), list(range(16, 19)), list(range(19, 22))]
NSLOT = 9
MERGE_SEQ = False
SLOT_ELEMS = 4096


class Sy:
    def __init__(s, nc, es):
        s.nc = nc
        s.es = es
        s.eng = dict(pe=nc.tensor, act=nc.scalar, dve=nc.vector, pool=nc.gpsimd, sp=nc.sync)
        s.sem = {k: es.enter_context(nc.semaphore("sem_" + k)) for k in ("pe", "act", "dve", "pool")}
        s.cnt = {k: 0 for k in s.sem}
        s.waited = {}
        s.ndma = 0

    def sig(s, e, ins):
        s.cnt[e] += 1
        ins.then_inc(s.sem[e], 1)
        return (e, s.cnt[e])

    def new_dma_sem(s, name):
        sem = s.es.enter_context(s.nc.semaphore(name))
        return [sem, 0, name, 0]

    def dma(s, e, dsem, out, in_):
        if dsem[3] > 0:
            s.wait(e, ("dma", dsem, dsem[3]))
        ins = s.eng[e].dma_start(out=out, in_=in_)
        dsem[1] += 16
        ins.then_inc(dsem[0], 16)
        return ("dma", dsem, dsem[1])

    def wait(s, e, *tokens):
        flat = []

        def fl(ts):
            for t in ts:
                if t is None:
                    continue
                if isinstance(t, list):
                    fl(t)
                else:
                    flat.append(t)
        fl(tokens)
        best = {}
        for t in flat:
            key = ("dma:" + t[1][2]) if t[0] == "dma" else t[0]
            v = t[2] if t[0] == "dma" else t[1]
            if key not in best or best[key][0] < v:
                best[key] = (v, t)
        for t in [bt for _, bt in best.values()]:
            if t[0] == "dma":
                _, dsem, v = t
                key = (e, "dma:" + dsem[2])
                dsem[3] = max(dsem[3], v)
                if s.waited.get(key, 0) >= v:
                    continue
                s.eng[e].wait_ge(dsem[0], v)
                s.waited[key] = v
            else:
                src, v = t
                key = (e, src)
                if s.waited.get(key, 0) >= v:
                    continue
                s.eng[e].wait_ge(s.sem[src], v)
                s.waited[key] = v


class Ring:
    def __init__(s, sy, es, nc):
        s.sy = sy
        s.slots = [es.enter_context(nc.sbuf_tensor("wslot%d" % i, [128, SLOT_ELEMS], BF16)) for i in range(NSLOT)]
        s.dsem = [sy.new_dma_sem("wsem%d" % i) for i in range(NSLOT)]
        s.plan = []
        s.issued = {}
        s.next = 0
        s.free_tok = [None] * NSLOT
        s.slot_busy = [False] * NSLOT

    def add(s, name, parts):
        s.plan.append((name, parts))

    def _issue(s, slot):
        if s.next >= len(s.plan):
            return
        name, parts = s.plan[s.next]
        s.next += 1
        s.sy.wait("pool", s.free_tok[slot])
        tok = None
        for viewfn, src in parts:
            tok = s.sy.dma("pool", s.dsem[slot], viewfn(s.slots[slot]), src)
        s.issued[name] = (slot, tok)
        s.slot_busy[slot] = True

    def start(s):
        for i in range(NSLOT):
            s._issue(i)

    def get(s, name):
        slot, tok = s.issued[name]
        return s.slots[slot], tok

    def release(s, name, *tokens):
        slot, _ = s.issued.pop(name)
        s.free_tok[slot] = list(tokens)
        s.slot_busy[slot] = False
        s._issue(slot)


class Kern:
    def __init__(k, debug_stage=None):
        k.debug_stage = debug_stage
        k.nc = bass.Bass("TRN2", target_bir_lowering=False)

    def build(k):
        nc = k.nc
        dt = nc.dram_tensor
        k.x = dt("x", [T, D], F32, kind="ExternalInput").ap()
        k.w = {}
        for nm, shp in [("ffn_pre_w_gate", [D, FF]), ("ffn_pre_w_up", [D, FF]), ("ffn_pre_w_down", [FF, D]),
                        ("ffn_post_w_gate", [D, FF]), ("ffn_post_w_up", [D, FF]), ("ffn_post_w_down", [FF, D]),
                        ("norm_pre_g", [1, D]), ("norm_pre_b", [1, D]), ("norm_mix_g", [1, D]), ("norm_mix_b", [1, D]),
                        ("norm_post_g", [1, D]), ("norm_post_b", [1, D]),
                        ("mix_w_in", [D, 4616]), ("dn_w_proj", [512, D]), ("pool_w_proj", [512, D]), ("mix_w_out", [D, D]),
                        ("pool_w", [4, 128, 128]), ("cw", [128, 12, 4]), ("pscale", [128, 4]), ("dn_norm_g", [1, 128]),
                        ("dn_a_log", [1, 4]), ("dn_dt_bias", [1, 4]),
                        ("ident", [128, 128]), ("Umat", [128, 128]), ("SLmat", [128, 128]), ("invc", [128, 16])]:
            k.w[nm] = dt(nm, shp, F32, kind="ExternalInput").ap()
        for nm in ("mask_u", "nmask_bd", "nmask_off", "ident4"):
            k.w[nm] = dt(nm, [128, 4, 128], BF16, kind="ExternalInput").ap()
        k.out = dt("out", [T, D], F32, kind="ExternalOutput").ap()
        k.scr = dt("scr", [T, D], F32, kind="Internal").ap()
        with contextlib.ExitStack() as es:
            k.es = es
            e = es.enter_context
            k.sy = sy = Sy(nc, es)
            k.hT = e(nc.sbuf_tensor("hT", [128, 8, T], BF16))
            k.ring = Ring(sy, es, nc)
            k.identf = e(nc.sbuf_tensor("identf", [128, 128], F32))
            k.identb = e(nc.sbuf_tensor("identb", [128, 128], BF16))
            k.xb = [e(nc.sbuf_tensor("xb%d" % i, [128, D], BF16)) for i in range(2)]
            k.st = e(nc.sbuf_tensor("st", [128, 4, 2, 6], F32))
            k.mv = e(nc.sbuf_tensor("mv", [128, 4, 2], F32))
            k.sd = e(nc.sbuf_tensor("sd", [128, 4], F32))
            k.rstd = e(nc.sbuf_tensor("rstd", [128, 4], F32))
            k.w_ab = e(nc.sbuf_tensor("w_ab", [128, 8, 8], BF16))
            k.w_pw = e(nc.sbuf_tensor("w_pw", [128, 4, 128], BF16))
            k.ds_x = [sy.new_dma_sem("ds_x%d" % i) for i in range(4)]
            k.ds_id = sy.new_dma_sem("ds_id")
            k.ds_p = sy.new_dma_sem("ds_p")
            k.ds_o = sy.new_dma_sem("ds_o")
            k.ds_c = sy.new_dma_sem("ds_c")
            k.ds_s = sy.new_dma_sem("ds_s")
            k.ds_r2 = [sy.new_dma_sem("ds_r2_%d" % i) for i in range(4)]
            k.ds_s4 = [sy.new_dma_sem("ds_s4_%d" % i) for i in range(4)]
            k.ds_sm = sy.new_dma_sem("ds_sm")
            k.free = {}
            k.ln_tok_last = None
            k.out_tok = []
            k.scr_tok = []
            k.trc = 0
            k.psT_free = [None, None]
            k.xb_free = [None, None]
            k.hT_tok = [None] * NB
            k.hT_rd = [None] * 4
            Win = k.w["mix_w_in"].rearrange("(kc p) n -> p kc n", p=128)
            sy.dma("pool", k.ds_sm, k.w_ab[:], Win[:, :, 2048:2056])
            k.sm_tok = sy.dma("pool", k.ds_sm, k.w_pw[:], k.w["pool_w"].rearrange("g c d -> c g d"))
            k.plan_weights()
            k.ring.start()
            t_id = sy.dma("sp", k.ds_id, k.identf[:], k.w["ident"])
            sy.wait("dve", t_id)
            k.id_tok = sy.sig("dve", nc.vector.tensor_copy(out=k.identb[:], in_=k.identf[:]))
            stage = k.debug_stage
            k.ffn_phase("ffn_pre", "norm_pre", k.x, first=True, mode=("out" if stage == "A" else "spill"))
            if stage != "A":
                k.barrier()
                k.mixer_phase(mode=("out" if stage == "B" else "spill"))
                if stage != "B":
                    k.barrier()
                    k.ffn_phase("ffn_post", "norm_post", k.scr, first=False, mode="out")
            sy.wait("sp", k.out_tok)
        return nc

    def barrier(k):
        sy = k.sy
        toks = [(e, sy.cnt[e]) for e in ("pe", "act", "dve", "pool") if sy.cnt[e] > 0]
        for e in ("pe", "act", "dve", "pool", "sp"):
            sy.wait(e, toks, k.scr_tok, k.out_tok)

    def plan_weights(k):
        ring = k.ring

        def v8(cols):
            return lambda s, h: s[:, 0:8 * cols].rearrange("p (kc c) -> p kc c", kc=8)[:, 4 * h:4 * h + 4, :]

        def ffn(prefix):
            Wg = k.w[prefix + "_w_gate"].rearrange("(kc p) n -> p kc n", p=128)
            Wu = k.w[prefix + "_w_up"].rearrange("(kc p) n -> p kc n", p=128)
            Wd = k.w[prefix + "_w_down"].rearrange("(fc p) n -> p fc n", p=128)
            for gi, grp in enumerate(FFN_GROUPS):
                n = len(grp)
                c0, c1 = grp[0] * 128, (grp[-1] + 1) * 128
                for nm, W in (("g", Wg), ("u", Wu)):
                    parts = []
                    for h in range(2):
                        parts.append((lambda s, f=v8(n * 128), h=h: f(s, h), W[:, 4 * h:4 * h + 4, c0:c1]))
                    ring.add("%s_%s%d" % (prefix, nm, gi), parts)
                parts = []
                for h in range(2):
                    a = 0 if h == 0 else n // 2
                    b = n // 2 if h == 0 else n
                    parts.append((lambda s, n=n, a=a, b=b: s[:, 0:n * 1024].rearrange("p (fc c) -> p fc c", fc=n)[:, a:b, :],
                                  Wd[:, grp[0] + a:grp[0] + b, :]))
                ring.add("%s_d%d" % (prefix, gi), parts)

        def win(name, c0):
            Win = k.w["mix_w_in"].rearrange("(kc p) n -> p kc n", p=128)
            parts = [(lambda s, f=v8(512), h=h: f(s, h), Win[:, 4 * h:4 * h + 4, c0:c0 + 512]) for h in range(2)]
            ring.add(name, parts)

        def k4(name, W):
            Wv = W.rearrange("(kc p) n -> p kc n", p=128)
            parts = [(lambda s, h=h: s[:, :].rearrange("p (kc c) -> p kc c", kc=4)[:, 2 * h:2 * h + 2, :], Wv[:, 2 * h:2 * h + 2, :])
                     for h in range(2)]
            ring.add(name, parts)

        ffn("ffn_pre")
        if k.debug_stage != "A":
            win("mix_q", 0)
            win("mix_k", 512)
            win("mix_v", 1024)
            win("mix_z", 1536)
            win("mix_p", 2056)
            k4("dn_proj", k.w["dn_w_proj"])
            win("gdn0", 2568)
            win("gdn1", 3080)
            k4("pool_proj", k.w["pool_w_proj"])
            win("gpool0", 3592)
            win("gpool1", 4104)
            k4("wout0", k.w["mix_w_out"][0:512, :])
            k4("wout1", k.w["mix_w_out"][512:1024, :])
            if k.debug_stage != "B":
                ffn("ffn_post")

    def emit_transposes(k, src_ap, b, src_tok, psT):
        nc, sy = k.nc, k.sy
        i = k.trc % 2
        k.trc += 1
        sy.wait("act", src_tok, k.xb_free[i])
        tB = sy.sig("act", nc.scalar.copy(out=k.xb[i][:], in_=src_ap))
        sy.wait("pe", tB, k.psT_free[i], k.id_tok)
        for kc in range(8):
            ins = nc.tensor.transpose(out=psT[i][:, kc, :], in_=k.xb[i][:, kc * 128:(kc + 1) * 128], identity=k.identb[:])
        tT = sy.sig("pe", ins)
        k.xb_free[i] = tT
        sy.wait("act", tT, k.hT_rd[b // 4])
        tH = sy.sig("act", nc.scalar.copy(out=k.hT[:, :, b * 128:(b + 1) * 128], in_=psT[i][:]))
        k.psT_free[i] = tH
        k.hT_tok[b] = tH

    def load_ln_params(k, norm):
        sy = k.sy
        sy.wait("sp", k.free.get("gam"))
        sy.dma("sp", k.ds_p, k.gam[:], k.w[norm + "_g"].to_broadcast([128, D]))
        return sy.dma("sp", k.ds_p, k.bet[:], k.w[norm + "_b"].to_broadcast([128, D]))

    def layernorm(k, blocks, tok_in, tgb):
        nc, sy = k.nc, k.sy
        nb = len(blocks)
        sy.wait("dve", tok_in, tgb, k.ln_tok_last)
        for bb, Rb in enumerate(blocks):
            nc.vector.bn_stats(out=k.st[:, bb, 0, :], in_=Rb[:, 0:512])
            ins = nc.vector.bn_stats(out=k.st[:, bb, 1, :], in_=Rb[:, 512:1024])
        t1 = sy.sig("dve", ins)
        sy.wait("dve", t1)
        for bb in range(nb):
            ins = nc.vector.bn_aggr(out=k.mv[:, bb, :], in_=k.st[:, bb, :, :])
        t2 = sy.sig("dve", ins)
        sy.wait("act", t2)
        tA = sy.sig("act", nc.scalar.activation(out=k.sd[:, 0:nb], in_=k.mv[:, 0:nb, 1], func=AF.Sqrt, bias=LN_EPS, scale=1.0))
        sy.wait("dve", tA)
        t3 = sy.sig("dve", nc.vector.reciprocal(out=k.rstd[:, 0:nb], in_=k.sd[:, 0:nb]))
        sy.wait("dve", t3)
        for bb, Rb in enumerate(blocks):
            ins = nc.vector.scalar_tensor_tensor(out=Rb, in0=Rb, scalar=k.mv[:, bb, 0:1], in1=k.gam[:], op0=ALU.subtract, op1=ALU.mult)
        t4 = sy.sig("dve", ins)
        sy.wait("dve", t4)
        for bb, Rb in enumerate(blocks):
            ins = nc.vector.scalar_tensor_tensor(out=Rb, in0=Rb, scalar=k.rstd[:, bb:bb + 1], in1=k.bet[:], op0=ALU.mult, op1=ALU.add)
        t6 = sy.sig("dve", ins)
        k.ln_tok_last = t6
        return t6

    def ffn_phase(k, prefix, norm, src, first, mode):
        nc, sy, ring = k.nc, k.sy, k.ring
        c = 0.5 / ALPHA
        NG = len(FFN_GROUPS)
        with contextlib.ExitStack() as es:
            e = es.enter_context
            R = e(nc.sbuf_tensor("R_" + prefix, [128, NB, D], F32))
            k.gam = e(nc.sbuf_tensor("gam_" + prefix, [128, D], F32))
            k.bet = e(nc.sbuf_tensor("bet_" + prefix, [128, D], F32))
            hid = [e(nc.sbuf_tensor("hid%d_%s" % (i, prefix), [128, 4, 512], BF16)) for i in range(2)]
            sg = [e(nc.sbuf_tensor("sg%d_%s" % (i, prefix), [128, 512], BF16)) for i in range(2)]
            psG = [e(nc.psum_tensor("psG%d_%s" % (i, prefix), [128, 512], F32)) for i in range(2)]
            psU = [e(nc.psum_tensor("psU%d_%s" % (i, prefix), [128, 512], F32)) for i in range(2)]
            psO = [e(nc.psum_tensor("psO%d_%s" % (i, prefix), [128, 512], F32)) for i in range(2)]
            psT = [e(nc.psum_tensor("psT%d_%s" % (i, prefix), [128, 8, 128], BF16)) for i in range(2)]
            k.psT_free = [None, None]
            fr = {}
            tgb = k.load_ln_params(norm)
            sv = src.rearrange("(b p) d -> p b d", p=128)
            R_tok = []
            for q in range(4):
                tq = sy.dma("sp", k.ds_x[q], R[:, 4 * q:4 * q + 4, :], sv[:, 4 * q:4 * q + 4, :])
                R_tok += [tq] * 4
            if first:
                for b in range(NB):
                    k.emit_transposes(R[:, b, :], b, R_tok[b], psT)
            gu_ctr = [0]
            o_ctr = [0]

            def GU(gi, t, pp):
                grp = FFN_GROUPS[gi]
                n = len(grp)
                Wg_s, tWg = ring.get("%s_g%d" % (prefix, gi))
                Wu_s, tWu = ring.get("%s_u%d" % (prefix, gi))
                Wg_v = Wg_s[:, 0:8 * n * 128].rearrange("p (kc c) -> p kc c", kc=8)
                Wu_v = Wu_s[:, 0:8 * n * 128].rearrange("p (kc c) -> p kc c", kc=8)
                toks = []
                lastmm = None
                for i in range(n):
                    j = gu_ctr[0] % 2
                    gu_ctr[0] += 1
                    sy.wait("pe", tWg, tWu, [k.hT_tok[b] for b in range(4 * t, 4 * t + 4)], fr.get("psG%d" % j), fr.get("psU%d" % j))
                    for kc in range(8):
                        ins = nc.tensor.matmul(psG[j][:], lhsT=Wg_v[:, kc, i * 128:(i + 1) * 128], rhs=k.hT[:, kc, t * 512:(t + 1) * 512],
                                               start=(kc == 0), stop=(kc == 7))
                    tG = sy.sig("pe", ins)
                    for kc in range(8):
                        ins = nc.tensor.matmul(psU[j][:], lhsT=Wu_v[:, kc, i * 128:(i + 1) * 128], rhs=k.hT[:, kc, t * 512:(t + 1) * 512],
                                               start=(kc == 0), stop=(kc == 7))
                    tU = sy.sig("pe", ins)
                    lastmm = tU
                    sy.wait("act", tG, fr.get("sg%d" % j))
                    tS = sy.sig("act", nc.scalar.activation(out=sg[j][:], in_=psG[j][:], func=AF.Silu))
                    fr["psG%d" % j] = tS
                    sy.wait("dve", tU, tS, fr.get("hid%d" % pp))
                    tH = sy.sig("dve", nc.vector.tensor_tensor(out=hid[pp][:, i, :], in0=sg[j][:], in1=psU[j][:], op=ALU.mult))
                    fr["psU%d" % j] = tH
                    fr["sg%d" % j] = tH
                    toks.append(tH)
                k.hT_rd[t] = lastmm
                if t == 3:
                    ring.release("%s_g%d" % (prefix, gi), lastmm)
                    ring.release("%s_u%d" % (prefix, gi), lastmm)
                return toks

            def DOWN(gi, t, pp, toks):
                grp = FFN_GROUPS[gi]
                n = len(grp)
                Wd_s, tWd = ring.get("%s_d%d" % (prefix, gi))
                Wd_v = Wd_s[:, 0:n * 1024].rearrange("p (fc c) -> p fc c", fc=n)
                lastmm = None
                rtok = None
                for bb in range(4):
                    b = 4 * t + bb
                    for half in range(2):
                        j = o_ctr[0] % 2
                        o_ctr[0] += 1
                        sy.wait("pe", tWd, toks, fr.get("psO%d" % j))
                        for i in range(n):
                            ins = nc.tensor.matmul(psO[j][:], lhsT=hid[pp][:, i, bb * 128:(bb + 1) * 128],
                                                   rhs=Wd_v[:, i, half * 512:(half + 1) * 512], start=(i == 0), stop=(i == n - 1))
                        tO = sy.sig("pe", ins)
                        lastmm = tO
                        sy.wait("dve", tO, R_tok[b])
                        Rv = R[:, b, half * 512:(half + 1) * 512]
                        rtok = sy.sig("dve", nc.vector.scalar_tensor_tensor(out=Rv, in0=psO[j][:], scalar=c, in1=Rv,
                                                                            op0=ALU.mult, op1=ALU.add))
                        fr["psO%d" % j] = rtok
                fr["hid%d" % pp] = lastmm
                if t == 3:
                    ring.release("%s_d%d" % (prefix, gi), lastmm)
                if gi == NG - 1:
                    t6 = k.layernorm([R[:, 4 * t + bb, :] for bb in range(4)], rtok, tgb)
                    if t == 3:
                        k.free["gam"] = t6
                    if mode == "out":
                        sy.wait("sp", t6)
                        ov = k.out.rearrange("(b p) d -> p b d", p=128)
                        k.out_tok.append(sy.dma("sp", k.ds_o, ov[:, 4 * t:4 * t + 4, :], R[:, 4 * t:4 * t + 4, :]))
                    else:
                        sy.wait("sp", t6)
                        ov = k.scr.rearrange("(b p) d -> p b d", p=128)
                        k.scr_tok.append(sy.dma("sp", k.ds_s, ov[:, 4 * t:4 * t + 4, :], R[:, 4 * t:4 * t + 4, :]))
                        pend.append((t, t6))

            pend = []

            def flush():
                while pend:
                    t_, t6_ = pend.pop(0)
                    for bb in range(4):
                        k.emit_transposes(R[:, 4 * t_ + bb, :], 4 * t_ + bb, t6_, psT)

            seq = [(gi, t) for gi in range(NG) for t in range(4)]
            prev = None
            for idx, (gi, t) in enumerate(seq):
                pp = idx % 2
                toks = GU(gi, t, pp)
                flush()
                if prev is not None:
                    DOWN(*prev)
                prev = (gi, t, pp, toks)
            DOWN(*prev)
            flush()
    def run(k, eng, fn, rd=(), wr=(), extra=(), cost=None):
        if getattr(k, "rec", None) is not None:
            k.rec.append((eng, fn, tuple(rd), tuple(wr), tuple(extra), cost))
            return None
        sy = k.sy
        toks = list(extra)
        for b in rd:
            toks.append(k.wtok.get(b))
        for b in wr:
            toks.append(k.wtok.get(b))
            toks += list(k.rtoks.get(b, {}).values())
        sy.wait(eng, toks)
        t = sy.sig(eng, fn())
        for b in rd:
            k.rtoks.setdefault(b, {})[eng] = t
        for b in wr:
            k.wtok[b] = t
            k.rtoks[b] = {}
        return t

    def record(k, fns):
        k.rec = []
        for f in fns:
            f()
        ops, k.rec = k.rec, None
        return ops

    def merge_emit(k, seqs):
        eng_free = {}
        avail = {}
        heads = [0] * len(seqs)
        DEF = {"pe": 0.5, "act": 0.65, "dve": 0.7, "pool": 2.5}
        while True:
            best = None
            for si, seq in enumerate(seqs):
                if heads[si] >= len(seq):
                    continue
                eng, fn, rd, wr, extra, cost = seq[heads[si]]
                st = eng_free.get(eng, 0.0)
                for b in rd + wr:
                    st = max(st, avail.get(b, 0.0))
                if MERGE_SEQ:
                    if best is None:
                        best = (st, si)
                elif best is None or st < best[0] - 1e-9:
                    best = (st, si)
            if best is None:
                break
            st, si = best
            eng, fn, rd, wr, extra, cost = seqs[si][heads[si]]
            heads[si] += 1
            c = cost if cost is not None else DEF[eng]
            fin = st + c
            eng_free[eng] = fin
            for b in wr:
                avail[b] = fin + 0.3
            for b in rd:
                avail[b] = max(avail.get(b, 0.0), st + 0.05)
            k.run(eng, fn, rd=rd, wr=wr, extra=extra)

    def mixer_phase(k, mode):
        nc, sy, ring = k.nc, k.sy, k.ring
        V, A_, PE, G = nc.vector, nc.scalar, nc.tensor, nc.gpsimd
        ENG = {"dve": V, "act": A_, "pool": G}
        k.wtok, k.rtoks = {}, {}
        run = k.run
        with contextlib.ExitStack() as es:
            e = es.enter_context

            def sb(name, shape, dtp=F32):
                return e(nc.sbuf_tensor("mx_" + name, shape, dtp))

            NBK = [4]
            bank = [e(nc.psum_tensor("mxb%d" % i, [128, 512], F32)) for i in range(6)]
            psTb = [e(nc.psum_tensor("mxT%d" % i, [128, 8, 128], BF16)) for i in range(2)]
            k.psT_free = [None, None]
            bctr = [0]
            tctr = [0]

            def getbank(pool=None):
                if pool == "prep":
                    i = pctr[0] % 3
                    pctr[0] += 1
                    return i
                if pool == "scan":
                    return 3
                i = bctr[0] % NBK[0]
                bctr[0] += 1
                return i
            pctr = [0]

            def gett():
                i = tctr[0] % 2
                tctr[0] += 1
                return i

            o_gT = sb("o_gT", [128, 4, T], BF16)
            Um = sb("Um", [128, 128]); SLm = sb("SLm", [128, 128]); ones_f = sb("ones_f", [128, 128])
            ones_b = sb("ones_b", [128, 128], BF16)
            cst = sb("cst", [128, 4, 128])
            mask_u = sb("mask_u", [128, 4, 128], BF16); nmask = sb("nmask", [128, 4, 128], BF16); nmoff = sb("nmoff", [128, 4, 128], BF16)
            id4 = sb("id4", [128, 4, 128], BF16)
            gnorm = sb("gnorm", [128, 4, 128], BF16)
            cw = sb("cw", [128, 12, 4]); pscale = sb("pscale", [128, 4])
            alog = sb("alog", [128, 4]); dtb = sb("dtb", [128, 4]); invc = sb("invc", [128, 16]); ea = sb("ea", [128, 4])
            for dst, nm in ((mask_u, "mask_u"), (nmask, "nmask_bd"), (nmoff, "nmask_off"), (id4, "ident4")):
                sy.dma("sp", k.ds_c, dst[:], k.w[nm])
            for h in range(4):
                tq = sy.dma("sp", k.ds_c, cst[:, h, :], k.w["dn_norm_g"].to_broadcast([128, 128]))
            k.wtok["cst"] = tq
            run("dve", lambda: V.tensor_copy(out=gnorm[:], in_=cst[:]), rd=["cst"], wr=["gnorm"])
            for dst, nm in ((Um, "Umat"), (SLm, "SLmat"), (cw, "cw"), (pscale, "pscale"), (invc, "invc")):
                tq = sy.dma("sp", k.ds_c, dst[:], k.w[nm])
            sy.dma("sp", k.ds_c, alog[:], k.w["dn_a_log"].to_broadcast([128, 4]))
            ctok = sy.dma("sp", k.ds_c, dtb[:], k.w["dn_dt_bias"].to_broadcast([128, 4]))
            for nm in ("Um", "SLm", "cw", "pscale", "invc", "alog", "dtb", "mask_u", "nmask_bd", "nmask_off", "ident4"):
                k.wtok[nm] = ctok
            run("dve", lambda: V.memset(ones_f[:], 1.0), wr=["ones_f"])
            run("dve", lambda: V.memset(ones_b[:], 1.0), wr=["ones_b"])
            run("act", lambda: A_.activation(out=ea[:], in_=alog[:], func=AF.Exp), rd=["alog"], wr=["ea"])
            for b in range(NB):
                k.wtok[("hT", b)] = k.hT_tok[b]
            hTt = lambda t: [("hT", b) for b in range(4 * t, 4 * t + 4)]
            k.wtok["w_ab"] = k.sm_tok
            k.wtok["w_pw"] = k.sm_tok

            def getw(name, kc):
                s_, tk = ring.get(name)
                k.wtok["W:" + name] = tk
                return s_[:, :].rearrange("p (kc c) -> p kc c", kc=kc)

            with contextlib.ExitStack() as es1:
                e1 = es1.enter_context

                def sb1(name, shape, dtp=F32):
                    return e1(nc.sbuf_tensor("b1_" + name, shape, dtp))
                ab = sb1("ab", [128, 16, 8]); beta = sb1("beta", [128, 16, 4]); lnb = sb1("lnb", [128, 16, 4])
                gt = sb1("gt", [128, 16, 4]); xs = sb1("xs", [128, 16, 4])
                egc = sb1("egc", [128, 64]); negc = sb1("negc", [128, 64]); kds = sb1("kds", [128, 64]); egl = sb1("egl", [128, 64])
                g2 = gt[:].rearrange("p b h -> p (b h)")
                lnb2 = lnb[:].rearrange("p b h -> p (b h)")
                bi = getbank()

                def f():
                    for b in range(NB):
                        for kc in range(8):
                            ins = PE.matmul(bank[bi][:, b * 8:(b + 1) * 8], lhsT=k.hT[:, kc, b * 128:(b + 1) * 128], rhs=k.w_ab[:, kc, :],
                                            start=(kc == 0), stop=(kc == 7))
                    return ins
                run("pe", f, rd=[("hT", b) for b in range(NB)] + ["w_ab"], wr=[("bk", bi)])
                run("act", lambda: A_.copy(out=ab[:], in_=bank[bi][:, 0:128].rearrange("p (b c) -> p b c", b=16)), rd=[("bk", bi)], wr=["ab"])
                run("act", lambda: A_.activation(out=beta[:], in_=ab[:, :, 0:4], func=AF.Sigmoid), rd=["ab"], wr=["beta"])
                run("act", lambda: A_.activation(out=lnb[:], in_=beta[:], func=AF.Ln), rd=["beta"], wr=["lnb"])
                run("dve", lambda: V.tensor_tensor(out=xs[:], in0=ab[:, :, 4:8], in1=dtb[:][:, None, :].broadcast_to([128, 16, 4]), op=ALU.add),
                    rd=["ab", "dtb"], wr=["xs"])
                run("act", lambda: A_.activation(out=xs[:], in_=xs[:], func=AF.Exp), rd=["xs"], wr=["xs"])
                run("act", lambda: A_.activation(out=xs[:], in_=xs[:], func=AF.Ln, bias=1.0, scale=1.0), rd=["xs"], wr=["xs"])
                run("dve", lambda: V.scalar_tensor_tensor(out=gt[:], in0=xs[:], scalar=-1.0, in1=ea[:][:, None, :].broadcast_to([128, 16, 4]),
                                                         op0=ALU.mult, op1=ALU.mult), rd=["xs", "ea"], wr=["gt"])
                pc = [sb1("pc%d" % i, [128, 515]) for i in range(2)]
                acc = [sb1("acc%d" % i, [128, 512]) for i in range(2)]
                sq = [sb1("sq%d" % i, [128, 512], BF16) for i in range(2)]
                hist = sb1("hist", [128, 12, 3])
                qkvT = sb1("qkvT", [128, 12, 512], BF16)
                kvtok = sb1("kvtok", [128, 4, 8, 128], BF16)
                gz = sb1("gz", [128, 4, 512], BF16)
                gSL = sb1("gSL", [128, 4, 128]); Db = sb1("Db", [128, 4, 128]); t1b = sb1("t1b", [128, 4, 128])
                Pm = [sb1("Pm%d" % i, [128, 4, 128], BF16) for i in range(2)]
                Pt = [sb1("Pt%d" % i, [128, 4, 128], BF16) for i in range(2)]
                Nt = [sb1("Nt%d" % i, [128, 4, 128], BF16) for i in range(2)]
                kdecb = [sb1("kdecb%d" % i, [128, 4, 128], BF16) for i in range(2)]
                QKb = [sb1("QKb%d" % i, [128, 4, 128], BF16) for i in range(2)]
                Tt = [sb1("Tt%d" % i, [128, 4, 128], BF16) for i in range(2)]
                S = sb1("S", [128, 4, 128]); Shi = sb1("Shi", [128, 4, 128], BF16); Slo = sb1("Slo", [128, 4, 128], BF16)
                r1f = sb1("r1f", [128, 4, 128]); r1b = sb1("r1b", [128, 4, 128], BF16); x1b = sb1("x1b", [128, 4, 128], BF16)
                r2b = sb1("r2b", [128, 4, 128], BF16)
                xhi = sb1("xhi", [128, 4, 128], BF16)
                o2s = sb1("o2s", [128, 4, 128]); ot = sb1("ot", [128, 4, 128])
                ssq = sb1("ssq", [128, 4]); rr = sb1("rr", [128, 4]); og = sb1("og", [128, 4, 128], BF16)
                fl = lambda a: a[:].rearrange("p h d -> p (h d)")
                run("dve", lambda: V.memset(hist[:], 0.0), wr=[("hist", ch) for ch in range(12)])
                run("dve", lambda: V.memset(S[:], 0.0), wr=["S"])
                run("dve", lambda: V.memset(Shi[:], 0.0), wr=["Shi"])
                run("dve", lambda: V.memset(Slo[:], 0.0), wr=["Slo"])
                Wq = getw("mix_q", 8); Wk = getw("mix_k", 8); Wv = getw("mix_v", 8); Wz = getw("mix_z", 8)

                def proj_front(t, ch):
                    tsl = slice(t * 512, (t + 1) * 512)
                    Wx, wn = (Wq, "W:mix_q") if ch < 4 else ((Wk, "W:mix_k") if ch < 8 else (Wv, "W:mix_v"))
                    c0 = (ch % 4) * 128
                    bi = getbank()
                    pi = ch % 2
                    pcb, accb = pc[pi], acc[pi]
                    pk, ak = ("pc", pi), ("acc", pi)

                    def f():
                        for kc in range(8):
                            ins = PE.matmul(bank[bi][:], lhsT=Wx[:, kc, c0:c0 + 128], rhs=k.hT[:, kc, tsl], start=(kc == 0), stop=(kc == 7))
                        return ins
                    run("pe", f, rd=hTt(t) + [wn], wr=[("bk", bi)])
                    run("act", lambda: A_.copy(out=pcb[:, 3:515], in_=bank[bi][:]), rd=[("bk", bi)], wr=[pk])
                    run("act", lambda: A_.mul(out=accb[:], in_=bank[bi][:], mul=cw[:, ch, 3:4]), rd=[("bk", bi), "cw"], wr=[ak])
                    run("dve", lambda: V.tensor_copy(out=pcb[:, 0:3], in_=hist[:, ch, :]), rd=[("hist", ch)], wr=[pk])
                    run("dve", lambda: V.tensor_copy(out=hist[:, ch, :], in_=pcb[:, 512:515]), rd=[pk], wr=[("hist", ch)])

                def proj_back(t, ch):
                    pi = ch % 2
                    pcb, accb = pc[pi], acc[pi]
                    pk, ak = ("pc", pi), ("acc", pi)
                    for j in (0, 1, 2):
                        run("dve", lambda j=j: V.scalar_tensor_tensor(out=accb[:], in0=pcb[:, j:j + 512], scalar=cw[:, ch, j:j + 1], in1=accb[:],
                                                                     op0=ALU.mult, op1=ALU.add), rd=[pk, ak], wr=[ak])
                    run("act", lambda: A_.activation(out=qkvT[:, ch, :], in_=accb[:], func=AF.Silu), rd=[ak], wr=[("qkvT", ch)])

                l2b = {}

                def l2norm_front(t, ch):
                    pi = ch % 2
                    sqb, sk_ = sq[pi], ("sq", pi)
                    run("act", lambda: A_.activation(out=sqb[:], in_=qkvT[:, ch, :], func=AF.Square), rd=[("qkvT", ch)], wr=[sk_])
                    b2 = getbank()
                    l2b[ch] = b2
                    run("pe", lambda: PE.matmul(bank[b2][:], lhsT=ones_b[:], rhs=sqb[:], start=True, stop=True), rd=[sk_, "ones_b"], wr=[("bk", b2)])

                def l2norm_back(t, ch):
                    pi = ch % 2
                    rinv, rk_ = acc[pi], ("acc", pi)
                    b2 = l2b[ch]
                    sc = 128.0 if ch < 4 else 1.0
                    run("act", lambda: A_.activation(out=rinv[:], in_=bank[b2][:], func=AF.Ln, bias=RMS_EPS * sc, scale=sc),
                        rd=[("bk", b2)], wr=[rk_])
                    run("act", lambda: A_.activation(out=rinv[:], in_=rinv[:], func=AF.Exp, scale=-0.5), rd=[rk_], wr=[rk_])
                    run("dve", lambda: V.tensor_tensor(out=qkvT[:, ch, :], in0=qkvT[:, ch, :], in1=rinv[:], op=ALU.mult),
                        rd=[("qkvT", ch), rk_], wr=[("qkvT", ch)])

                def kv_transposes(t, bb):
                    i = gett()

                    def f():
                        for c8 in range(8):
                            ins = PE.transpose(out=psTb[i][:, c8, :], in_=qkvT[:, 4 + c8, bb * 128:(bb + 1) * 128], identity=k.identb[:])
                        return ins
                    run("pe", f, rd=[("qkvT", 4 + c8) for c8 in range(8)], wr=[("pt", i)], extra=[k.id_tok])
                    run("act", lambda: A_.copy(out=kvtok[:, bb, :, :], in_=psTb[i][:]), rd=[("pt", i)], wr=[("kvtok", bb)])

                def z_block(t, bb):
                    b = 4 * t + bb
                    bi = getbank()

                    def f():
                        for kc in range(8):
                            ins = PE.matmul(bank[bi][:], lhsT=k.hT[:, kc, b * 128:(b + 1) * 128], rhs=Wz[:, kc, :], start=(kc == 0), stop=(kc == 7))
                        return ins
                    run("pe", f, rd=[("hT", b), "W:mix_z"], wr=[("bk", bi)])
                    run("act", lambda: A_.activation(out=gz[:, bb, :], in_=bank[bi][:], func=AF.Silu), rd=[("bk", bi)], wr=[("gz", bb)])
                    run("dve", lambda: V.tensor_tensor(out=gz[:, bb, :], in0=gz[:, bb, :], in1=fl(gnorm), op=ALU.mult),
                        rd=[("gz", bb), "gnorm"], wr=[("gz", bb)])

                NIT = 5

                def prep_pieces(t, bb):
                    c = 4 * t + bb
                    par = c % 2
                    csl = slice(bb * 128, (bb + 1) * 128)
                    col = lambda h: slice(c * 4 + h, c * 4 + h + 1)
                    kd, qk, tt = kdecb[par], QKb[par], Tt[par]
                    kdk, qkk, ttk = ("kdecb", par), ("QKb", par), ("Tt", par)
                    pieces = []

                    def pA():
                        def f():
                            for h in range(4):
                                ins = V.tensor_scalar(out=kd[:, h, :], in0=kvtok[:, bb, h, :], scalar1=kds[:, col(h)], scalar2=None, op0=ALU.mult)
                            return ins
                        run("dve", f, rd=[("kvtok", bb), "kds"], wr=[kdk])

                        def f():
                            for h in range(4):
                                ins = V.tensor_scalar(out=gSL[:, h, :], in0=SLm[:], scalar1=g2[:, col(h)], scalar2=None, op0=ALU.mult)
                            return ins
                        run("dve", f, rd=["SLm", "gt"], wr=["gSL"])
                        bA, bQK, bLD = getbank("prep"), getbank("prep"), getbank("prep")

                        def f():
                            for h in range(4):
                                hs = slice(h * 128, (h + 1) * 128)
                                PE.matmul(bank[bA][:, hs], lhsT=qkvT[:, 4 + h, csl], rhs=qkvT[:, 4 + h, csl], start=True, stop=True)
                                ins = PE.matmul(bank[bQK][:, hs], lhsT=qkvT[:, 4 + h, csl], rhs=qkvT[:, h, csl], start=True, stop=True)
                            return ins
                        run("pe", f, rd=[("qkvT", c8) for c8 in range(8)], wr=[("bk", bA), ("bk", bQK)])

                        def f():
                            for h in range(4):
                                ins = PE.matmul(bank[bLD][:, h * 128:(h + 1) * 128], lhsT=gSL[:, h, :], rhs=Um[:], start=True, stop=True)
                            return ins
                        run("pe", f, rd=["gSL", "Um"], wr=[("bk", bLD)])

                        def f():
                            for h in range(4):
                                ins = A_.activation(out=Db[:, h, :], in_=bank[bLD][:, h * 128:(h + 1) * 128], func=AF.Exp,
                                                    bias=lnb2[:, col(h)], scale=1.0)
                            return ins
                        run("act", f, rd=[("bk", bLD), "lnb"], wr=["Db"])
                        run("dve", lambda: V.tensor_tensor(out=fl(t1b), in0=bank[bA][:], in1=fl(Db), op=ALU.mult), rd=[("bk", bA), "Db"], wr=["t1b"])
                        run("dve", lambda: V.tensor_tensor(out=Pt[0][:], in0=t1b[:], in1=nmask[:], op=ALU.mult), rd=["t1b", "nmask_bd"], wr=[("Pt", 0)])
                        run("dve", lambda: V.tensor_tensor(out=Nt[par][:], in0=t1b[:], in1=nmoff[:], op=ALU.mult), rd=["t1b", "nmask_off"], wr=[("Nt", par)])
                        run("dve", lambda: V.tensor_tensor(out=fl(t1b), in0=bank[bQK][:], in1=fl(Db), op=ALU.mult), rd=[("bk", bQK), "Db"], wr=["t1b"])
                        run("pool", lambda: G.tensor_tensor(out=qk[:], in0=t1b[:], in1=mask_u[:], op=ALU.mult), rd=["t1b", "mask_u"], wr=[qkk])
                        run("pool", lambda: G.tensor_tensor(out=tt[:], in0=Pt[0][:], in1=id4[:], op=ALU.add), rd=[("Pt", 0), "ident4"], wr=[ttk])
                        i = 0

                        def f():
                            for h in range(4):
                                ins = PE.transpose(out=psTb[i][:, h, :], in_=Pt[0][:, h, :], identity=k.identb[:])
                            return ins
                        run("pe", f, rd=[("Pt", 0)], wr=[("pt", i)])
                        run("act", lambda: A_.copy(out=Pm[0][:], in_=psTb[i][:, 0:4, :]), rd=[("pt", i)], wr=[("Pm", 0)])
                    pieces.append(pA)

                    def mk_iter(n):
                        cur = (n - 1) % 2
                        nxt = n % 2

                        def pB():
                            bP = getbank("prep")

                            def f():
                                for h in range(4):
                                    ins = PE.matmul(bank[bP][:, h * 128:(h + 1) * 128], lhsT=Pt[cur][:, h, :], rhs=Pm[cur][:, h, :], start=True, stop=True)
                                return ins
                            run("pe", f, rd=[("Pt", cur), ("Pm", cur)], wr=[("bk", bP)])
                            if n < NIT:
                                bPt = getbank("prep")

                                def f():
                                    for h in range(4):
                                        ins = PE.matmul(bank[bPt][:, h * 128:(h + 1) * 128], lhsT=Pm[cur][:, h, :], rhs=Pt[cur][:, h, :], start=True, stop=True)
                                    return ins
                                run("pe", f, rd=[("Pt", cur), ("Pm", cur)], wr=[("bk", bPt)])
                            run("act", lambda: A_.copy(out=fl(Pm[nxt]), in_=bank[bP][:]), rd=[("bk", bP)], wr=[("Pm", nxt)])
                            if n < NIT:
                                if n % 2 == 0:
                                    run("act", lambda: A_.copy(out=fl(Pt[nxt]), in_=bank[bPt][:]), rd=[("bk", bPt)], wr=[("Pt", nxt)])
                                else:
                                    run("dve", lambda: V.tensor_copy(out=fl(Pt[nxt]), in_=bank[bPt][:]), rd=[("bk", bPt)], wr=[("Pt", nxt)])
                            bT = getbank("prep")

                            def f():
                                for h in range(4):
                                    ins = PE.matmul(bank[bT][:, h * 128:(h + 1) * 128], lhsT=Pm[nxt][:, h, :], rhs=tt[:, h, :], start=True, stop=True)
                                return ins
                            run("pe", f, rd=[("Pm", nxt), ttk], wr=[("bk", bT)])
                            run("dve", lambda: V.tensor_tensor(out=fl(tt), in0=bank[bT][:], in1=fl(tt), op=ALU.add), rd=[("bk", bT), ttk], wr=[ttk])
                        return pB
                    for n in range(1, NIT + 1):
                        pieces.append(mk_iter(n))
                    return pieces

                def scan_pieces(t, bb):
                    c = 4 * t + bb
                    par = c % 2
                    csl = slice(bb * 128, (bb + 1) * 128)
                    col = lambda h: slice(c * 4 + h, c * 4 + h + 1)
                    kd, qk, tt = kdecb[par], QKb[par], Tt[par]
                    kdk, qkk, ttk = ("kdecb", par), ("QKb", par), ("Tt", par)
                    st_ = {}

                    def H1():
                        bKS, bQS = getbank("scan"), 4
                        st_["bQS"] = bQS

                        def f():
                            for h in range(4):
                                hs = slice(h * 128, (h + 1) * 128)
                                PE.matmul(bank[bKS][:, hs], lhsT=qkvT[:, 4 + h, csl], rhs=Shi[:, h, :], start=True, stop=True)
                            for h in range(4):
                                hs = slice(h * 128, (h + 1) * 128)
                                ins = PE.matmul(bank[bQS][:, hs], lhsT=qkvT[:, h, csl], rhs=Shi[:, h, :], start=True, stop=True)
                            return ins
                        run("pe", f, rd=[("qkvT", c8) for c8 in range(8)] + ["Shi"], wr=[("bk", bKS), ("bk", bQS)])

                        def f():
                            for h in range(4):
                                ins = V.scalar_tensor_tensor(out=r1f[:, h, :], in0=bank[bKS][:, h * 128:(h + 1) * 128], scalar=negc[:, col(h)],
                                                             in1=kvtok[:, bb, 4 + h, :], op0=ALU.mult, op1=ALU.add)
                            return ins
                        run("dve", f, rd=[("bk", bKS), "negc", ("kvtok", bb)], wr=["r1f"])
                        run("act", lambda: A_.copy(out=r1b[:], in_=r1f[:]), rd=["r1f"], wr=["r1b"])

                    def H2():
                        nt, ntk = Nt[par], ("Nt", par)
                        bX1 = getbank("scan")

                        def f():
                            for h in range(4):
                                ins = PE.matmul(bank[bX1][:, h * 128:(h + 1) * 128], lhsT=tt[:, h, :], rhs=r1b[:, h, :], start=True, stop=True)
                            return ins
                        run("pe", f, rd=[ttk, "r1b"], wr=[("bk", bX1)])
                        run("act", lambda: A_.copy(out=fl(x1b), in_=bank[bX1][:]), rd=[("bk", bX1)], wr=["x1b"])
                        bY = getbank("scan")

                        def f():
                            for h in range(4):
                                ins = PE.matmul(bank[bY][:, h * 128:(h + 1) * 128], lhsT=nt[:, h, :], rhs=x1b[:, h, :], start=True, stop=True)
                            return ins
                        run("pe", f, rd=[ntk, "x1b"], wr=[("bk", bY)])
                        run("dve", lambda: V.tensor_tensor(out=fl(r2b), in0=bank[bY][:], in1=fl(r1f), op=ALU.add), rd=[("bk", bY), "r1f"], wr=["r2b"])
                        bX = getbank("scan")

                        def f():
                            for h in range(4):
                                ins = PE.matmul(bank[bX][:, h * 128:(h + 1) * 128], lhsT=tt[:, h, :], rhs=r2b[:, h, :], start=True, stop=True)
                            return ins
                        run("pe", f, rd=[ttk, "r2b"], wr=[("bk", bX)])
                        run("act", lambda: A_.copy(out=fl(xhi), in_=bank[bX][:]), rd=[("bk", bX)], wr=["xhi"])

                    def H3():
                        bO2, bSU = 5, getbank("scan")
                        st_["bO2"] = bO2

                        def f():
                            for h in range(4):
                                hs = slice(h * 128, (h + 1) * 128)
                                PE.matmul(bank[bSU][:, hs], lhsT=kd[:, h, :], rhs=xhi[:, h, :], start=True, stop=True)
                            for h in range(4):
                                hs = slice(h * 128, (h + 1) * 128)
                                ins = PE.matmul(bank[bO2][:, hs], lhsT=qk[:, h, :], rhs=xhi[:, h, :], start=True, stop=True)
                            return ins
                        run("pe", f, rd=[kdk, qkk, "xhi"], wr=[("bk", bSU), ("bk", bO2)])

                        def f():
                            for h in range(4):
                                ins = V.scalar_tensor_tensor(out=S[:, h, :], in0=S[:, h, :], scalar=egl[:, col(h)],
                                                             in1=bank[bSU][:, h * 128:(h + 1) * 128], op0=ALU.mult, op1=ALU.add)
                            return ins
                        run("dve", f, rd=[("bk", bSU), "egl", "S"], wr=["S"])
                        run("act", lambda: A_.copy(out=Shi[:], in_=S[:]), rd=["S"], wr=["Shi"])

                    def H4():
                        bO2, bQS = st_["bO2"], st_["bQS"]
                        run("act", lambda: A_.copy(out=fl(o2s), in_=bank[bO2][:]), rd=[("bk", bO2)], wr=["o2s"])

                        def f():
                            for h in range(4):
                                ins = V.scalar_tensor_tensor(out=ot[:, h, :], in0=bank[bQS][:, h * 128:(h + 1) * 128], scalar=egc[:, col(h)],
                                                             in1=o2s[:, h, :], op0=ALU.mult, op1=ALU.add)
                            return ins
                        run("dve", f, rd=[("bk", bQS), "egc", "o2s"], wr=["ot"])

                        def f():
                            for h in range(4):
                                ins = A_.activation(out=o2s[:, h, :], in_=ot[:, h, :], func=AF.Square, accum_out=ssq[:, h:h + 1])
                            return ins
                        run("act", f, rd=["ot"], wr=["ssq", "o2s"])
                        run("act", lambda: A_.activation(out=rr[:], in_=ssq[:], func=AF.Ln, bias=RMS_EPS, scale=1.0 / 128.0), rd=["ssq"], wr=["rr"])
                        run("act", lambda: A_.activation(out=rr[:], in_=rr[:], func=AF.Exp, scale=-0.5), rd=["rr"], wr=["rr"])

                        def f():
                            for h in range(4):
                                ins = V.scalar_tensor_tensor(out=og[:, h, :], in0=ot[:, h, :], scalar=rr[:, h:h + 1],
                                                             in1=gz[:, bb, h * 128:(h + 1) * 128], op0=ALU.mult, op1=ALU.mult)
                            return ins
                        run("dve", f, rd=["ot", "rr", ("gz", bb)], wr=["og"])
                        i = 1

                        def f():
                            for h in range(4):
                                ins = PE.transpose(out=psTb[i][:, h, :], in_=og[:, h, :], identity=k.identb[:])
                            return ins
                        run("pe", f, rd=["og"], wr=[("pt", i)])
                        run("act", lambda: A_.copy(out=o_gT[:, :, c * 128:(c + 1) * 128], in_=psTb[i][:, 0:4, :]), rd=[("pt", i)], wr=[("o_gT", c)])
                    return [H1, H2, H3, H4]

                def ab_part2():
                    bgc, bgm, bgl = getbank(), getbank(), getbank()

                    def f():
                        PE.matmul(bank[bgc][:, 0:64], lhsT=Um[:], rhs=g2, start=True, stop=True)
                        PE.matmul(bank[bgm][:, 0:64], lhsT=SLm[:], rhs=g2, start=True, stop=True)
                        return PE.matmul(bank[bgl][:, 0:64], lhsT=ones_f[:], rhs=g2, start=True, stop=True)
                    run("pe", f, rd=["gt", "Um", "SLm", "ones_f"], wr=[("bk", bgc), ("bk", bgm), ("bk", bgl)])

                    def f():
                        A_.activation(out=egc[:], in_=bank[bgc][:, 0:64], func=AF.Exp)
                        A_.activation(out=kds[:], in_=bank[bgm][:, 0:64], func=AF.Exp)
                        return A_.activation(out=egl[:], in_=bank[bgl][:, 0:64], func=AF.Exp)
                    run("act", f, rd=[("bk", bgc), ("bk", bgm), ("bk", bgl)], wr=["egc", "kds", "egl"])
                    run("dve", lambda: V.tensor_scalar(out=negc[:], in0=egc[:], scalar1=-1.0, scalar2=None, op0=ALU.mult), rd=["egc"], wr=["negc"])
                    run("dve", lambda: V.tensor_tensor(out=kds[:], in0=kds[:], in1=beta[:].rearrange("p b h -> p (b h)"), op=ALU.mult),
                        rd=["kds", "beta"], wr=["kds"])

                def bulk(t, mid=None):
                    proj_front(t, 0)
                    for ch in range(12):
                        if ch + 1 < 12:
                            proj_front(t, ch + 1)
                        proj_back(t, ch)
                        if ch % 3 == 2:
                            z_block(t, ch // 3)
                    if mid is not None:
                        mid()
                    l2norm_front(t, 0)
                    for ch in range(8):
                        if ch + 1 < 8:
                            l2norm_front(t, ch + 1)
                        l2norm_back(t, ch)
                    for bb in range(4):
                        kv_transposes(t, bb)
                chunks = [(t, bb) for t in range(4) for bb in range(4)]
                bulk(0, mid=ab_part2)
                for p_ in prep_pieces(0, 0):
                    p_()
                for ci, (t, bb) in enumerate(chunks):
                    sc = scan_pieces(t, bb)
                    nxt = chunks[ci + 1] if ci + 1 < len(chunks) else None
                    if nxt is not None and nxt[0] == t:
                        pp = prep_pieces(*nxt)
                        k.merge_emit([k.record(sc), k.record(pp)])
                    else:
                        for s_ in sc:
                            s_()
                        if nxt is not None:
                            bulk(nxt[0])
                            for p_ in prep_pieces(*nxt):
                                p_()
                lastpe = ("pe", sy.cnt["pe"])
                for nm in ("mix_q", "mix_k", "mix_v", "mix_z"):
                    ring.release(nm, lastpe)
                b1_end = [(en, sy.cnt[en]) for en in ("pe", "act", "dve", "pool") if sy.cnt[en] > 0]

            with contextlib.ExitStack() as es2:
                e2 = es2.enter_context

                def sb2(name, shape, dtp=F32):
                    return e2(nc.sbuf_tensor("b2_" + name, shape, dtp))
                NBK[0] = 6
                k.gam = sb2("gam", [128, D]); k.bet = sb2("bet", [128, D])
                R2 = sb2("R2", [128, 4, D])
                pb = [sb2("pb%d" % i, [128, 528]) for i in range(2)]
                sA = sb2("sA", [128, 528]); sB = sb2("sB", [128, 528])
                phist = sb2("phist", [128, 4, 16]); tmpc = sb2("tmpc", [128, 16])
                pooledT2 = [sb2("pooledT%d" % i, [128, 4, 512], BF16) for i in range(2)]
                pooled2T2 = [sb2("pooled2T%d" % i, [128, 4, 512], BF16) for i in range(2)]
                mergedT = sb2("mergedT", [128, 8, 512], BF16)
                s1 = [sb2("s1_%d" % i, [128, 512]) for i in range(2)]
                s2 = [sb2("s2_%d" % i, [128, 512]) for i in range(2)]
                m1 = [sb2("m1_%d" % i, [128, 512]) for i in range(2)]
                for en in ("pe", "act", "dve", "pool", "sp"):
                    sy.wait(en, b1_end)
                k.wtok = {kk: v for kk, v in k.wtok.items() if isinstance(kk, tuple) and kk[0] in ("hT", "o_gT") or kk in ("w_pw", "pscale", "invc")}
                k.rtoks = {}
                run("dve", lambda: V.memset(phist[:], 0.0), wr=["phist"])
                tgb = k.load_ln_params("norm_mix")
                scv = k.scr.rearrange("(b p) d -> p b d", p=128)
                ogt = lambda t: [("o_gT", c) for c in range(4 * t, 4 * t + 4)]
                Wp = getw("mix_p", 8)
                Wdn = getw("dn_proj", 4); Wpp = getw("pool_proj", 4)
                Wgd = [getw("gdn0", 8), getw("gdn1", 8)]
                Wgp = [getw("gpool0", 8), getw("gpool1", 8)]
                Wo = [getw("wout0", 4), getw("wout1", 4)]
                def pool_piece(t, g):
                    tsl = slice(t * 512, (t + 1) * 512)
                    pooledT, pooled2T = pooledT2[t % 2], pooled2T2[t % 2]
                    pkey, p2key = ("pooledT", t % 2, g), ("pooled2T", t % 2, g)
                    w = 2 ** (g + 1)
                    bi = getbank()
                    pbb, pbk = pb[g % 2], ("pb", g % 2)

                    def f():
                        for kc in range(8):
                            ins = PE.matmul(bank[bi][:], lhsT=Wp[:, kc, g * 128:(g + 1) * 128], rhs=k.hT[:, kc, tsl], start=(kc == 0), stop=(kc == 7))
                        return ins
                    run("pe", f, rd=hTt(t) + ["W:mix_p"], wr=[("bk", bi)])
                    run("act", lambda: A_.copy(out=pbb[:, 16:528], in_=bank[bi][:]), rd=[("bk", bi)], wr=[pbk])
                    run("dve", lambda: V.tensor_copy(out=pbb[:, 0:16], in_=phist[:, g, :]), rd=["phist"], wr=[pbk])
                    run("dve", lambda: V.tensor_copy(out=phist[:, g, 1:16], in_=pbb[:, 513:528]), rd=[pbk], wr=["phist"])
                    src_, dst_, sk, dk = pbb, sA, pbk, "sA"
                    sh = 1
                    while sh < w:
                        en = "dve"
                        E_ = ENG[en]
                        run(en, lambda E_=E_, src_=src_, dst_=dst_, sh=sh: E_.tensor_tensor(out=dst_[:, 2 * sh:528], in0=src_[:, 2 * sh:528],
                                                                                          in1=src_[:, sh:528 - sh], op=ALU.add),
                            rd=[sk], wr=[dk])
                        src_, sk = dst_, dk
                        dst_, dk = (sB, "sB") if dst_ is sA else (sA, "sA")
                        sh *= 2
                    run("dve", lambda src_=src_: V.scalar_tensor_tensor(out=pooledT[:, g, :], in0=src_[:, 16:528], scalar=1.0 / w, in1=pbb[:, 16:528],
                                                                       op0=ALU.mult, op1=ALU.subtract), rd=[sk, pbk], wr=[pkey])
                    if t == 0:
                        run("dve", lambda src_=src_: V.tensor_tensor(out=tmpc[:, 0:w - 1], in0=src_[:, 16:16 + w - 1], in1=invc[:, 0:w - 1], op=ALU.mult),
                            rd=[sk, "invc"], wr=["tmpc"])
                        run("dve", lambda: V.tensor_tensor(out=pooledT[:, g, 0:w - 1], in0=tmpc[:, 0:w - 1], in1=pbb[:, 16:16 + w - 1], op=ALU.subtract),
                            rd=["tmpc", pbk], wr=[pkey])
                    b2_ = getbank()
                    run("pe", lambda: PE.matmul(bank[b2_][:], lhsT=k.w_pw[:, g, :], rhs=pooledT[:, g, :], start=True, stop=True),
                        rd=["w_pw", pkey], wr=[("bk", b2_)])
                    run("act", lambda: A_.mul(out=pooled2T[:, g, :], in_=bank[b2_][:], mul=pscale[:, g:g + 1]), rd=[("bk", b2_), "pscale"],
                        wr=[p2key])

                pend_tr = []
                pend_ln = []
                lastln = [None]

                def do_ln_block(t, bb):
                    rk = ("R2", bb)
                    t6 = k.layernorm([R2[:, bb, :]], k.wtok[rk], tgb)
                    lastln[0] = t6
                    k.wtok[rk] = t6
                    k.rtoks[rk] = {}
                    sy.wait("sp", t6)
                    if mode == "out":
                        ov = k.out.rearrange("(b p) d -> p b d", p=128)
                        tst = sy.dma("sp", k.ds_s4[bb], ov[:, 4 * t + bb, :], R2[:, bb, :])
                        k.out_tok.append(tst)
                    else:
                        tst = sy.dma("sp", k.ds_s4[bb], scv[:, 4 * t + bb, :], R2[:, bb, :])
                        k.scr_tok.append(tst)
                        pend_tr.append((t, bb, t6))
                    k.rtoks[rk]["sp"] = tst

                def do_transposes(item):
                    t_, bb_, t6_ = item
                    k.emit_transposes(R2[:, bb_, :], 4 * t_ + bb_, t6_, psTb)
                    k.rtoks[("R2", bb_)]["act"] = ("act", sy.cnt["act"])

                for g in range(4):
                    pool_piece(0, g)
                for t in range(4):
                    tsl = slice(t * 512, (t + 1) * 512)
                    pooledT, pooled2T = pooledT2[t % 2], pooled2T2[t % 2]
                    for dc in range(8):
                        wg_, wp_ = Wgd[dc // 4], Wgp[dc // 4]
                        dl = slice((dc % 4) * 128, (dc % 4 + 1) * 128)
                        dsl = slice(dc * 128, (dc + 1) * 128)
                        b1_, b2_, b3_, b4_ = getbank(), getbank(), getbank(), getbank()
                        pi = dc % 2

                        def f():
                            for kc in range(8):
                                ins = PE.matmul(bank[b2_][:], lhsT=wg_[:, kc, dl], rhs=k.hT[:, kc, tsl], start=(kc == 0), stop=(kc == 7))
                            return ins
                        run("pe", f, rd=hTt(t) + ["W:gdn%d" % (dc // 4)], wr=[("bk", b2_)])

                        def f():
                            for kc in range(4):
                                ins = PE.matmul(bank[b1_][:], lhsT=Wdn[:, kc, dsl], rhs=o_gT[:, kc, tsl], start=(kc == 0), stop=(kc == 3))
                            return ins
                        run("pe", f, rd=ogt(t) + ["W:dn_proj"], wr=[("bk", b1_)])

                        def f():
                            for kc in range(8):
                                ins = PE.matmul(bank[b4_][:], lhsT=wp_[:, kc, dl], rhs=k.hT[:, kc, tsl], start=(kc == 0), stop=(kc == 7))
                            return ins
                        run("pe", f, rd=hTt(t) + ["W:gpool%d" % (dc // 4)], wr=[("bk", b4_)])

                        def f():
                            for kc in range(4):
                                ins = PE.matmul(bank[b3_][:], lhsT=Wpp[:, kc, dsl], rhs=pooled2T[:, kc, :], start=(kc == 0), stop=(kc == 3))
                            return ins
                        run("pe", f, rd=[("pooled2T", t % 2, g) for g in range(4)] + ["W:pool_proj"], wr=[("bk", b3_)])
                        run("act", lambda: A_.activation(out=s1[pi][:], in_=bank[b2_][:], func=AF.Sigmoid), rd=[("bk", b2_)], wr=[("s1", pi)])
                        run("act", lambda: A_.activation(out=s2[pi][:], in_=bank[b4_][:], func=AF.Sigmoid), rd=[("bk", b4_)], wr=[("s2", pi)])
                        run("dve", lambda: V.tensor_tensor(out=m1[pi][:], in0=bank[b1_][:], in1=s1[pi][:], op=ALU.mult), rd=[("bk", b1_), ("s1", pi)],
                            wr=[("m1", pi)])
                        run("dve", lambda: V.tensor_tensor(out=s2[pi][:], in0=bank[b3_][:], in1=s2[pi][:], op=ALU.mult), rd=[("bk", b3_), ("s2", pi)],
                            wr=[("s2", pi)])
                        run("pool", lambda: G.tensor_tensor(out=mergedT[:, dc, :], in0=m1[pi][:], in1=s2[pi][:], op=ALU.add), rd=[("m1", pi), ("s2", pi)],
                            wr=[("mergedT", dc)])
                        if dc % 2 == 1 and t + 1 < 4:
                            pool_piece(t + 1, dc // 2)
                        if dc % 2 == 1:
                            if pend_tr:
                                do_transposes(pend_tr.pop(0))
                            if pend_ln:
                                do_ln_block(*pend_ln.pop(0))
                    k.hT_rd[t] = ("pe", sy.cnt["pe"])
                    while pend_ln:
                        do_ln_block(*pend_ln.pop(0))
                    while pend_tr:
                        do_transposes(pend_tr.pop(0))
                    for bb in range(4):
                        rk = ("R2", bb)
                        sy.wait("sp", k.wtok.get(rk), list(k.rtoks.get(rk, {}).values()))
                        k.wtok[rk] = sy.dma("sp", k.ds_r2[bb], R2[:, bb, :], scv[:, 4 * t + bb, :])
                        k.rtoks[rk] = {}
                    if t == 3:
                        lp = ("pe", sy.cnt["pe"])
                        for nm in ("mix_p", "dn_proj", "pool_proj", "gdn0", "gdn1", "gpool0", "gpool1"):
                            ring.release(nm, lp)
                    for bb in range(4):
                        rk = ("R2", bb)
                        for half in range(2):
                            bi = getbank()

                            def f():
                                for kc in range(8):
                                    ins = PE.matmul(bank[bi][:], lhsT=mergedT[:, kc, bb * 128:(bb + 1) * 128],
                                                    rhs=Wo[kc // 4][:, kc % 4, half * 512:(half + 1) * 512], start=(kc == 0), stop=(kc == 7))
                                return ins
                            run("pe", f, rd=[("mergedT", dc) for dc in range(8)] + ["W:wout0", "W:wout1"], wr=[("bk", bi)])
                            Rv = R2[:, bb, half * 512:(half + 1) * 512]
                            run("dve", lambda: V.scalar_tensor_tensor(out=Rv, in0=bank[bi][:], scalar=1.0 / ALPHA, in1=Rv, op0=ALU.mult, op1=ALU.add),
                                rd=[("bk", bi), rk], wr=[rk])
                        if t == 3 and bb == 3:
                            lp = ("pe", sy.cnt["pe"])
                            ring.release("wout0", lp)
                            ring.release("wout1", lp)
                    for bb in range(4):
                        pend_ln.append((t, bb))
                while pend_ln:
                    do_ln_block(*pend_ln.pop(0))
                while pend_tr:
                    do_transposes(pend_tr.pop(0))
                k.free["gam"] = lastln[0]


_CACHE = {}


def _get_nc(debug_stage=None):
    if debug_stage not in _CACHE:
        kk = Kern(debug_stage)
        _CACHE[debug_stage] = kk.build()
    return _CACHE[debug_stage]


def _consts():
    i = np.arange(128)
    c = {}
    c["ident"] = np.eye(128, dtype=np.float32)
    c["Umat"] = (i[:, None] <= i[None, :]).astype(np.float32)
    c["SLmat"] = (i[:, None] > i[None, :]).astype(np.float32)
    mu = (i[None, :] >= i[:, None]).astype(np.float32)
    su = (i[None, :] > i[:, None]).astype(np.float32)
    c["mask_u"] = np.ascontiguousarray(np.repeat(mu[:, None, :], 4, axis=1))
    blk = (i[:, None] // 64 == i[None, :] // 64).astype(np.float32)
    c["nmask_bd"] = np.ascontiguousarray(np.repeat((-su * blk)[:, None, :], 4, axis=1))
    c["nmask_off"] = np.ascontiguousarray(np.repeat((-su * (1.0 - blk))[:, None, :], 4, axis=1))
    c["ident4"] = np.ascontiguousarray(np.repeat(np.eye(128, dtype=np.float32)[:, None, :], 4, axis=1))
    c["invc"] = np.ascontiguousarray(np.repeat((1.0 / (np.arange(16) + 1.0))[None, :], 128, axis=0).astype(np.float32))
    for nm in ("mask_u", "nmask_bd", "nmask_off", "ident4"):
        c[nm] = c[nm].astype(ml_dtypes.bfloat16)
    return c


def kernel(debug_stage=None, trace=False, **inputs):
    nc = _get_nc(debug_stage)
    x = np.ascontiguousarray(inputs["x"], dtype=np.float32)
    shared = _consts()
    for nm in ["ffn_pre_w_gate", "ffn_pre_w_up", "ffn_pre_w_down", "ffn_post_w_gate", "ffn_post_w_up", "ffn_post_w_down",
               "mix_w_in", "dn_w_proj", "pool_w_proj", "mix_w_out", "pool_w"]:
        shared[nm] = np.ascontiguousarray(inputs[nm][0], dtype=np.float32)
    for nm in ["norm_pre_g", "norm_pre_b", "norm_mix_g", "norm_mix_b", "norm_post_g", "norm_post_b"]:
        shared[nm] = np.ascontiguousarray(inputs[nm], dtype=np.float32).reshape(1, D)
    cwv = np.asarray(inputs["mix_conv_w"], dtype=np.float32)[0]
    shared["cw"] = np.ascontiguousarray(cwv.reshape(4, 12, 128).transpose(2, 1, 0))
    shared["pscale"] = np.ascontiguousarray(np.asarray(inputs["pool_scale"], dtype=np.float32)[0].T)
    shared["dn_norm_g"] = np.ascontiguousarray(inputs["dn_norm_g"], dtype=np.float32).reshape(1, 128)
    shared["dn_a_log"] = np.ascontiguousarray(inputs["dn_a_log"], dtype=np.float32).reshape(1, 4)
    shared["dn_dt_bias"] = np.ascontiguousarray(inputs["dn_dt_bias"], dtype=np.float32).reshape(1, 4)
    in_maps = []
    for c in range(8):
        m = dict(shared)
        m["x"] = x[c]
        in_maps.append(m)
    res = run_bass_kernel_spmd(nc, in_maps, core_ids=list(range(8)), **({"trace": True} if trace else {}))
    out = np.stack([r["out"] for r in res.results], axis=0).astype(np.float32)
    if trace:
        return out, res
    return out
```

```python
import contextlib
import numpy as np
import ml_dtypes
import concourse.bass as bass
import concourse.mybir as mybir
from concourse.bass_utils import run_bass_kernel_spmd

F32, BF16 = mybir.dt.float32, mybir.dt.bfloat16
AF = mybir.ActivationFunctionType
ALU = mybir.AluOpType

T = 2048
NB = 16
D = 1024
FF = 2816
NFC = 22
ALPHA = 2.0 ** 0.25
LN_EPS = 1e-5 / (ALPHA * ALPHA)
RMS_EPS = 1e-6
FFN_GROUPS = [list(range(0, 4)), list(range(4, 8)), list(range(8, 12)), list(range(12, 16)), list(range(16, 19)), list(range(19, 22))]
NSLOT = 9
MERGE_SEQ = False
SLOT_ELEMS = 4096


class Sy:
    def __init__(s, nc, es):
        s.nc = nc
        s.es = es
        s.eng = dict(pe=nc.tensor, act=nc.scalar, dve=nc.vector, pool=nc.gpsimd, sp=nc.sync)
        s.sem = {k: es.enter_context(nc.semaphore("sem_" + k)) for k in ("pe", "act", "dve", "pool")}
        s.cnt = {k: 0 for k in s.sem}
        s.waited = {}
        s.ndma = 0

    def sig(s, e, ins):
        s.cnt[e] += 1
        ins.then_inc(s.sem[e], 1)
        return (e, s.cnt[e])

    def new_dma_sem(s, name):
        sem = s.es.enter_context(s.nc.semaphore(name))
        return [sem, 0, name, 0]

    def dma(s, e, dsem, out, in_):
        if dsem[3] > 0:
            s.wait(e, ("dma", dsem, dsem[3]))
        ins = s.eng[e].dma_start(out=out, in_=in_)
        dsem[1] += 16
        ins.then_inc(dsem[0], 16)
        return ("dma", dsem, dsem[1])

    def wait(s, e, *tokens):
        flat = []

        def fl(ts):
            for t in ts:
                if t is None:
                    continue
                if isinstance(t, list):
                    fl(t)
                else:
                    flat.append(t)
        fl(tokens)
        best = {}
        for t in flat:
            key = ("dma:" + t[1][2]) if t[0] == "dma" else t[0]
            v = t[2] if t[0] == "dma" else t[1]
            if key not in best or best[key][0] < v:
                best[key] = (v, t)
        for t in [bt for _, bt in best.values()]:
            if t[0] == "dma":
                _, dsem, v = t
                key = (e, "dma:" + dsem[2])
                dsem[3] = max(dsem[3], v)
                if s.waited.get(key, 0) >= v:
                    continue
                s.eng[e].wait_ge(dsem[0], v)
                s.waited[key] = v
            else:
                src, v = t
                key = (e, src)
                if s.waited.get(key, 0) >= v:
                    continue
                s.eng[e].wait_ge(s.sem[src], v)
                s.waited[key] = v


class Ring:
    def __init__(s, sy, es, nc):
        s.sy = sy
        s.slots = [es.enter_context(nc.sbuf_tensor("wslot%d" % i, [128, SLOT_ELEMS], BF16)) for i in range(NSLOT)]
        s.dsem = [sy.new_dma_sem("wsem%d" % i) for i in range(NSLOT)]
        s.plan = []
        s.issued = {}
        s.next = 0
        s.free_tok = [None] * NSLOT
        s.slot_busy = [False] * NSLOT

    def add(s, name, parts):
        s.plan.append((name, parts))

    def _issue(s, slot):
        if s.next >= len(s.plan):
            return
        name, parts = s.plan[s.next]
        s.next += 1
        s.sy.wait("pool", s.free_tok[slot])
        tok = None
        for viewfn, src in parts:
            tok = s.sy.dma("pool", s.dsem[slot], viewfn(s.slots[slot]), src)
        s.issued[name] = (slot, tok)
        s.slot_busy[slot] = True

    def start(s):
        for i in range(NSLOT):
            s._issue(i)

    def get(s, name):
        slot, tok = s.issued[name]
        return s.slots[slot], tok

    def release(s, name, *tokens):
        slot, _ = s.issued.pop(name)
        s.free_tok[slot] = list(tokens)
        s.slot_busy[slot] = False
        s._issue(slot)


class Kern:
    def __init__(k, debug_stage=None):
        k.debug_stage = debug_stage
        k.nc = bass.Bass("TRN2", target_bir_lowering=False)

    def build(k):
        nc = k.nc
        dt = nc.dram_tensor
        k.x = dt("x", [T, D], F32, kind="ExternalInput").ap()
        k.w = {}
        for nm, shp in [("ffn_pre_w_gate", [D, FF]), ("ffn_pre_w_up", [D, FF]), ("ffn_pre_w_down", [FF, D]),
                        ("ffn_post_w_gate", [D, FF]), ("ffn_post_w_up", [D, FF]), ("ffn_post_w_down", [FF, D]),
                        ("norm_pre_g", [1, D]), ("norm_pre_b", [1, D]), ("norm_mix_g", [1, D]), ("norm_mix_b", [1, D]),
                        ("norm_post_g", [1, D]), ("norm_post_b", [1, D]),
                        ("mix_w_in", [D, 4616]), ("dn_w_proj", [512, D]), ("pool_w_proj", [512, D]), ("mix_w_out", [D, D]),
                        ("pool_w", [4, 128, 128]), ("cw", [128, 12, 4]), ("pscale", [128, 4]), ("dn_norm_g", [1, 128]),
                        ("dn_a_log", [1, 4]), ("dn_dt_bias", [1, 4]),
                        ("ident", [128, 128]), ("Umat", [128, 128]), ("SLmat", [128, 128]), ("invc", [128, 16])]:
            k.w[nm] = dt(nm, shp, F32, kind="ExternalInput").ap()
        for nm in ("mask_u", "nmask_bd", "nmask_off", "ident4"):
            k.w[nm] = dt(nm, [128, 4, 128], BF16, kind="ExternalInput").ap()
        k.out = dt("out", [T, D], F32, kind="ExternalOutput").ap()
        k.scr = dt("scr", [T, D], F32, kind="Internal").ap()
        with contextlib.ExitStack() as es:
            k.es = es
            e = es.enter_context
            k.sy = sy = Sy(nc, es)
            k.hT = e(nc.sbuf_tensor("hT", [128, 8, T], BF16))
            k.ring = Ring(sy, es, nc)
            k.identf = e(nc.sbuf_tensor("identf", [128, 128], F32))
            k.identb = e(nc.sbuf_tensor("identb", [128, 128], BF16))
            k.xb = [e(nc.sbuf_tensor("xb%d" % i, [128, D], BF16)) for i in range(2)]
            k.st = e(nc.sbuf_tensor("st", [128, 4, 2, 6], F32))
            k.mv = e(nc.sbuf_tensor("mv", [128, 4, 2], F32))
            k.sd = e(nc.sbuf_tensor("sd", [128, 4], F32))
            k.rstd = e(nc.sbuf_tensor("rstd", [128, 4], F32))
            k.w_ab = e(nc.sbuf_tensor("w_ab", [128, 8, 8], BF16))
            k.w_pw = e(nc.sbuf_tensor("w_pw", [128, 4, 128], BF16))
            k.ds_x = [sy.new_dma_sem("ds_x%d" % i) for i in range(4)]
            k.ds_id = sy.new_dma_sem("ds_id")
            k.ds_p = sy.new_dma_sem("ds_p")
            k.ds_o = sy.new_dma_sem("ds_o")
            k.ds_c = sy.new_dma_sem("ds_c")
            k.ds_s = sy.new_dma_sem("ds_s")
            k.ds_r2 = [sy.new_dma_sem("ds_r2_%d" % i) for i in range(4)]
            k.ds_s4 = [sy.new_dma_sem("ds_s4_%d" % i) for i in range(4)]
            k.ds_sm = sy.new_dma_sem("ds_sm")
            k.free = {}
            k.ln_tok_last = None
            k.out_tok = []
            k.scr_tok = []
            k.trc = 0
            k.psT_free = [None, None]
            k.xb_free = [None, None]
            k.hT_tok = [None] * NB
            k.hT_rd = [None] * 4
            Win = k.w["mix_w_in"].rearrange("(kc p) n -> p kc n", p=128)
            sy.dma("pool", k.ds_sm, k.w_ab[:], Win[:, :, 2048:2056])
            k.sm_tok = sy.dma("pool", k.ds_sm, k.w_pw[:], k.w["pool_w"].rearrange("g c d -> c g d"))
            k.plan_weights()
            k.ring.start()
            t_id = sy.dma("sp", k.ds_id, k.identf[:], k.w["ident"])
            sy.wait("dve", t_id)
            k.id_tok = sy.sig("dve", nc.vector.tensor_copy(out=k.identb[:], in_=k.identf[:]))
            stage = k.debug_stage
            k.ffn_phase("ffn_pre", "norm_pre", k.x, first=True, mode=("out" if stage == "A" else "spill"))
            if stage != "A":
                k.barrier()
                k.mixer_phase(mode=("out" if stage == "B" else "spill"))
                if stage != "B":
                    k.barrier()
                    k.ffn_phase("ffn_post", "norm_post", k.scr, first=False, mode="out")
            sy.wait("sp", k.out_tok)
        return nc

    def barrier(k):
        sy = k.sy
        toks = [(e, sy.cnt[e]) for e in ("pe", "act", "dve", "pool") if sy.cnt[e] > 0]
        for e in ("pe", "act", "dve", "pool", "sp"):
            sy.wait(e, toks, k.scr_tok, k.out_tok)

    def plan_weights(k):
        ring = k.ring

        def v8(cols):
            return lambda s, h: s[:, 0:8 * cols].rearrange("p (kc c) -> p kc c", kc=8)[:, 4 * h:4 * h + 4, :]

        def ffn(prefix):
            Wg = k.w[prefix + "_w_gate"].rearrange("(kc p) n -> p kc n", p=128)
            Wu = k.w[prefix + "_w_up"].rearrange("(kc p) n -> p kc n", p=128)
            Wd = k.w[prefix + "_w_down"].rearrange("(fc p) n -> p fc n", p=128)
            for gi, grp in enumerate(FFN_GROUPS):
                n = len(grp)
                c0, c1 = grp[0] * 128, (grp[-1] + 1) * 128
                for nm, W in (("g", Wg), ("u", Wu)):
                    parts = []
                    for h in range(2):
                        parts.append((lambda s, f=v8(n * 128), h=h: f(s, h), W[:, 4 * h:4 * h + 4, c0:c1]))
                    ring.add("%s_%s%d" % (prefix, nm, gi), parts)
                parts = []
                for h in range(2):
                    a = 0 if h == 0 else n // 2
                    b = n // 2 if h == 0 else n
                    parts.append((lambda s, n=n, a=a, b=b: s[:, 0:n * 1024].rearrange("p (fc c) -> p fc c", fc=n)[:, a:b, :],
                                  Wd[:, grp[0] + a:grp[0] + b, :]))
                ring.add("%s_d%d" % (prefix, gi), parts)

        def win(name, c0):
            Win = k.w["mix_w_in"].rearrange("(kc p) n -> p kc n", p=128)
            parts = [(lambda s, f=v8(512), h=h: f(s, h), Win[:, 4 * h:4 * h + 4, c0:c0 + 512]) for h in range(2)]
            ring.add(name, parts)

        def k4(name, W):
            Wv = W.rearrange("(kc p) n -> p kc n", p=128)
            parts = [(lambda s, h=h: s[:, :].rearrange("p (kc c) -> p kc c", kc=4)[:, 2 * h:2 * h + 2, :], Wv[:, 2 * h:2 * h + 2, :])
                     for h in range(2)]
            ring.add(name, parts)

        ffn("ffn_pre")
        if k.debug_stage != "A":
            win("mix_q", 0)
            win("mix_k", 512)
            win("mix_v", 1024)
            win("mix_z", 1536)
            win("mix_p", 2056)
            k4("dn_proj", k.w["dn_w_proj"])
            win("gdn0", 2568)
            win("gdn1", 3080)
            k4("pool_proj", k.w["pool_w_proj"])
            win("gpool0", 3592)
            win("gpool1", 4104)
            k4("wout0", k.w["mix_w_out"][0:512, :])
            k4("wout1", k.w["mix_w_out"][512:1024, :])
            if k.debug_stage != "B":
                ffn("ffn_post")

    def emit_transposes(k, src_ap, b, src_tok, psT):
        nc, sy = k.nc, k.sy
        i = k.trc % 2
        k.trc += 1
        sy.wait("act", src_tok, k.xb_free[i])
        tB = sy.sig("act", nc.scalar.copy(out=k.xb[i][:], in_=src_ap))
        sy.wait("pe", tB, k.psT_free[i], k.id_tok)
        for kc in range(8):
            ins = nc.tensor.transpose(out=psT[i][:, kc, :], in_=k.xb[i][:, kc * 128:(kc + 1) * 128], identity=k.identb[:])
        tT = sy.sig("pe", ins)
        k.xb_free[i] = tT
        sy.wait("act", tT, k.hT_rd[b // 4])
        tH = sy.sig("act", nc.scalar.copy(out=k.hT[:, :, b * 128:(b + 1) * 128], in_=psT[i][:]))
        k.psT_free[i] = tH
        k.hT_tok[b] = tH

    def ln_stats(k, blocks, tok_in, tgb):
        nc, sy = k.nc, k.sy
        nb = len(blocks)
        sy.wait("dve", tok_in, tgb, k.ln_tok_last)
        for bb, Rb in enumerate(blocks):
            nc.vector.bn_stats(out=k.st[:, bb, 0, :], in_=Rb[:, 0:512])
            ins = nc.vector.bn_stats(out=k.st[:, bb, 1, :], in_=Rb[:, 512:1024])
        t1 = sy.sig("dve", ins)
        sy.wait("dve", t1)
        for bb in range(nb):
            ins = nc.vector.bn_aggr(out=k.mv[:, bb, :], in_=k.st[:, bb, :, :])
        t2 = sy.sig("dve", ins)
        sy.wait("act", t2)
        tA = sy.sig("act", nc.scalar.activation(out=k.sd[:, 0:nb], in_=k.mv[:, 0:nb, 1], func=AF.Sqrt, bias=LN_EPS, scale=1.0))
        sy.wait("dve", tA)
        t3 = sy.sig("dve", nc.vector.reciprocal(out=k.rstd[:, 0:nb], in_=k.sd[:, 0:nb]))
        k.ln_tok_last = t3
        return t3

    def ln_apply(k, bb, Rb, t3):
        nc, sy = k.nc, k.sy
        sy.wait("dve", t3)
        t4 = sy.sig("dve", nc.vector.scalar_tensor_tensor(out=Rb, in0=Rb, scalar=k.mv[:, bb, 0:1], in1=k.gam[:], op0=ALU.subtract, op1=ALU.mult))
        sy.wait("dve", t4)
        t6 = sy.sig("dve", nc.vector.scalar_tensor_tensor(out=Rb, in0=Rb, scalar=k.rstd[:, bb:bb + 1], in1=k.bet[:], op0=ALU.mult, op1=ALU.add))
        k.ln_tok_last = t6
        return t6

    def load_ln_params(k, norm):
        sy = k.sy
        sy.wait("sp", k.free.get("gam"))
        sy.dma("sp", k.ds_p, k.gam[:], k.w[norm + "_g"].to_broadcast([128, D]))
        return sy.dma("sp", k.ds_p, k.bet[:], k.w[norm + "_b"].to_broadcast([128, D]))

    def layernorm(k, blocks, tok_in, tgb):
        nc, sy = k.nc, k.sy
        nb = len(blocks)
        sy.wait("dve", tok_in, tgb, k.ln_tok_last)
        for bb, Rb in enumerate(blocks):
            nc.vector.bn_stats(out=k.st[:, bb, 0, :], in_=Rb[:, 0:512])
            ins = nc.vector.bn_stats(out=k.st[:, bb, 1, :], in_=Rb[:, 512:1024])
        t1 = sy.sig("dve", ins)
        sy.wait("dve", t1)
        for bb in range(nb):
            ins = nc.vector.bn_aggr(out=k.mv[:, bb, :], in_=k.st[:, bb, :, :])
        t2 = sy.sig("dve", ins)
        sy.wait("act", t2)
        tA = sy.sig("act", nc.scalar.activation(out=k.sd[:, 0:nb], in_=k.mv[:, 0:nb, 1], func=AF.Sqrt, bias=LN_EPS, scale=1.0))
        sy.wait("dve", tA)
        t3 = sy.sig("dve", nc.vector.reciprocal(out=k.rstd[:, 0:nb], in_=k.sd[:, 0:nb]))
        sy.wait("dve", t3)
        for bb, Rb in enumerate(blocks):
            ins = nc.vector.scalar_tensor_tensor(out=Rb, in0=Rb, scalar=k.mv[:, bb, 0:1], in1=k.gam[:], op0=ALU.subtract, op1=ALU.mult)
        t4 = sy.sig("dve", ins)
        sy.wait("dve", t4)
        for bb, Rb in enumerate(blocks):
            ins = nc.vector.scalar_tensor_tensor(out=Rb, in0=Rb, scalar=k.rstd[:, bb:bb + 1], in1=k.bet[:], op0=ALU.mult, op1=ALU.add)
        t6 = sy.sig("dve", ins)
        k.ln_tok_last = t6
        return t6

    def ffn_phase(k, prefix, norm, src, first, mode):
        nc, sy, ring = k.nc, k.sy, k.ring
        c = 0.5 / ALPHA
        NG = len(FFN_GROUPS)
        with contextlib.ExitStack() as es:
            e = es.enter_context
            R = e(nc.sbuf_tensor("R_" + prefix, [128, NB, D], F32))
            k.gam = e(nc.sbuf_tensor("gam_" + prefix, [128, D], F32))
            k.bet = e(nc.sbuf_tensor("bet_" + prefix, [128, D], F32))
            hid = [e(nc.sbuf_tensor("hid%d_%s" % (i, prefix), [128, 4, 512], BF16)) for i in range(2)]
            sg = [e(nc.sbuf_tensor("sg%d_%s" % (i, prefix), [128, 512], BF16)) for i in range(2)]
            psG = [e(nc.psum_tensor("psG%d_%s" % (i, prefix), [128, 512], F32)) for i in range(2)]
            psU = [e(nc.psum_tensor("psU%d_%s" % (i, prefix), [128, 512], F32)) for i in range(2)]
            psO = [e(nc.psum_tensor("psO%d_%s" % (i, prefix), [128, 512], F32)) for i in range(2)]
            psT = [e(nc.psum_tensor("psT%d_%s" % (i, prefix), [128, 8, 128], BF16)) for i in range(2)]
            k.psT_free = [None, None]
            fr = {}
            tgb = k.load_ln_params(norm)
            sv = src.rearrange("(b p) d -> p b d", p=128)
            R_tok = []
            for q in range(4):
                tq = sy.dma("sp", k.ds_x[q], R[:, 4 * q:4 * q + 4, :], sv[:, 4 * q:4 * q + 4, :])
                R_tok += [tq] * 4
            if first:
                for b in range(NB):
                    k.emit_transposes(R[:, b, :], b, R_tok[b], psT)
            gu_ctr = [0]
            o_ctr = [0]

            def GU(gi, t, pp):
                grp = FFN_GROUPS[gi]
                n = len(grp)
                Wg_s, tWg = ring.get("%s_g%d" % (prefix, gi))
                Wu_s, tWu = ring.get("%s_u%d" % (prefix, gi))
                Wg_v = Wg_s[:, 0:8 * n * 128].rearrange("p (kc c) -> p kc c", kc=8)
                Wu_v = Wu_s[:, 0:8 * n * 128].rearrange("p (kc c) -> p kc c", kc=8)
                toks = []
                lastmm = None
                for i in range(n):
                    j = gu_ctr[0] % 2
                    gu_ctr[0] += 1
                    sy.wait("pe", tWg, tWu, [k.hT_tok[b] for b in range(4 * t, 4 * t + 4)], fr.get("psG%d" % j), fr.get("psU%d" % j))
                    for kc in range(8):
                        ins = nc.tensor.matmul(psG[j][:], lhsT=Wg_v[:, kc, i * 128:(i + 1) * 128], rhs=k.hT[:, kc, t * 512:(t + 1) * 512],
                                               start=(kc == 0), stop=(kc == 7))
                    tG = sy.sig("pe", ins)
                    for kc in range(8):
                        ins = nc.tensor.matmul(psU[j][:], lhsT=Wu_v[:, kc, i * 128:(i + 1) * 128], rhs=k.hT[:, kc, t * 512:(t + 1) * 512],
                                               start=(kc == 0), stop=(kc == 7))
                    tU = sy.sig("pe", ins)
                    lastmm = tU
                    sy.wait("act", tG, fr.get("sg%d" % j))
                    tS = sy.sig("act", nc.scalar.activation(out=sg[j][:], in_=psG[j][:], func=AF.Silu))
                    fr["psG%d" % j] = tS
                    sy.wait("dve", tU, tS, fr.get("hid%d" % pp))
                    tH = sy.sig("dve", nc.vector.tensor_tensor(out=hid[pp][:, i, :], in0=sg[j][:], in1=psU[j][:], op=ALU.mult))
                    fr["psU%d" % j] = tH
                    fr["sg%d" % j] = tH
                    toks.append(tH)
                k.hT_rd[t] = lastmm
                if t == 3:
                    ring.release("%s_g%d" % (prefix, gi), lastmm)
                    ring.release("%s_u%d" % (prefix, gi), lastmm)
                return toks

            def DOWN(gi, t, pp, toks):
                grp = FFN_GROUPS[gi]
                n = len(grp)
                Wd_s, tWd = ring.get("%s_d%d" % (prefix, gi))
                Wd_v = Wd_s[:, 0:n * 1024].rearrange("p (fc c) -> p fc c", fc=n)
                lastmm = None
                rtok = None
                for bb in range(4):
                    b = 4 * t + bb
                    for half in range(2):
                        j = o_ctr[0] % 2
                        o_ctr[0] += 1
                        sy.wait("pe", tWd, toks, fr.get("psO%d" % j))
                        for i in range(n):
                            ins = nc.tensor.matmul(psO[j][:], lhsT=hid[pp][:, i, bb * 128:(bb + 1) * 128],
                                                   rhs=Wd_v[:, i, half * 512:(half + 1) * 512], start=(i == 0), stop=(i == n - 1))
                        tO = sy.sig("pe", ins)
                        lastmm = tO
                        sy.wait("dve", tO, R_tok[b])
                        Rv = R[:, b, half * 512:(half + 1) * 512]
                        rtok = sy.sig("dve", nc.vector.scalar_tensor_tensor(out=Rv, in0=psO[j][:], scalar=c, in1=Rv,
                                                                            op0=ALU.mult, op1=ALU.add))
                        fr["psO%d" % j] = rtok
                fr["hid%d" % pp] = lastmm
                if t == 3:
                    ring.release("%s_d%d" % (prefix, gi), lastmm)
                if gi == NG - 1:
                    t6 = k.layernorm([R[:, 4 * t + bb, :] for bb in range(4)], rtok, tgb)
                    if t == 3:
                        k.free["gam"] = t6
                    if mode == "out":
                        sy.wait("sp", t6)
                        ov = k.out.rearrange("(b p) d -> p b d", p=128)
                        k.out_tok.append(sy.dma("sp", k.ds_o, ov[:, 4 * t:4 * t + 4, :], R[:, 4 * t:4 * t + 4, :]))
                    else:
                        sy.wait("sp", t6)
                        ov = k.scr.rearrange("(b p) d -> p b d", p=128)
                        k.scr_tok.append(sy.dma("sp", k.ds_s, ov[:, 4 * t:4 * t + 4, :], R[:, 4 * t:4 * t + 4, :]))
                        pend.append((t, t6))

            pend = []

            def flush():
                while pend:
                    t_, t6_ = pend.pop(0)
                    for bb in range(4):
                        k.emit_transposes(R[:, 4 * t_ + bb, :], 4 * t_ + bb, t6_, psT)

            seq = [(gi, t) for gi in range(NG) for t in range(4)]
            prev = None
            for idx, (gi, t) in enumerate(seq):
                pp = idx % 2
                toks = GU(gi, t, pp)
                flush()
                if prev is not None:
                    DOWN(*prev)
                prev = (gi, t, pp, toks)
            DOWN(*prev)
            flush()
    def run(k, eng, fn, rd=(), wr=(), extra=(), cost=None):
        if getattr(k, "rec", None) is not None:
            k.rec.append((eng, fn, tuple(rd), tuple(wr), tuple(extra), cost))
            return None
        sy = k.sy
        toks = list(extra)
        for b in rd:
            toks.append(k.wtok.get(b))
        for b in wr:
            toks.append(k.wtok.get(b))
            toks += list(k.rtoks.get(b, {}).values())
        sy.wait(eng, toks)
        t = sy.sig(eng, fn())
        for b in rd:
            k.rtoks.setdefault(b, {})[eng] = t
        for b in wr:
            k.wtok[b] = t
            k.rtoks[b] = {}
        return t

    def record(k, fns):
        k.rec = []
        for f in fns:
            f()
        ops, k.rec = k.rec, None
        return ops

    def merge_emit(k, seqs):
        eng_free = {}
        avail = {}
        heads = [0] * len(seqs)
        DEF = {"pe": 0.5, "act": 0.65, "dve": 0.7, "pool": 2.5}
        while True:
            best = None
            for si, seq in enumerate(seqs):
                if heads[si] >= len(seq):
                    continue
                eng, fn, rd, wr, extra, cost = seq[heads[si]]
                st = eng_free.get(eng, 0.0)
                for b in rd + wr:
                    st = max(st, avail.get(b, 0.0))
                if MERGE_SEQ:
                    if best is None:
                        best = (st, si)
                elif best is None or st < best[0] - 1e-9:
                    best = (st, si)
            if best is None:
                break
            st, si = best
            eng, fn, rd, wr, extra, cost = seqs[si][heads[si]]
            heads[si] += 1
            c = cost if cost is not None else DEF[eng]
            fin = st + c
            eng_free[eng] = fin
            for b in wr:
                avail[b] = fin + 0.3
            for b in rd:
                avail[b] = max(avail.get(b, 0.0), st + 0.05)
            k.run(eng, fn, rd=rd, wr=wr, extra=extra)

    def mixer_phase(k, mode):
        nc, sy, ring = k.nc, k.sy, k.ring
        V, A_, PE, G = nc.vector, nc.scalar, nc.tensor, nc.gpsimd
        ENG = {"dve": V, "act": A_, "pool": G}
        k.wtok, k.rtoks = {}, {}
        run = k.run
        with contextlib.ExitStack() as es:
            e = es.enter_context

            def sb(name, shape, dtp=F32):
                return e(nc.sbuf_tensor("mx_" + name, shape, dtp))

            NBK = [4]
            bank = [e(nc.psum_tensor("mxb%d" % i, [128, 512], F32)) for i in range(6)]
            psTb = [e(nc.psum_tensor("mxT%d" % i, [128, 8, 128], BF16)) for i in range(2)]
            k.psT_free = [None, None]
            bctr = [0]
            tctr = [0]

            def getbank(pool=None):
                if pool == "prep":
                    i = pctr[0] % 3
                    pctr[0] += 1
                    return i
                if pool == "scan":
                    return 3
                i = bctr[0] % NBK[0]
                bctr[0] += 1
                return i
            pctr = [0]

            def gett():
                i = tctr[0] % 2
                tctr[0] += 1
                return i

            o_gT = sb("o_gT", [128, 4, T], BF16)
            Um = sb("Um", [128, 128]); SLm = sb("SLm", [128, 128]); ones_f = sb("ones_f", [128, 128])
            ones_b = sb("ones_b", [128, 128], BF16)
            cst = sb("cst", [128, 4, 128])
            mask_u = sb("mask_u", [128, 4, 128], BF16); nmask = sb("nmask", [128, 4, 128], BF16); nmoff = sb("nmoff", [128, 4, 128], BF16)
            id4 = sb("id4", [128, 4, 128], BF16)
            gnorm = sb("gnorm", [128, 4, 128], BF16)
            cw = sb("cw", [128, 12, 4]); pscale = sb("pscale", [128, 4])
            alog = sb("alog", [128, 4]); dtb = sb("dtb", [128, 4]); invc = sb("invc", [128, 16]); ea = sb("ea", [128, 4])
            for dst, nm in ((mask_u, "mask_u"), (nmask, "nmask_bd"), (nmoff, "nmask_off"), (id4, "ident4")):
                sy.dma("sp", k.ds_c, dst[:], k.w[nm])
            for h in range(4):
                tq = sy.dma("sp", k.ds_c, cst[:, h, :], k.w["dn_norm_g"].to_broadcast([128, 128]))
            k.wtok["cst"] = tq
            run("dve", lambda: V.tensor_copy(out=gnorm[:], in_=cst[:]), rd=["cst"], wr=["gnorm"])
            for dst, nm in ((Um, "Umat"), (SLm, "SLmat"), (cw, "cw"), (pscale, "pscale"), (invc, "invc")):
                tq = sy.dma("sp", k.ds_c, dst[:], k.w[nm])
            sy.dma("sp", k.ds_c, alog[:], k.w["dn_a_log"].to_broadcast([128, 4]))
            ctok = sy.dma("sp", k.ds_c, dtb[:], k.w["dn_dt_bias"].to_broadcast([128, 4]))
            for nm in ("Um", "SLm", "cw", "pscale", "invc", "alog", "dtb", "mask_u", "nmask_bd", "nmask_off", "ident4"):
                k.wtok[nm] = ctok
            run("dve", lambda: V.memset(ones_f[:], 1.0), wr=["ones_f"])
            run("dve", lambda: V.memset(ones_b[:], 1.0), wr=["ones_b"])
            run("act", lambda: A_.activation(out=ea[:], in_=alog[:], func=AF.Exp), rd=["alog"], wr=["ea"])
            for b in range(NB):
                k.wtok[("hT", b)] = k.hT_tok[b]
            hTt = lambda t: [("hT", b) for b in range(4 * t, 4 * t + 4)]
            k.wtok["w_ab"] = k.sm_tok
            k.wtok["w_pw"] = k.sm_tok

            def getw(name, kc):
                s_, tk = ring.get(name)
                k.wtok["W:" + name] = tk
                return s_[:, :].rearrange("p (kc c) -> p kc c", kc=kc)

            with contextlib.ExitStack() as es1:
                e1 = es1.enter_context

                def sb1(name, shape, dtp=F32):
                    return e1(nc.sbuf_tensor("b1_" + name, shape, dtp))
                ab = sb1("ab", [128, 16, 8]); beta = sb1("beta", [128, 16, 4]); lnb = sb1("lnb", [128, 16, 4])
                gt = sb1("gt", [128, 16, 4]); xs = sb1("xs", [128, 16, 4])
                egc = sb1("egc", [128, 64]); negc = sb1("negc", [128, 64]); kds = sb1("kds", [128, 64]); egl = sb1("egl", [128, 64])
                g2 = gt[:].rearrange("p b h -> p (b h)")
                lnb2 = lnb[:].rearrange("p b h -> p (b h)")
                bi = getbank()

                def f():
                    for b in range(NB):
                        for kc in range(8):
                            ins = PE.matmul(bank[bi][:, b * 8:(b + 1) * 8], lhsT=k.hT[:, kc, b * 128:(b + 1) * 128], rhs=k.w_ab[:, kc, :],
                                            start=(kc == 0), stop=(kc == 7))
                    return ins
                run("pe", f, rd=[("hT", b) for b in range(NB)] + ["w_ab"], wr=[("bk", bi)])
                run("act", lambda: A_.copy(out=ab[:], in_=bank[bi][:, 0:128].rearrange("p (b c) -> p b c", b=16)), rd=[("bk", bi)], wr=["ab"])
                run("act", lambda: A_.activation(out=beta[:], in_=ab[:, :, 0:4], func=AF.Sigmoid), rd=["ab"], wr=["beta"])
                run("act", lambda: A_.activation(out=lnb[:], in_=beta[:], func=AF.Ln), rd=["beta"], wr=["lnb"])
                run("dve", lambda: V.tensor_tensor(out=xs[:], in0=ab[:, :, 4:8], in1=dtb[:][:, None, :].broadcast_to([128, 16, 4]), op=ALU.add),
                    rd=["ab", "dtb"], wr=["xs"])
                run("act", lambda: A_.activation(out=xs[:], in_=xs[:], func=AF.Exp), rd=["xs"], wr=["xs"])
                run("act", lambda: A_.activation(out=xs[:], in_=xs[:], func=AF.Ln, bias=1.0, scale=1.0), rd=["xs"], wr=["xs"])
                run("dve", lambda: V.scalar_tensor_tensor(out=gt[:], in0=xs[:], scalar=-1.0, in1=ea[:][:, None, :].broadcast_to([128, 16, 4]),
                                                         op0=ALU.mult, op1=ALU.mult), rd=["xs", "ea"], wr=["gt"])
                pc = [sb1("pc%d" % i, [128, 515]) for i in range(2)]
                acc = [sb1("acc%d" % i, [128, 512]) for i in range(2)]
                sq = [sb1("sq%d" % i, [128, 512], BF16) for i in range(2)]
                hist = sb1("hist", [128, 12, 3])
                qkvT = sb1("qkvT", [128, 12, 512], BF16)
                kvtok = sb1("kvtok", [128, 4, 8, 128], BF16)
                gz = sb1("gz", [128, 4, 512], BF16)
                gSL = sb1("gSL", [128, 4, 128]); Db = sb1("Db", [128, 4, 128]); t1b = sb1("t1b", [128, 4, 128])
                Pm = [sb1("Pm%d" % i, [128, 4, 128], BF16) for i in range(2)]
                Pt = [sb1("Pt%d" % i, [128, 4, 128], BF16) for i in range(2)]
                Nt = [sb1("Nt%d" % i, [128, 4, 128], BF16) for i in range(2)]
                kdecb = [sb1("kdecb%d" % i, [128, 4, 128], BF16) for i in range(2)]
                QKb = [sb1("QKb%d" % i, [128, 4, 128], BF16) for i in range(2)]
                Tt = [sb1("Tt%d" % i, [128, 4, 128], BF16) for i in range(2)]
                S = sb1("S", [128, 4, 128]); Shi = sb1("Shi", [128, 4, 128], BF16); Slo = sb1("Slo", [128, 4, 128], BF16)
                r1f = sb1("r1f", [128, 4, 128]); r1b = sb1("r1b", [128, 4, 128], BF16); x1b = sb1("x1b", [128, 4, 128], BF16)
                r2b = sb1("r2b", [128, 4, 128], BF16)
                xhi = sb1("xhi", [128, 4, 128], BF16)
                o2s = sb1("o2s", [128, 4, 128]); ot = sb1("ot", [128, 4, 128])
                ssq = sb1("ssq", [128, 4]); rr = sb1("rr", [128, 4]); og = sb1("og", [128, 4, 128], BF16)
                fl = lambda a: a[:].rearrange("p h d -> p (h d)")
                run("dve", lambda: V.memset(hist[:], 0.0), wr=[("hist", ch) for ch in range(12)])
                run("dve", lambda: V.memset(S[:], 0.0), wr=["S"])
                run("dve", lambda: V.memset(Shi[:], 0.0), wr=["Shi"])
                run("dve", lambda: V.memset(Slo[:], 0.0), wr=["Slo"])
                Wq = getw("mix_q", 8); Wk = getw("mix_k", 8); Wv = getw("mix_v", 8); Wz = getw("mix_z", 8)

                def proj_front(t, ch):
                    tsl = slice(t * 512, (t + 1) * 512)
                    Wx, wn = (Wq, "W:mix_q") if ch < 4 else ((Wk, "W:mix_k") if ch < 8 else (Wv, "W:mix_v"))
                    c0 = (ch % 4) * 128
                    bi = getbank()
                    pi = ch % 2
                    pcb, accb = pc[pi], acc[pi]
                    pk, ak = ("pc", pi), ("acc", pi)

                    def f():
                        for kc in range(8):
                            ins = PE.matmul(bank[bi][:], lhsT=Wx[:, kc, c0:c0 + 128], rhs=k.hT[:, kc, tsl], start=(kc == 0), stop=(kc == 7))
                        return ins
                    run("pe", f, rd=hTt(t) + [wn], wr=[("bk", bi)])
                    run("act", lambda: A_.copy(out=pcb[:, 3:515], in_=bank[bi][:]), rd=[("bk", bi)], wr=[pk])
                    run("act", lambda: A_.mul(out=accb[:], in_=bank[bi][:], mul=cw[:, ch, 3:4]), rd=[("bk", bi), "cw"], wr=[ak])
                    run("dve", lambda: V.tensor_copy(out=pcb[:, 0:3], in_=hist[:, ch, :]), rd=[("hist", ch)], wr=[pk])
                    run("dve", lambda: V.tensor_copy(out=hist[:, ch, :], in_=pcb[:, 512:515]), rd=[pk], wr=[("hist", ch)])

                def proj_back(t, ch):
                    pi = ch % 2
                    pcb, accb = pc[pi], acc[pi]
                    pk, ak = ("pc", pi), ("acc", pi)
                    for j in (0, 1, 2):
                        run("dve", lambda j=j: V.scalar_tensor_tensor(out=accb[:], in0=pcb[:, j:j + 512], scalar=cw[:, ch, j:j + 1], in1=accb[:],
                                                                     op0=ALU.mult, op1=ALU.add), rd=[pk, ak], wr=[ak])
                    run("act", lambda: A_.activation(out=qkvT[:, ch, :], in_=accb[:], func=AF.Silu), rd=[ak], wr=[("qkvT", ch)])

                l2b = {}

                def l2norm_front(t, ch):
                    pi = ch % 2
                    sqb, sk_ = sq[pi], ("sq", pi)
                    run("act", lambda: A_.activation(out=sqb[:], in_=qkvT[:, ch, :], func=AF.Square), rd=[("qkvT", ch)], wr=[sk_])
                    b2 = getbank()
                    l2b[ch] = b2
                    run("pe", lambda: PE.matmul(bank[b2][:], lhsT=ones_b[:], rhs=sqb[:], start=True, stop=True), rd=[sk_, "ones_b"], wr=[("bk", b2)])

                def l2norm_back(t, ch):
                    pi = ch % 2
                    rinv, rk_ = acc[pi], ("acc", pi)
                    b2 = l2b[ch]
                    sc = 128.0 if ch < 4 else 1.0
                    run("act", lambda: A_.activation(out=rinv[:], in_=bank[b2][:], func=AF.Ln, bias=RMS_EPS * sc, scale=sc),
                        rd=[("bk", b2)], wr=[rk_])
                    run("act", lambda: A_.activation(out=rinv[:], in_=rinv[:], func=AF.Exp, scale=-0.5), rd=[rk_], wr=[rk_])
                    run("dve", lambda: V.tensor_tensor(out=qkvT[:, ch, :], in0=qkvT[:, ch, :], in1=rinv[:], op=ALU.mult),
                        rd=[("qkvT", ch), rk_], wr=[("qkvT", ch)])

                def kv_transposes(t, bb):
                    i = gett()

                    def f():
                        for c8 in range(8):
                            ins = PE.transpose(out=psTb[i][:, c8, :], in_=qkvT[:, 4 + c8, bb * 128:(bb + 1) * 128], identity=k.identb[:])
                        return ins
                    run("pe", f, rd=[("qkvT", 4 + c8) for c8 in range(8)], wr=[("pt", i)], extra=[k.id_tok])
                    run("act", lambda: A_.copy(out=kvtok[:, bb, :, :], in_=psTb[i][:]), rd=[("pt", i)], wr=[("kvtok", bb)])

                def z_block(t, bb):
                    b = 4 * t + bb
                    bi = getbank()

                    def f():
                        for kc in range(8):
                            ins = PE.matmul(bank[bi][:], lhsT=k.hT[:, kc, b * 128:(b + 1) * 128], rhs=Wz[:, kc, :], start=(kc == 0), stop=(kc == 7))
                        return ins
                    run("pe", f, rd=[("hT", b), "W:mix_z"], wr=[("bk", bi)])
                    run("act", lambda: A_.activation(out=gz[:, bb, :], in_=bank[bi][:], func=AF.Silu), rd=[("bk", bi)], wr=[("gz", bb)])
                    run("dve", lambda: V.tensor_tensor(out=gz[:, bb, :], in0=gz[:, bb, :], in1=fl(gnorm), op=ALU.mult),
                        rd=[("gz", bb), "gnorm"], wr=[("gz", bb)])

                NIT = 5

                def prep_pieces(t, bb):
                    c = 4 * t + bb
                    par = c % 2
                    csl = slice(bb * 128, (bb + 1) * 128)
                    col = lambda h: slice(c * 4 + h, c * 4 + h + 1)
                    kd, qk, tt = kdecb[par], QKb[par], Tt[par]
                    kdk, qkk, ttk = ("kdecb", par), ("QKb", par), ("Tt", par)
                    pieces = []

                    def pA():
                        def f():
                            for h in range(4):
                                ins = V.tensor_scalar(out=kd[:, h, :], in0=kvtok[:, bb, h, :], scalar1=kds[:, col(h)], scalar2=None, op0=ALU.mult)
                            return ins
                        run("dve", f, rd=[("kvtok", bb), "kds"], wr=[kdk])

                        def f():
                            for h in range(4):
                                ins = V.tensor_scalar(out=gSL[:, h, :], in0=SLm[:], scalar1=g2[:, col(h)], scalar2=None, op0=ALU.mult)
                            return ins
                        run("dve", f, rd=["SLm", "gt"], wr=["gSL"])
                        bA, bQK, bLD = getbank("prep"), getbank("prep"), getbank("prep")

                        def f():
                            for h in range(4):
                                hs = slice(h * 128, (h + 1) * 128)
                                PE.matmul(bank[bA][:, hs], lhsT=qkvT[:, 4 + h, csl], rhs=qkvT[:, 4 + h, csl], start=True, stop=True)
                                ins = PE.matmul(bank[bQK][:, hs], lhsT=qkvT[:, 4 + h, csl], rhs=qkvT[:, h, csl], start=True, stop=True)
                            return ins
                        run("pe", f, rd=[("qkvT", c8) for c8 in range(8)], wr=[("bk", bA), ("bk", bQK)])

                        def f():
                            for h in range(4):
                                ins = PE.matmul(bank[bLD][:, h * 128:(h + 1) * 128], lhsT=gSL[:, h, :], rhs=Um[:], start=True, stop=True)
                            return ins
                        run("pe", f, rd=["gSL", "Um"], wr=[("bk", bLD)])

                        def f():
                            for h in range(4):
                                ins = A_.activation(out=Db[:, h, :], in_=bank[bLD][:, h * 128:(h + 1) * 128], func=AF.Exp,
                                                    bias=lnb2[:, col(h)], scale=1.0)
                            return ins
                        run("act", f, rd=[("bk", bLD), "lnb"], wr=["Db"])
                        run("dve", lambda: V.tensor_tensor(out=fl(t1b), in0=bank[bA][:], in1=fl(Db), op=ALU.mult), rd=[("bk", bA), "Db"], wr=["t1b"])
                        run("dve", lambda: V.tensor_tensor(out=Pt[0][:], in0=t1b[:], in1=nmask[:], op=ALU.mult), rd=["t1b", "nmask_bd"], wr=[("Pt", 0)])
                        run("dve", lambda: V.tensor_tensor(out=Nt[par][:], in0=t1b[:], in1=nmoff[:], op=ALU.mult), rd=["t1b", "nmask_off"], wr=[("Nt", par)])
                        run("dve", lambda: V.tensor_tensor(out=fl(t1b), in0=bank[bQK][:], in1=fl(Db), op=ALU.mult), rd=[("bk", bQK), "Db"], wr=["t1b"])
                        run("pool", lambda: G.tensor_tensor(out=qk[:], in0=t1b[:], in1=mask_u[:], op=ALU.mult), rd=["t1b", "mask_u"], wr=[qkk])
                        run("pool", lambda: G.tensor_tensor(out=tt[:], in0=Pt[0][:], in1=id4[:], op=ALU.add), rd=[("Pt", 0), "ident4"], wr=[ttk])
                        i = 0

                        def f():
                            for h in range(4):
                                ins = PE.transpose(out=psTb[i][:, h, :], in_=Pt[0][:, h, :], identity=k.identb[:])
                            return ins
                        run("pe", f, rd=[("Pt", 0)], wr=[("pt", i)])
                        run("act", lambda: A_.copy(out=Pm[0][:], in_=psTb[i][:, 0:4, :]), rd=[("pt", i)], wr=[("Pm", 0)])
                    pieces.append(pA)

                    def mk_iter(n):
                        cur = (n - 1) % 2
                        nxt = n % 2

                        def pB():
                            bP = getbank("prep")

                            def f():
                                for h in range(4):
                                    ins = PE.matmul(bank[bP][:, h * 128:(h + 1) * 128], lhsT=Pt[cur][:, h, :], rhs=Pm[cur][:, h, :], start=True, stop=True)
                                return ins
                            run("pe", f, rd=[("Pt", cur), ("Pm", cur)], wr=[("bk", bP)])
                            if n < NIT:
                                bPt = getbank("prep")

                                def f():
                                    for h in range(4):
                                        ins = PE.matmul(bank[bPt][:, h * 128:(h + 1) * 128], lhsT=Pm[cur][:, h, :], rhs=Pt[cur][:, h, :], start=True, stop=True)
                                    return ins
                                run("pe", f, rd=[("Pt", cur), ("Pm", cur)], wr=[("bk", bPt)])
                            run("act", lambda: A_.copy(out=fl(Pm[nxt]), in_=bank[bP][:]), rd=[("bk", bP)], wr=[("Pm", nxt)])
                            if n < NIT:
                                if n % 2 == 0:
                                    run("act", lambda: A_.copy(out=fl(Pt[nxt]), in_=bank[bPt][:]), rd=[("bk", bPt)], wr=[("Pt", nxt)])
                                else:
                                    run("dve", lambda: V.tensor_copy(out=fl(Pt[nxt]), in_=bank[bPt][:]), rd=[("bk", bPt)], wr=[("Pt", nxt)])
                            bT = getbank("prep")

                            def f():
                                for h in range(4):
                                    ins = PE.matmul(bank[bT][:, h * 128:(h + 1) * 128], lhsT=Pm[nxt][:, h, :], rhs=tt[:, h, :], start=True, stop=True)
                                return ins
                            run("pe", f, rd=[("Pm", nxt), ttk], wr=[("bk", bT)])
                            run("dve", lambda: V.tensor_tensor(out=fl(tt), in0=bank[bT][:], in1=fl(tt), op=ALU.add), rd=[("bk", bT), ttk], wr=[ttk])
                        return pB
                    for n in range(1, NIT + 1):
                        pieces.append(mk_iter(n))
                    return pieces

                def scan_pieces(t, bb):
                    c = 4 * t + bb
                    par = c % 2
                    csl = slice(bb * 128, (bb + 1) * 128)
                    col = lambda h: slice(c * 4 + h, c * 4 + h + 1)
                    kd, qk, tt = kdecb[par], QKb[par], Tt[par]
                    kdk, qkk, ttk = ("kdecb", par), ("QKb", par), ("Tt", par)
                    st_ = {}

                    def H1():
                        bKS, bQS = getbank("scan"), 4
                        st_["bQS"] = bQS

                        def f():
                            for h in range(4):
                                hs = slice(h * 128, (h + 1) * 128)
                                PE.matmul(bank[bKS][:, hs], lhsT=qkvT[:, 4 + h, csl], rhs=Shi[:, h, :], start=True, stop=False)
                                PE.matmul(bank[bKS][:, hs], lhsT=qkvT[:, 4 + h, csl], rhs=Slo[:, h, :], start=False, stop=True)
                            for h in range(4):
                                hs = slice(h * 128, (h + 1) * 128)
                                PE.matmul(bank[bQS][:, hs], lhsT=qkvT[:, h, csl], rhs=Shi[:, h, :], start=True, stop=False)
                                ins = PE.matmul(bank[bQS][:, hs], lhsT=qkvT[:, h, csl], rhs=Slo[:, h, :], start=False, stop=True)
                            return ins
                        run("pe", f, rd=[("qkvT", c8) for c8 in range(8)] + ["Shi", "Slo"], wr=[("bk", bKS), ("bk", bQS)])

                        def f():
                            for h in range(4):
                                ins = V.scalar_tensor_tensor(out=r1f[:, h, :], in0=bank[bKS][:, h * 128:(h + 1) * 128], scalar=negc[:, col(h)],
                                                             in1=kvtok[:, bb, 4 + h, :], op0=ALU.mult, op1=ALU.add)
                            return ins
                        run("dve", f, rd=[("bk", bKS), "negc", ("kvtok", bb)], wr=["r1f"])
                        run("act", lambda: A_.copy(out=r1b[:], in_=r1f[:]), rd=["r1f"], wr=["r1b"])

                    def H2():
                        nt, ntk = Nt[par], ("Nt", par)
                        bX1 = getbank("scan")

                        def f():
                            for h in range(4):
                                ins = PE.matmul(bank[bX1][:, h * 128:(h + 1) * 128], lhsT=tt[:, h, :], rhs=r1b[:, h, :], start=True, stop=True)
                            return ins
                        run("pe", f, rd=[ttk, "r1b"], wr=[("bk", bX1)])
                        run("act", lambda: A_.copy(out=fl(x1b), in_=bank[bX1][:]), rd=[("bk", bX1)], wr=["x1b"])
                        bY = getbank("scan")

                        def f():
                            for h in range(4):
                                ins = PE.matmul(bank[bY][:, h * 128:(h + 1) * 128], lhsT=nt[:, h, :], rhs=x1b[:, h, :], start=True, stop=True)
                            return ins
                        run("pe", f, rd=[ntk, "x1b"], wr=[("bk", bY)])
                        run("dve", lambda: V.tensor_tensor(out=fl(r2b), in0=bank[bY][:], in1=fl(r1f), op=ALU.add), rd=[("bk", bY), "r1f"], wr=["r2b"])
                        bX = getbank("scan")

                        def f():
                            for h in range(4):
                                ins = PE.matmul(bank[bX][:, h * 128:(h + 1) * 128], lhsT=tt[:, h, :], rhs=r2b[:, h, :], start=True, stop=True)
                            return ins
                        run("pe", f, rd=[ttk, "r2b"], wr=[("bk", bX)])
                        run("act", lambda: A_.copy(out=fl(xhi), in_=bank[bX][:]), rd=[("bk", bX)], wr=["xhi"])

                    def H3():
                        bO2, bSU = 5, getbank("scan")
                        st_["bO2"] = bO2

                        def f():
                            for h in range(4):
                                hs = slice(h * 128, (h + 1) * 128)
                                PE.matmul(bank[bSU][:, hs], lhsT=kd[:, h, :], rhs=xhi[:, h, :], start=True, stop=True)
                            for h in range(4):
                                hs = slice(h * 128, (h + 1) * 128)
                                ins = PE.matmul(bank[bO2][:, hs], lhsT=qk[:, h, :], rhs=xhi[:, h, :], start=True, stop=True)
                            return ins
                        run("pe", f, rd=[kdk, qkk, "xhi"], wr=[("bk", bSU), ("bk", bO2)])

                        def f():
                            for h in range(4):
                                ins = V.scalar_tensor_tensor(out=S[:, h, :], in0=S[:, h, :], scalar=egl[:, col(h)],
                                                             in1=bank[bSU][:, h * 128:(h + 1) * 128], op0=ALU.mult, op1=ALU.add)
                            return ins
                        run("dve", f, rd=[("bk", bSU), "egl", "S"], wr=["S"])
                        run("act", lambda: A_.copy(out=Shi[:], in_=S[:]), rd=["S"], wr=["Shi"])
                        run("dve", lambda: V.tensor_tensor(out=Slo[:], in0=S[:], in1=Shi[:], op=ALU.subtract), rd=["S", "Shi"], wr=["Slo"])

                    def H4():
                        bO2, bQS = st_["bO2"], st_["bQS"]
                        run("act", lambda: A_.copy(out=fl(o2s), in_=bank[bO2][:]), rd=[("bk", bO2)], wr=["o2s"])

                        def f():
                            for h in range(4):
                                ins = V.scalar_tensor_tensor(out=ot[:, h, :], in0=bank[bQS][:, h * 128:(h + 1) * 128], scalar=egc[:, col(h)],
                                                             in1=o2s[:, h, :], op0=ALU.mult, op1=ALU.add)
                            return ins
                        run("dve", f, rd=[("bk", bQS), "egc", "o2s"], wr=["ot"])

                        def f():
                            for h in range(4):
                                ins = A_.activation(out=o2s[:, h, :], in_=ot[:, h, :], func=AF.Square, accum_out=ssq[:, h:h + 1])
                            return ins
                        run("act", f, rd=["ot"], wr=["ssq", "o2s"])
                        run("act", lambda: A_.activation(out=rr[:], in_=ssq[:], func=AF.Ln, bias=RMS_EPS, scale=1.0 / 128.0), rd=["ssq"], wr=["rr"])
                        run("act", lambda: A_.activation(out=rr[:], in_=rr[:], func=AF.Exp, scale=-0.5), rd=["rr"], wr=["rr"])

                        def f():
                            for h in range(4):
                                ins = V.scalar_tensor_tensor(out=og[:, h, :], in0=ot[:, h, :], scalar=rr[:, h:h + 1],
                                                             in1=gz[:, bb, h * 128:(h + 1) * 128], op0=ALU.mult, op1=ALU.mult)
                            return ins
                        run("dve", f, rd=["ot", "rr", ("gz", bb)], wr=["og"])
                        i = 1

                        def f():
                            for h in range(4):
                                ins = PE.transpose(out=psTb[i][:, h, :], in_=og[:, h, :], identity=k.identb[:])
                            return ins
                        run("pe", f, rd=["og"], wr=[("pt", i)])
                        run("act", lambda: A_.copy(out=o_gT[:, :, c * 128:(c + 1) * 128], in_=psTb[i][:, 0:4, :]), rd=[("pt", i)], wr=[("o_gT", c)])
                    return [H1, H2, H3, H4]

                def ab_part2():
                    bgc, bgm, bgl = getbank(), getbank(), getbank()

                    def f():
                        PE.matmul(bank[bgc][:, 0:64], lhsT=Um[:], rhs=g2, start=True, stop=True)
                        PE.matmul(bank[bgm][:, 0:64], lhsT=SLm[:], rhs=g2, start=True, stop=True)
                        return PE.matmul(bank[bgl][:, 0:64], lhsT=ones_f[:], rhs=g2, start=True, stop=True)
                    run("pe", f, rd=["gt", "Um", "SLm", "ones_f"], wr=[("bk", bgc), ("bk", bgm), ("bk", bgl)])

                    def f():
                        A_.activation(out=egc[:], in_=bank[bgc][:, 0:64], func=AF.Exp)
                        A_.activation(out=kds[:], in_=bank[bgm][:, 0:64], func=AF.Exp)
                        return A_.activation(out=egl[:], in_=bank[bgl][:, 0:64], func=AF.Exp)
                    run("act", f, rd=[("bk", bgc), ("bk", bgm), ("bk", bgl)], wr=["egc", "kds", "egl"])
                    run("dve", lambda: V.tensor_scalar(out=negc[:], in0=egc[:], scalar1=-1.0, scalar2=None, op0=ALU.mult), rd=["egc"], wr=["negc"])
                    run("dve", lambda: V.tensor_tensor(out=kds[:], in0=kds[:], in1=beta[:].rearrange("p b h -> p (b h)"), op=ALU.mult),
                        rd=["kds", "beta"], wr=["kds"])

                def bulk(t, mid=None):
                    proj_front(t, 0)
                    for ch in range(12):
                        if ch + 1 < 12:
                            proj_front(t, ch + 1)
                        proj_back(t, ch)
                        if ch % 3 == 2:
                            z_block(t, ch // 3)
                    if mid is not None:
                        mid()
                    l2norm_front(t, 0)
                    for ch in range(8):
                        if ch + 1 < 8:
                            l2norm_front(t, ch + 1)
                        l2norm_back(t, ch)
                    for bb in range(4):
                        kv_transposes(t, bb)
                chunks = [(t, bb) for t in range(4) for bb in range(4)]
                bulk(0, mid=ab_part2)
                for p_ in prep_pieces(0, 0):
                    p_()
                for ci, (t, bb) in enumerate(chunks):
                    sc = scan_pieces(t, bb)
                    nxt = chunks[ci + 1] if ci + 1 < len(chunks) else None
                    if nxt is not None and nxt[0] == t:
                        pp = prep_pieces(*nxt)
                        k.merge_emit([k.record(sc), k.record(pp)])
                    else:
                        for s_ in sc:
                            s_()
                        if nxt is not None:
                            bulk(nxt[0])
                            for p_ in prep_pieces(*nxt):
                                p_()
                lastpe = ("pe", sy.cnt["pe"])
                for nm in ("mix_q", "mix_k", "mix_v", "mix_z"):
                    ring.release(nm, lastpe)
                b1_end = [(en, sy.cnt[en]) for en in ("pe", "act", "dve", "pool") if sy.cnt[en] > 0]

            with contextlib.ExitStack() as es2:
                e2 = es2.enter_context

                def sb2(name, shape, dtp=F32):
                    return e2(nc.sbuf_tensor("b2_" + name, shape, dtp))
                NBK[0] = 6
                k.gam = sb2("gam", [128, D]); k.bet = sb2("bet", [128, D])
                R2 = sb2("R2", [128, 4, D])
                pb = [sb2("pb%d" % i, [128, 528]) for i in range(2)]
                sA = sb2("sA", [128, 528]); sB = sb2("sB", [128, 528])
                phist = sb2("phist", [128, 4, 16]); tmpc = sb2("tmpc", [128, 16])
                pooledT2 = [sb2("pooledT%d" % i, [128, 4, 512], BF16) for i in range(2)]
                pooled2T2 = [sb2("pooled2T%d" % i, [128, 4, 512], BF16) for i in range(2)]
                mergedT = sb2("mergedT", [128, 8, 512], BF16)
                s1 = [sb2("s1_%d" % i, [128, 512]) for i in range(2)]
                s2 = [sb2("s2_%d" % i, [128, 512]) for i in range(2)]
                m1 = [sb2("m1_%d" % i, [128, 512]) for i in range(2)]
                for en in ("pe", "act", "dve", "pool", "sp"):
                    sy.wait(en, b1_end)
                k.wtok = {kk: v for kk, v in k.wtok.items() if isinstance(kk, tuple) and kk[0] in ("hT", "o_gT") or kk in ("w_pw", "pscale", "invc")}
                k.rtoks = {}
                run("dve", lambda: V.memset(phist[:], 0.0), wr=["phist"])
                tgb = k.load_ln_params("norm_mix")
                scv = k.scr.rearrange("(b p) d -> p b d", p=128)
                ogt = lambda t: [("o_gT", c) for c in range(4 * t, 4 * t + 4)]
                Wp = getw("mix_p", 8)
                Wdn = getw("dn_proj", 4); Wpp = getw("pool_proj", 4)
                Wgd = [getw("gdn0", 8), getw("gdn1", 8)]
                Wgp = [getw("gpool0", 8), getw("gpool1", 8)]
                Wo = [getw("wout0", 4), getw("wout1", 4)]
                def pool_piece(t, g):
                    tsl = slice(t * 512, (t + 1) * 512)
                    pooledT, pooled2T = pooledT2[t % 2], pooled2T2[t % 2]
                    pkey, p2key = ("pooledT", t % 2, g), ("pooled2T", t % 2, g)
                    w = 2 ** (g + 1)
                    bi = getbank()
                    pbb, pbk = pb[g % 2], ("pb", g % 2)

                    def f():
                        for kc in range(8):
                            ins = PE.matmul(bank[bi][:], lhsT=Wp[:, kc, g * 128:(g + 1) * 128], rhs=k.hT[:, kc, tsl], start=(kc == 0), stop=(kc == 7))
                        return ins
                    run("pe", f, rd=hTt(t) + ["W:mix_p"], wr=[("bk", bi)])
                    run("act", lambda: A_.copy(out=pbb[:, 16:528], in_=bank[bi][:]), rd=[("bk", bi)], wr=[pbk])
                    run("dve", lambda: V.tensor_copy(out=pbb[:, 0:16], in_=phist[:, g, :]), rd=["phist"], wr=[pbk])
                    run("dve", lambda: V.tensor_copy(out=phist[:, g, 1:16], in_=pbb[:, 513:528]), rd=[pbk], wr=["phist"])
                    src_, dst_, sk, dk = pbb, sA, pbk, "sA"
                    sh = 1
                    while sh < w:
                        en = "dve"
                        E_ = ENG[en]
                        run(en, lambda E_=E_, src_=src_, dst_=dst_, sh=sh: E_.tensor_tensor(out=dst_[:, 2 * sh:528], in0=src_[:, 2 * sh:528],
                                                                                          in1=src_[:, sh:528 - sh], op=ALU.add),
                            rd=[sk], wr=[dk])
                        src_, sk = dst_, dk
                        dst_, dk = (sB, "sB") if dst_ is sA else (sA, "sA")
                        sh *= 2
                    run("dve", lambda src_=src_: V.scalar_tensor_tensor(out=pooledT[:, g, :], in0=src_[:, 16:528], scalar=1.0 / w, in1=pbb[:, 16:528],
                                                                       op0=ALU.mult, op1=ALU.subtract), rd=[sk, pbk], wr=[pkey])
                    if t == 0:
                        run("dve", lambda src_=src_: V.tensor_tensor(out=tmpc[:, 0:w - 1], in0=src_[:, 16:16 + w - 1], in1=invc[:, 0:w - 1], op=ALU.mult),
                            rd=[sk, "invc"], wr=["tmpc"])
                        run("dve", lambda: V.tensor_tensor(out=pooledT[:, g, 0:w - 1], in0=tmpc[:, 0:w - 1], in1=pbb[:, 16:16 + w - 1], op=ALU.subtract),
                            rd=["tmpc", pbk], wr=[pkey])
                    b2_ = getbank()
                    run("pe", lambda: PE.matmul(bank[b2_][:], lhsT=k.w_pw[:, g, :], rhs=pooledT[:, g, :], start=True, stop=True),
                        rd=["w_pw", pkey], wr=[("bk", b2_)])
                    run("act", lambda: A_.mul(out=pooled2T[:, g, :], in_=bank[b2_][:], mul=pscale[:, g:g + 1]), rd=[("bk", b2_), "pscale"],
                        wr=[p2key])

                pend_tr = []
                pend_ln = []
                lastln = [None]

                def do_ln_block(t, bb, t3):
                    rk = ("R2", bb)
                    t6 = k.ln_apply(bb, R2[:, bb, :], t3)
                    lastln[0] = t6
                    k.wtok[rk] = t6
                    k.rtoks[rk] = {}
                    sy.wait("sp", t6)
                    if mode == "out":
                        ov = k.out.rearrange("(b p) d -> p b d", p=128)
                        tst = sy.dma("sp", k.ds_s4[bb], ov[:, 4 * t + bb, :], R2[:, bb, :])
                        k.out_tok.append(tst)
                    else:
                        tst = sy.dma("sp", k.ds_s4[bb], scv[:, 4 * t + bb, :], R2[:, bb, :])
                        k.scr_tok.append(tst)
                        pend_tr.append((t, bb, t6))
                    k.rtoks[rk]["sp"] = tst

                def do_transposes(item):
                    t_, bb_, t6_ = item
                    k.emit_transposes(R2[:, bb_, :], 4 * t_ + bb_, t6_, psTb)
                    k.rtoks[("R2", bb_)]["act"] = ("act", sy.cnt["act"])

                for g in range(4):
                    pool_piece(0, g)
                for t in range(4):
                    tsl = slice(t * 512, (t + 1) * 512)
                    pooledT, pooled2T = pooledT2[t % 2], pooled2T2[t % 2]
                    for dc in range(8):
                        wg_, wp_ = Wgd[dc // 4], Wgp[dc // 4]
                        dl = slice((dc % 4) * 128, (dc % 4 + 1) * 128)
                        dsl = slice(dc * 128, (dc + 1) * 128)
                        b1_, b2_, b3_, b4_ = getbank(), getbank(), getbank(), getbank()
                        pi = dc % 2

                        def f():
                            for kc in range(8):
                                ins = PE.matmul(bank[b2_][:], lhsT=wg_[:, kc, dl], rhs=k.hT[:, kc, tsl], start=(kc == 0), stop=(kc == 7))
                            return ins
                        run("pe", f, rd=hTt(t) + ["W:gdn%d" % (dc // 4)], wr=[("bk", b2_)])

                        def f():
                            for kc in range(4):
                                ins = PE.matmul(bank[b1_][:], lhsT=Wdn[:, kc, dsl], rhs=o_gT[:, kc, tsl], start=(kc == 0), stop=(kc == 3))
                            return ins
                        run("pe", f, rd=ogt(t) + ["W:dn_proj"], wr=[("bk", b1_)])

                        def f():
                            for kc in range(8):
                                ins = PE.matmul(bank[b4_][:], lhsT=wp_[:, kc, dl], rhs=k.hT[:, kc, tsl], start=(kc == 0), stop=(kc == 7))
                            return ins
                        run("pe", f, rd=hTt(t) + ["W:gpool%d" % (dc // 4)], wr=[("bk", b4_)])

                        def f():
                            for kc in range(4):
                                ins = PE.matmul(bank[b3_][:], lhsT=Wpp[:, kc, dsl], rhs=pooled2T[:, kc, :], start=(kc == 0), stop=(kc == 3))
                            return ins
                        run("pe", f, rd=[("pooled2T", t % 2, g) for g in range(4)] + ["W:pool_proj"], wr=[("bk", b3_)])
                        run("act", lambda: A_.activation(out=s1[pi][:], in_=bank[b2_][:], func=AF.Sigmoid), rd=[("bk", b2_)], wr=[("s1", pi)])
                        run("act", lambda: A_.activation(out=s2[pi][:], in_=bank[b4_][:], func=AF.Sigmoid), rd=[("bk", b4_)], wr=[("s2", pi)])
                        run("dve", lambda: V.tensor_tensor(out=m1[pi][:], in0=bank[b1_][:], in1=s1[pi][:], op=ALU.mult), rd=[("bk", b1_), ("s1", pi)],
                            wr=[("m1", pi)])
                        run("dve", lambda: V.tensor_tensor(out=s2[pi][:], in0=bank[b3_][:], in1=s2[pi][:], op=ALU.mult), rd=[("bk", b3_), ("s2", pi)],
                            wr=[("s2", pi)])
                        run("pool", lambda: G.tensor_tensor(out=mergedT[:, dc, :], in0=m1[pi][:], in1=s2[pi][:], op=ALU.add), rd=[("m1", pi), ("s2", pi)],
                            wr=[("mergedT", dc)])
                        if dc % 2 == 1 and t + 1 < 4:
                            pool_piece(t + 1, dc // 2)
                        if dc % 2 == 1:
                            if pend_tr:
                                do_transposes(pend_tr.pop(0))
                            if pend_ln:
                                do_ln_block(*pend_ln.pop(0))
                    k.hT_rd[t] = ("pe", sy.cnt["pe"])
                    while pend_ln:
                        do_ln_block(*pend_ln.pop(0))
                    while pend_tr:
                        do_transposes(pend_tr.pop(0))
                    for bb in range(4):
                        rk = ("R2", bb)
                        sy.wait("sp", k.wtok.get(rk), list(k.rtoks.get(rk, {}).values()))
                        k.wtok[rk] = sy.dma("sp", k.ds_r2[bb], R2[:, bb, :], scv[:, 4 * t + bb, :])
                        k.rtoks[rk] = {}
                    if t == 3:
                        lp = ("pe", sy.cnt["pe"])
                        for nm in ("mix_p", "dn_proj", "pool_proj", "gdn0", "gdn1", "gpool0", "gpool1"):
                            ring.release(nm, lp)
                    for bb in range(4):
                        rk = ("R2", bb)
                        for half in range(2):
                            bi = getbank()

                            def f():
                                for kc in range(8):
                                    ins = PE.matmul(bank[bi][:], lhsT=mergedT[:, kc, bb * 128:(bb + 1) * 128],
                                                    rhs=Wo[kc // 4][:, kc % 4, half * 512:(half + 1) * 512], start=(kc == 0), stop=(kc == 7))
                                return ins
                            run("pe", f, rd=[("mergedT", dc) for dc in range(8)] + ["W:wout0", "W:wout1"], wr=[("bk", bi)])
                            Rv = R2[:, bb, half * 512:(half + 1) * 512]
                            run("dve", lambda: V.scalar_tensor_tensor(out=Rv, in0=bank[bi][:], scalar=1.0 / ALPHA, in1=Rv, op0=ALU.mult, op1=ALU.add),
                                rd=[("bk", bi), rk], wr=[rk])
                        if t == 3 and bb == 3:
                            lp = ("pe", sy.cnt["pe"])
                            ring.release("wout0", lp)
                            ring.release("wout1", lp)
                    t3_ = k.ln_stats([R2[:, bb, :] for bb in range(4)], [k.wtok[("R2", bb)] for bb in range(4)], tgb)
                    for bb in range(4):
                        pend_ln.append((t, bb, t3_))
                while pend_ln:
                    do_ln_block(*pend_ln.pop(0))
                while pend_tr:
                    do_transposes(pend_tr.pop(0))
                k.free["gam"] = lastln[0]


_CACHE = {}


def _get_nc(debug_stage=None):
    if debug_stage not in _CACHE:
        kk = Kern(debug_stage)
        _CACHE[debug_stage] = kk.build()
    return _CACHE[debug_stage]


def _consts():
    i = np.arange(128)
    c = {}
    c["ident"] = np.eye(128, dtype=np.float32)
    c["Umat"] = (i[:, None] <= i[None, :]).astype(np.float32)
    c["SLmat"] = (i[:, None] > i[None, :]).astype(np.float32)
    mu = (i[None, :] >= i[:, None]).astype(np.float32)
    su = (i[None, :] > i[:, None]).astype(np.float32)
    c["mask_u"] = np.ascontiguousarray(np.repeat(mu[:, None, :], 4, axis=1))
    blk = (i[:, None] // 64 == i[None, :] // 64).astype(np.float32)
    c["nmask_bd"] = np.ascontiguousarray(np.repeat((-su * blk)[:, None, :], 4, axis=1))
    c["nmask_off"] = np.ascontiguousarray(np.repeat((-su * (1.0 - blk))[:, None, :], 4, axis=1))
    c["ident4"] = np.ascontiguousarray(np.repeat(np.eye(128, dtype=np.float32)[:, None, :], 4, axis=1))
    c["invc"] = np.ascontiguousarray(np.repeat((1.0 / (np.arange(16) + 1.0))[None, :], 128, axis=0).astype(np.float32))
    for nm in ("mask_u", "nmask_bd", "nmask_off", "ident4"):
        c[nm] = c[nm].astype(ml_dtypes.bfloat16)
    return c


def kernel(debug_stage=None, trace=False, **inputs):
    nc = _get_nc(debug_stage)
    x = np.ascontiguousarray(inputs["x"], dtype=np.float32)
    shared = _consts()
    for nm in ["ffn_pre_w_gate", "ffn_pre_w_up", "ffn_pre_w_down", "ffn_post_w_gate", "ffn_post_w_up", "ffn_post_w_down",
               "mix_w_in", "dn_w_proj", "pool_w_proj", "mix_w_out", "pool_w"]:
        shared[nm] = np.ascontiguousarray(inputs[nm][0], dtype=np.float32)
    for nm in ["norm_pre_g", "norm_pre_b", "norm_mix_g", "norm_mix_b", "norm_post_g", "norm_post_b"]:
        shared[nm] = np.ascontiguousarray(inputs[nm], dtype=np.float32).reshape(1, D)
    cwv = np.asarray(inputs["mix_conv_w"], dtype=np.float32)[0]
    shared["cw"] = np.ascontiguousarray(cwv.reshape(4, 12, 128).transpose(2, 1, 0))
    shared["pscale"] = np.ascontiguousarray(np.asarray(inputs["pool_scale"], dtype=np.float32)[0].T)
    shared["dn_norm_g"] = np.ascontiguousarray(inputs["dn_norm_g"], dtype=np.float32).reshape(1, 128)
    shared["dn_a_log"] = np.ascontiguousarray(inputs["dn_a_log"], dtype=np.float32).reshape(1, 4)
    shared["dn_dt_bias"] = np.ascontiguousarray(inputs["dn_dt_bias"], dtype=np.float32).reshape(1, 4)
    in_maps = []
    for c in range(8):
        m = dict(shared)
        m["x"] = x[c]
        in_maps.append(m)
    res = run_bass_kernel_spmd(nc, in_maps, core_ids=list(range(8)), **({"trace": True} if trace else {}))
    out = np.stack([r["out"] for r in res.results], axis=0).astype(np.float32)
    if trace:
        return out, res
    return out
```
